# Optimizing a Trainium2 kernel written in Bass

```python
import math
import jax, jax.numpy as jnp
from jax import lax
import numpy as np

D_MODEL = 2048
BATCH = 4
SEQ = 2048
DEPTH = 2

N_BRANCH = 4
W_BRANCH = D_MODEL // 4
W_SSM = W_BRANCH
W_POOL = W_BRANCH
W_CONV = W_BRANCH
W_GMLP = W_BRANCH
SSM_GROUP = 16
SSM_GROUPS = W_SSM // SSM_GROUP
SSM_STATE = 64
POOL_WINDOWS = (2, 4, 8, 16)
POOL_GROUPS = len(POOL_WINDOWS)
POOL_GW = W_POOL // POOL_GROUPS
CONV_WIDTH = 31
GMLP_CHUNK = 128
GMLP_HEADS = 4
GMLP_HD = W_GMLP // GMLP_HEADS
D_FF = 4 * D_MODEL
EPS = 1e-6

OFF_SSM = 0
OFF_POOL = OFF_SSM + W_SSM
OFF_CONV = OFF_POOL + W_POOL
OFF_GMLP = OFF_CONV + 2 * W_CONV
OFF_GATE = OFF_GMLP + 2 * W_GMLP
IN_COLS = OFF_GATE + N_BRANCH * D_MODEL

kernel_name = 'hybrid_s5_pool_conv_gmlp_block'


def rms_norm(x, g):
    xf = x.astype(jnp.float32)
    y = xf * lax.rsqrt(jnp.mean(xf * xf, axis=-1, keepdims=True) + EPS)
    return (y * g.astype(jnp.float32)).astype(x.dtype)


def layer_norm(x, g, b):
    xf = x.astype(jnp.float32)
    mu = jnp.mean(xf, axis=-1, keepdims=True)
    xc = xf - mu
    var = jnp.mean(xc * xc, axis=-1, keepdims=True)
    y = xc * lax.rsqrt(var + EPS)
    return (y * g.astype(jnp.float32) + b.astype(jnp.float32)).astype(x.dtype)


def s5_mixer(u, a_re, a_im, log_dt, b_re, b_im, c_re, c_im, d_skip, w_glu):
    nb, L, _ = u.shape
    uf = u.astype(jnp.float32)
    ug = uf.reshape(nb, L, SSM_GROUPS, SSM_GROUP)
    dt = jnp.exp(log_dt.astype(jnp.float32))[:, None]
    ar = a_re.astype(jnp.float32)
    ai = a_im.astype(jnp.float32)
    mag = jnp.exp(ar * dt)
    ang = ai * dt
    lr = mag * jnp.cos(ang)
    li = mag * jnp.sin(ang)
    den = ar * ar + ai * ai
    fr = ((lr - 1.0) * ar + li * ai) / den
    fi = (li * ar - (lr - 1.0) * ai) / den
    br = b_re.astype(jnp.float32)
    bi = b_im.astype(jnp.float32)
    bbr = fr[:, :, None] * br - fi[:, :, None] * bi
    bbi = fr[:, :, None] * bi + fi[:, :, None] * br
    bur = jnp.einsum('gnp,blgp->blgn', bbr, ug)
    bui = jnp.einsum('gnp,blgp->blgn', bbi, ug)
    lam_r = jnp.broadcast_to(lr, bur.shape)
    lam_i = jnp.broadcast_to(li, bur.shape)

    def combine(e1, e2):
        a1r, a1i, b1r, b1i = e1
        a2r, a2i, b2r, b2i = e2
        return (a1r * a2r - a1i * a2i,
                a1r * a2i + a1i * a2r,
                a2r * b1r - a2i * b1i + b2r,
                a2r * b1i + a2i * b1r + b2i)

    _, _, sr, si = lax.associative_scan(combine, (lam_r, lam_i, bur, bui), axis=1)
    y = (jnp.einsum('gpn,blgn->blgp', c_re.astype(jnp.float32), sr)
         - jnp.einsum('gpn,blgn->blgp', c_im.astype(jnp.float32), si))
    y = y.reshape(nb, L, W_SSM) + d_skip.astype(jnp.float32) * uf
    z = jax.nn.gelu(y)
    zg = z @ w_glu.astype(jnp.float32)
    out = zg[..., :W_SSM] * jax.nn.sigmoid(zg[..., W_SSM:])
    return out.astype(u.dtype)


def pool_mixer(u, w_pool, pool_scale):
    nb, L, _ = u.shape
    uf = u.astype(jnp.float32)
    cs = jnp.pad(jnp.cumsum(uf, axis=1), ((0, 0), (1, 0), (0, 0)))
    t = jnp.arange(L)
    outs = []
    for gi, w in enumerate(POOL_WINDOWS):
        c = cs[:, :, gi * POOL_GW:(gi + 1) * POOL_GW]
        upper = c[:, 1:]
        lower = jnp.pad(c[:, :L + 1 - w], ((0, 0), (w - 1, 0), (0, 0)))
        count = jnp.minimum(t + 1, w).astype(jnp.float32)[None, :, None]
        mean = (upper - lower) / count
        outs.append(mean - uf[:, :, gi * POOL_GW:(gi + 1) * POOL_GW])
    pooled = jnp.stack(outs, axis=2)
    mixed = jnp.einsum('blgc,gcd->blgd', pooled, w_pool.astype(jnp.float32))
    out = mixed.reshape(nb, L, W_POOL) * pool_scale.astype(jnp.float32)
    return out.astype(u.dtype)


def conv_mixer(val, gate, w_dw, b_dw, ln_g, ln_b):
    v = val * jax.nn.sigmoid(gate)
    kern = w_dw.astype(v.dtype).reshape(CONV_WIDTH, 1, W_CONV)
    y = lax.conv_general_dilated(v, kern, window_strides=(1,), padding=[(CONV_WIDTH - 1, 0)],
                                 dimension_numbers=('NWC', 'WIO', 'NWC'),
                                 feature_group_count=W_CONV)
    y = y + b_dw.astype(y.dtype)
    y = layer_norm(y, ln_g, ln_b)
    return jax.nn.silu(y)


def gmlp_mixer(u, v, ln_g, ln_b, w_s, b_s):
    nb, L, _ = u.shape
    u = jax.nn.gelu(u)
    v = layer_norm(jax.nn.gelu(v), ln_g, ln_b)
    nc = L // GMLP_CHUNK
    vc = v.reshape(nb, nc, GMLP_CHUNK, GMLP_HEADS, GMLP_HD)
    mask = jnp.tril(jnp.ones((GMLP_CHUNK, GMLP_CHUNK), dtype=bool))
    ws = jnp.where(mask[None], w_s, jnp.zeros_like(w_s))
    sv = jnp.einsum('hts,bcshd->bcthd', ws, vc) + jnp.transpose(b_s)[None, None, :, :, None]
    return u * sv.reshape(nb, L, W_GMLP).astype(u.dtype)


def setup_inputs(seed: int = 0) -> dict:
    key = jax.random.key(seed)
    ks = iter(jax.random.split(key, 40))

    def nrm(shape, scale):
        return jax.random.normal(next(ks), shape, jnp.float32) * scale

    def gain(shape):
        return 1.0 + nrm(shape, 0.02)

    G, N, P = SSM_GROUPS, SSM_STATE, SSM_GROUP
    x = jax.random.normal(next(ks), (BATCH, SEQ, D_MODEL), jnp.float32)
    a_re = -0.5 + nrm((DEPTH, G, N), 0.01)
    a_im = jnp.pi * jnp.arange(N, dtype=jnp.float32)[None, None, :] + nrm((DEPTH, G, N), 0.01)
    log_dt = jax.random.uniform(next(ks), (DEPTH, G), jnp.float32,
                                minval=math.log(1e-3), maxval=math.log(1e-1))
    return {
        'x': x,
        'g_pre_mix': gain((DEPTH, D_MODEL)),
        'w_in': nrm((DEPTH, D_MODEL, IN_COLS), D_MODEL ** -0.5),
        'ssm_a_re': a_re,
        'ssm_a_im': a_im,
        'ssm_log_dt': log_dt,
        'ssm_b_re': nrm((DEPTH, G, N, P), (2 * P) ** -0.5),
        'ssm_b_im': nrm((DEPTH, G, N, P), (2 * P) ** -0.5),
        'ssm_c_re': nrm((DEPTH, G, P, N), (2 * N) ** -0.5),
        'ssm_c_im': nrm((DEPTH, G, P, N), (2 * N) ** -0.5),
        'ssm_d': nrm((DEPTH, W_SSM), 1.0),
        'ssm_w_glu': nrm((DEPTH, W_SSM, 2 * W_SSM), W_SSM ** -0.5),
        'pool_w': nrm((DEPTH, POOL_GROUPS, POOL_GW, POOL_GW), POOL_GW ** -0.5),
        'pool_scale': gain((DEPTH, W_POOL)),
        'conv_w': nrm((DEPTH, CONV_WIDTH, W_CONV), CONV_WIDTH ** -0.5),
        'conv_b': nrm((DEPTH, W_CONV), 0.02),
        'conv_ln_g': gain((DEPTH, W_CONV)),
        'conv_ln_b': nrm((DEPTH, W_CONV), 0.02),
        'gmlp_ln_g': gain((DEPTH, W_GMLP)),
        'gmlp_ln_b': nrm((DEPTH, W_GMLP), 0.02),
        'gmlp_ws': nrm((DEPTH, GMLP_HEADS, GMLP_CHUNK, GMLP_CHUNK), GMLP_CHUNK ** -0.5),
        'gmlp_bs': gain((DEPTH, GMLP_HEADS, GMLP_CHUNK)),
        'w_branch': nrm((DEPTH, N_BRANCH, W_BRANCH, D_MODEL), W_BRANCH ** -0.5),
        'w_o': nrm((DEPTH, D_MODEL, D_MODEL), D_MODEL ** -0.5),
        'g_post_mix': gain((DEPTH, D_MODEL)),
        'g_pre_mlp': gain((DEPTH, D_MODEL)),
        'w_ff1': nrm((DEPTH, D_MODEL, D_FF), D_MODEL ** -0.5),
        'w_ff2': nrm((DEPTH, D_FF, D_MODEL), D_FF ** -0.5),
        'g_post_mlp': gain((DEPTH, D_MODEL)),
    }


def reference(x, g_pre_mix, w_in, ssm_a_re, ssm_a_im, ssm_log_dt, ssm_b_re, ssm_b_im,
              ssm_c_re, ssm_c_im, ssm_d, ssm_w_glu, pool_w, pool_scale, conv_w, conv_b,
              conv_ln_g, conv_ln_b, gmlp_ln_g, gmlp_ln_b, gmlp_ws, gmlp_bs, w_branch, w_o,
              g_post_mix, g_pre_mlp, w_ff1, w_ff2, g_post_mlp):
    nb, L, _ = x.shape
    for l in range(DEPTH):
        h = rms_norm(x, g_pre_mix[l])
        proj = h @ w_in[l]
        y_ssm = s5_mixer(proj[..., OFF_SSM:OFF_SSM + W_SSM], ssm_a_re[l], ssm_a_im[l],
                         ssm_log_dt[l], ssm_b_re[l], ssm_b_im[l], ssm_c_re[l], ssm_c_im[l],
                         ssm_d[l], ssm_w_glu[l])
        y_pool = pool_mixer(proj[..., OFF_POOL:OFF_POOL + W_POOL], pool_w[l], pool_scale[l])
        y_conv = conv_mixer(proj[..., OFF_CONV:OFF_CONV + W_CONV],
                            proj[..., OFF_CONV + W_CONV:OFF_GMLP],
                            conv_w[l], conv_b[l], conv_ln_g[l], conv_ln_b[l])
        y_gmlp = gmlp_mixer(proj[..., OFF_GMLP:OFF_GMLP + W_GMLP],
                            proj[..., OFF_GMLP + W_GMLP:OFF_GATE],
                            gmlp_ln_g[l], gmlp_ln_b[l], gmlp_ws[l], gmlp_bs[l])
        ys = jnp.stack([y_ssm, y_pool, y_conv, y_gmlp], axis=2)
        yb = jnp.einsum('blkc,kcd->blkd', ys, w_branch[l])
        gates = jax.nn.sigmoid(
            proj[..., OFF_GATE:].reshape(nb, L, N_BRANCH, D_MODEL).astype(jnp.float32))
        merged = jnp.sum(yb.astype(jnp.float32) * gates, axis=2).astype(x.dtype)
        mix = merged @ w_o[l]
        x = x + rms_norm(mix, g_post_mix[l])
        h2 = rms_norm(x, g_pre_mlp[l])
        f = jnp.square(jax.nn.relu(h2 @ w_ff1[l])) @ w_ff2[l]
        x = x + rms_norm(f, g_post_mlp[l])
    return x
```

```python
import contextlib
import math
import numpy as np
import concourse.bass as bass
import concourse.mybir as mybir
from concourse.bass_utils import run_bass_kernel_spmd

F32 = mybir.dt.float32
BF16 = mybir.dt.bfloat16
AF = mybir.ActivationFunctionType
ALU = mybir.AluOpType

D = 2048
SEQ = 2048
NB = 4
L = 2
TT = 1024
NT = SEQ // TT
H = 512
IN_COLS = 11264
OFF_POOL, OFF_CONV, OFF_GMLP, OFF_GATE = 512, 1024, 2048, 3072
DFF = 8192
EPS = 1e-6
NUNIT = 4
UNIT = 2048
WSLOT = 2 * UNIT
WCAP = 294912


class Sched:
    def __init__(self, samesync=True):
        self.ops = []
        self.last_w = {}
        self.readers = {}
        self.last_dma = {}
        self.samesync = samesync

    def add(self, eng, fn, reads=(), writes=(), dma_key=None, deps=()):
        idx = len(self.ops)
        d = set(deps)
        for k in reads:
            w = self.last_w.get(k)
            if w is not None:
                d.add(w)
        for k in writes:
            w = self.last_w.get(k)
            if w is not None:
                d.add(w)
            for r in self.readers.get(k, {}).values():
                d.add(r)
        if dma_key is not None:
            p = self.last_dma.get(dma_key)
            if p is not None:
                d.add(p)
            self.last_dma[dma_key] = idx
        d.discard(idx)
        stream = ("dma", idx) if dma_key is not None else eng
        for k in reads:
            self.readers.setdefault(k, {})[stream] = idx
        for k in writes:
            self.last_w[k] = idx
            self.readers[k] = {}
        self.ops.append(dict(eng=eng, fn=fn, deps=sorted(d), dma_key=dma_key, observed=False, waits=[]))
        return idx

    def finalize(self):
        ops = self.ops
        known = {e: {} for e in ("pe", "act", "dve", "pool", "sp")}
        snaps = [None] * len(ops)
        for i, op in enumerate(ops):
            e = op["eng"]
            kn = known[e]
            best = {}
            for j in op["deps"]:
                pj = ops[j]
                if pj["dma_key"] is not None:
                    key = ("dma", pj["dma_key"])
                else:
                    key = pj["eng"]
                    if key == e and op["dma_key"] is None and (not self.samesync or e == "pe"):
                        continue
                if kn.get(key, -1) >= j:
                    continue
                if best.get(key, -1) < j:
                    best[key] = j
            for key, j in sorted(best.items(), key=lambda kv: kv[1]):
                if kn.get(key, -1) >= j:
                    continue
                ops[j]["observed"] = True
                op["waits"].append(j)
                kn[key] = j
                for k2, v2 in snaps[j].items():
                    if kn.get(k2, -1) < v2:
                        kn[k2] = v2
            snaps[i] = dict(kn)
        cnt = {}
        for op in ops:
            if op["dma_key"] is not None:
                key = ("dma", op["dma_key"])
                cnt[key] = cnt.get(key, 0) + 16
                op["token"] = (key, cnt[key])
            elif op["observed"]:
                key = op["eng"]
                cnt[key] = cnt.get(key, 0) + 1
                op["token"] = (key, cnt[key])
            else:
                op["token"] = None
        self.sem_keys = sorted(cnt.keys(), key=str)
        return self

    def stats(self):
        from collections import Counter
        c = Counter(op["eng"] for op in self.ops)
        return dict(ops=dict(c), waits=sum(len(op["waits"]) for op in self.ops),
                    observed=sum(1 for op in self.ops if op["observed"]), sems=len(self.sem_keys))

    def run(self, engine_obj, ename, sems):
        ops = self.ops
        for op in ops:
            if op["eng"] != ename:
                continue
            for j in op["waits"]:
                key, val = ops[j]["token"]
                engine_obj.wait_ge(sems[key], val)
            ins = op["fn"](engine_obj)
            if op["token"] is not None and ins is not None:
                key, val = op["token"]
                ins.then_inc(sems[key], 16 if isinstance(key, tuple) else 1)


def hs(hf):
    return slice(hf * H, (hf + 1) * H)


class WSpec:
    def __init__(self, name, K, N):
        self.name, self.K, self.N = name, K, N

    def __getitem__(self, idx):
        if not isinstance(idx, tuple):
            idx = (idx, slice(None), slice(None))
        l, rs, cs = idx
        r0, r1, _ = rs.indices(self.K)
        c0, c1, _ = cs.indices(self.N)
        return (self.name, l, r0, r1 - r0, c0, c1 - c0)


class Builder:
    def __init__(self, layers, debug=None, ntiles=NT, phases="spcgm34"):
        self.ntiles = ntiles
        self.phases = phases
        self.layers = layers
        self.debug = debug
        self.S = Sched()
        self.steps = []
        self.bank_rr = 0

    def op(self, eng, fn, r=(), w=()):
        return self.S.add(eng, fn, reads=r, writes=w)

    def dma(self, eng, out, in_, key, r=(), w=()):
        return self.S.add(eng, lambda e: e.dma_start(out=out, in_=in_), reads=r, writes=w, dma_key=key)

    def mm(self, out, lhsT, rhs, start, stop, r, w):
        return self.op("pe", lambda e: e.matmul(out, lhsT, rhs, start=start, stop=stop), r=r, w=w)

    def act(self, out, in_, func, r, w, bias=None, scale=None, accum_out=None):
        kw = {}
        if bias is not None:
            kw["bias"] = bias
        if scale is not None:
            kw["scale"] = scale
        if accum_out is not None:
            kw["accum_out"] = accum_out
        return self.op("act", lambda e: e.activation(out=out, in_=in_, func=func, **kw), r=r, w=w)

    def tt(self, out, in0, in1, op_, r, w, eng="dve"):
        return self.op(eng, lambda e: e.tensor_tensor(out=out, in0=in0, in1=in1, op=op_), r=r, w=w)

    def ts(self, out, in0, s1, op0, r, w, s2=None, op1=None, eng="dve"):
        if op1 is None:
            return self.op(eng, lambda e: e.tensor_scalar(out=out, in0=in0, scalar1=s1, scalar2=None, op0=op0), r=r, w=w)
        return self.op(eng, lambda e: e.tensor_scalar(out=out, in0=in0, scalar1=s1, scalar2=s2, op0=op0, op1=op1), r=r, w=w)

    def stt(self, out, in0, scalar, in1, op0, op1, r, w):
        return self.op("dve", lambda e: e.scalar_tensor_tensor(out=out, in0=in0, scalar=scalar, in1=in1, op0=op0, op1=op1), r=r, w=w)

    def cp(self, out, in_, r, w, eng="dve"):
        return self.op(eng, lambda e: e.tensor_copy(out=out, in_=in_), r=r, w=w)

    def nb(self):
        b = self.bank_rr
        self.bank_rr = (self.bank_rr + 1) % 4
        return b

    def wstep(self, src, kc, ncols, fn):
        parts = tuple(src) if isinstance(src, list) else ((src, kc),)
        n = sum(k for _, k in parts) * ncols
        assert n <= WSLOT
        key = (parts, ncols)
        if key not in self.cat:
            if not self.wt_tensors or self.wt_tensors[-1][2] + n > WCAP:
                name = "wt%d" % len(self.wt_tensors)
                self.wt_tensors.append([name, self._dr(name, [128, WCAP]), 0, []])
            tns = self.wt_tensors[-1]
            self.cat[key] = (len(self.wt_tensors) - 1, tns[2], n)
            tns[3].append((key, tns[2], n))
            tns[2] += n
        self.steps.append((self.cat[key], parts, ncols, fn))

    def step(self, fn):
        self.steps.append((None, None, 0, fn))

    def wkeys(self, slot):
        u0, cnt = slot
        return [("w", u0 + i) for i in range(cnt)]

    def run_steps(self):
        steps = self.steps
        tiles = [i for i, s in enumerate(steps) if s[0] is not None]
        owner = [-1] * NUNIT
        loaded = {}
        st = dict(next=0, ptr=0)

        def try_issue(cur):
            while st["next"] < len(tiles):
                i = tiles[st["next"]]
                (ti, off, n), parts, ncols, _ = steps[i]
                cnt = 1 if n <= UNIT else 2
                p = st["ptr"]
                if cnt == 2 and p % 2 == 1:
                    p = (p + 1) % NUNIT
                if p + cnt > NUNIT:
                    p = 0
                if any(owner[p + k] >= cur for k in range(cnt)):
                    return
                kct = sum(kk for _, kk in parts)
                dst = self.wring[:, p * UNIT:p * UNIT + n]
                self.dma("pool", dst, self.wt_tensors[ti][1][:, off:off + n], key=("w", p), w=[("w", p + k) for k in range(cnt)])
                for k in range(cnt):
                    owner[p + k] = i
                loaded[i] = ((p, cnt), dst.rearrange("p (c n) -> p c n", c=kct))
                st["ptr"] = (p + cnt) % NUNIT
                st["next"] += 1

        for i, s in enumerate(steps):
            try_issue(i)
            if s[0] is not None:
                assert i in loaded
                owner_save = None
                s[3](*loaded[i])
            else:
                s[3]()
        self.steps = []

    def gemm(self, bank, pairs, rkeys, first=True, last=True, n=H):
        np_ = len(pairs)
        for i, (lt, rh, rk) in enumerate(pairs):
            self.mm(self.ps[:, bank, 0:n], lt, rh, start=(first and i == 0), stop=(last and i == np_ - 1),
                    r=list(rkeys) + list(rk), w=[("ps", bank)])

    def build(self):
        nc = bass.Bass("TRN2", target_bir_lowering=False)
        self.nc = nc
        es = contextlib.ExitStack()
        self.es = es
        dr = lambda name, shape, kind="ExternalInput": nc.dram_tensor(name, list(shape), F32, kind=kind).ap()
        self.xin = dr("xin", [NT, 128, 16, TT])
        self.yout = dr("yout", [NT, 128, 16, TT], "ExternalOutput")
        self.xs = self.yout
        self.w_in = WSpec("w_in", D, IN_COLS)
        self.w_glu = WSpec("w_glu", 512, 1024)
        self.pool_w = dr("pool_w", [L, 512, 128])
        self.w_branch = WSpec("w_branch", 2048, D)
        self.w_o = WSpec("w_o", D, D)
        self.w_ff1 = WSpec("w_ff1", D, DFF)
        self.w_ff2 = WSpec("w_ff2", DFF, D)
        self.cat = {}
        self.wt_tensors = []
        self._dr = dr
        self.gv_d = dr("gv", [128, L, 4, 16])
        self.ssm_a_d = dr("ssm_a", [128, L, 3, 16])
        self.ssm_bT_d = dr("ssm_bT", [L, 2, 16, 128, 128])
        self.ssm_cT_d = dr("ssm_cT", [L, 2, 16, 128, 128])
        self.pvec_d = dr("pvec", [128, L, 5, 4])
        self.convw_d = dr("convw", [128, L, 4, 31])
        self.gln_d = dr("gln", [L, 3, 512])
        self.wsT_d = dr("wsT", [L, 128, 4, 128])
        self.mask_d = dr("mask", [128, 128])
        self.invcnt_d = dr("invcnt", [128, 4, 15])
        if self.debug:
            self.dbg = dr("dbg", [128, 16, TT], "ExternalOutput")

        sb = lambda name, shape, dt=F32: es.enter_context(nc.sbuf_tensor(name, list(shape), dt))
        with es:
            self.xT = sb("xT", [128, 16, TT])
            self.hT = sb("hT", [128, 16, TT], BF16)
            self.RC = sb("RC", [128, 32768], BF16)
            self.wring = sb("wring", [128, NUNIT * UNIT], BF16)
            self.ps = es.enter_context(nc.psum_tensor("ps", [128, 8, H], F32))
            RC = self.RC
            self.ysT = RC[:, 0:16384].rearrange("p (c t) -> p c t", c=16)
            self.mgT = RC[:, 16384:32768].rearrange("p (c t) -> p c t", c=16)
            self.mixF = RC[:, 0:16384].bitcast(F32).rearrange("p (c t) -> p c t", c=16)
            self.f1T = RC[:, :].rearrange("p (c t) -> p c t", c=64)
            SCR = RC[:, 16384:32768]
            self.SCR = SCR
            self.ones = sb("ones", [128, 128], BF16)
            self.gv = sb("gvs", [128, L, 4, 16])
            self.pvec = sb("pvecs", [128, L, 5, 4])
            self.convw = sb("convws", [128, L, 4, 31])
            self.ssma = sb("ssmas", [128, 3, 16])
            self.spt = sb("spt", [128, 16, 16])
            self.Pr = sb("Pr", [128, 10, 16])
            self.Pi = sb("Pi", [128, 10, 16])
            self.Pn = sb("Pn", [128, 10, 16])
            self.Cre = RC[:, 12288:14336].rearrange("p (c t) -> p c t", c=16)
            self.Cim = RC[:, 14336:16384].rearrange("p (c t) -> p c t", c=16)
            self.ctmp = sb("ctmp", [128, 4, 128])
            self.Bt = sb("Bt", [128, 2, 2, 128], BF16)
            self.carry = sb("carry", [128, 16, 2])
            self.poolc = sb("poolc", [128, 4, 15])
            self.convc = sb("convc", [128, 4, 30])
            self.gln = sb("glns", [128, 3, 512])
            self.wsT = sb("wsTs", [128, 4, 128], BF16)
            self.maskt = sb("maskt", [128, 128])
            self.invc = sb("invc", [128, 4, 15])
            self.poolw = sb("poolw", [128, 4, 128], BF16)
            self.rs = sb("rs", [128, 2, H])
            self.tmpf = sb("tmpf", [128, 2, H])
            self.sq = sb("sq", [128, 2, H], BF16)
            self.st4 = sb("st4", [128, 8, 8])

            self.emit_all()

            self.S.finalize()
            print("sched:", self.S.stats(), "sbuf_remaining", nc.sbuf_bytes_remaining, flush=True)
            sems = {k: es.enter_context(nc.semaphore("sem%d" % i)) for i, k in enumerate(self.S.sem_keys)}
            block = es.enter_context(nc.Block())
            S = self.S

            @block.tensor
            def _(e):
                S.run(e, "pe", sems)

            @block.scalar
            def _(e):
                S.run(e, "act", sems)

            @block.vector
            def _(e):
                S.run(e, "dve", sems)

            @block.gpsimd
            def _(e):
                S.run(e, "pool", sems)

            @block.sync
            def _(e):
                S.run(e, "sp", sems)
        return nc

    def emit_all(self):
        self.op("dve", lambda e: e.memset(self.ones[:, :], 1.0), w=["ones"])
        self.dma("sp", self.gv[:], self.gv_d, "c_gv", w=["gv"])
        self.dma("sp", self.pvec[:], self.pvec_d, "c_pvec", w=["pvec"])
        self.dma("sp", self.convw[:], self.convw_d, "c_convw", w=["convw"])
        self.dma("sp", self.maskt[:], self.mask_d, "c_mask", w=["mask"])
        self.dma("sp", self.invc[:], self.invcnt_d, "c_invc", w=["invc"])
        fin = []
        nl = len(self.layers)
        for li, l in enumerate(self.layers):
            self.layer_setup(l)
            for tile in range(self.ntiles):
                src = self.xin if li == 0 else self.xs
                dst = self.yout if li == nl - 1 else self.xs
                for q in range(4):
                    self.dma("sp", self.xT[:, 4 * q:4 * q + 4, :], src[tile, :, 4 * q:4 * q + 4, :], "xload%d" % q,
                             r=[("xs", tile)] if li > 0 else [], w=[("x", c, hf) for c in range(4 * q, 4 * q + 4) for hf in (0, 1)])
                self.prenorm(l, 0, (0, 1))
                P = self.phases
                if "s" in P:
                    self.phase1_ssm(l, tile)
                if "p" in P:
                    self.phase1_pool(l, tile)
                if "c" in P:
                    self.phase1_conv(l, tile)
                if "g" in P:
                    self.phase1_gmlp(l, tile)
                if "m" in P:
                    self.phase2_merge(l)
                if "3" in P:
                    for hf in (0, 1):
                        self.phase3_wo(l, hf)
                if "4" in P:
                    for hf in (0, 1):
                        self.phase4_ffn(l, hf)
                self.run_steps()
                if self.debug == (li, tile):
                    self.dump_debug()
                for q in range(4):
                    d_ = self.dma("sp", dst[tile, :, 4 * q:4 * q + 4, :], self.xT[:, 4 * q:4 * q + 4, :], "xstore%d" % q,
                                  r=[("x", c, hf) for c in range(4 * q, 4 * q + 4) for hf in (0, 1)],
                                  w=[("xs", tile)])
                    fin.append(d_)
        self.S.add("sp", lambda e: None, deps=fin)

    def dump_debug(self):
        pass

    def layer_setup(self, l):
        t = lambda i: self.spt[:, i, :]
        K = "spt"
        self.dma("sp", self.ssma[:], self.ssm_a_d[:, l, :, :], "c_ssma", w=["ssma"])
        self.dma("sp", self.gln[:, 0, :], self.gln_d[l, 0:1, :].partition_broadcast(128), "c_gln0", w=["gln"])
        self.dma("sp", self.gln[:, 1, :], self.gln_d[l, 1:2, :].partition_broadcast(128), "c_gln1", w=["gln"])
        self.dma("sp", self.gln[:, 2, :], self.gln_d[l, 2:3, :].partition_broadcast(128), "c_gln2", w=["gln"])
        CT = ["ctmp0", "ctmp1", "ctmp2", "ctmp3"]
        self.dma("sp", self.ctmp[:], self.wsT_d[l], "c_wsf", w=CT)
        self.dma("pool", self.poolw[:], self.pool_w[l].rearrange("(g p) d -> p g d", p=128), "c_poolw", w=["poolw"])
        for hd in range(4):
            self.tt(self.wsT[:, hd, :], self.ctmp[:, hd, :], self.maskt[:, :], ALU.mult, r=CT + ["mask"], w=["wsT"])
        ar, ai, ldt = self.ssma[:, 0, :], self.ssma[:, 1, :], self.ssma[:, 2, :]
        dt, mag, ang, c, s, t1, t2, lr, li_, den, lm1, fr, fi, t3 = [t(i) for i in range(14)]
        R, W = ["ssma", K], [K]
        self.act(dt, ldt, AF.Exp, r=R, w=W)
        self.tt(t1, ar, dt, ALU.mult, r=R, w=W)
        self.act(mag, t1, AF.Exp, r=R, w=W)
        self.tt(ang, ai, dt, ALU.mult, r=R, w=W)
        self.act(s, ang, AF.Sin, r=R, w=W, scale=1.0 / 32.0)
        self.ts(t2, ang, 1.0 / 32.0, ALU.mult, r=R, w=W, s2=math.pi / 2.0, op1=ALU.add)
        self.act(c, t2, AF.Sin, r=R, w=W)
        for _ in range(5):
            self.tt(t1, c, c, ALU.mult, r=R, w=W)
            self.tt(t2, s, s, ALU.mult, r=R, w=W)
            self.stt(s, c, 2.0, s, ALU.mult, ALU.mult, r=R, w=W)
            self.tt(c, t1, t2, ALU.subtract, r=R, w=W)
        self.tt(lr, mag, c, ALU.mult, r=R, w=W)
        self.tt(li_, mag, s, ALU.mult, r=R, w=W)
        self.tt(t1, ar, ar, ALU.mult, r=R, w=W)
        self.tt(t2, ai, ai, ALU.mult, r=R, w=W)
        self.tt(den, t1, t2, ALU.add, r=R, w=W)
        self.op("dve", lambda e: e.reciprocal(out=den, in_=den), r=R, w=W)
        self.ts(lm1, lr, -1.0, ALU.add, r=R, w=W)
        self.tt(t1, lm1, ar, ALU.mult, r=R, w=W)
        self.tt(t2, li_, ai, ALU.mult, r=R, w=W)
        self.tt(t1, t1, t2, ALU.add, r=R, w=W)
        self.tt(fr, t1, den, ALU.mult, r=R, w=W)
        self.tt(t1, li_, ar, ALU.mult, r=R, w=W)
        self.tt(t2, lm1, ai, ALU.mult, r=R, w=W)
        self.tt(t1, t1, t2, ALU.subtract, r=R, w=W)
        self.tt(fi, t1, den, ALU.mult, r=R, w=W)
        RP, WP = [K, "P"], ["P"]
        self.cp(self.Pr[:, 0, :], lr, r=RP, w=WP)
        self.cp(self.Pi[:, 0, :], li_, r=RP, w=WP)
        for d in range(9):
            pr, pi = self.Pr[:, d, :], self.Pi[:, d, :]
            self.tt(t1, pr, pr, ALU.mult, r=RP, w=[K])
            self.tt(t2, pi, pi, ALU.mult, r=RP, w=[K])
            self.tt(self.Pr[:, d + 1, :], t1, t2, ALU.subtract, r=RP, w=WP)
            self.stt(self.Pi[:, d + 1, :], pr, 2.0, pi, ALU.mult, ALU.mult, r=RP, w=WP)
        self.ts(self.Pn[:, :, :], self.Pi[:, :, :], -1.0, ALU.mult, r=RP, w=WP)

    def ssm_ctables(self, l):
        K = "spt"
        CK = [("ys", 12), ("ys", 13), ("ys", 14), ("ys", 15)]
        for gp in range(16):
            cr_t, ci_t, u1, u2 = [self.ctmp[:, i, :] for i in range(4)]
            self.dma("sp", cr_t, self.ssm_cT_d[l, 0, gp], "c_cr", w=["ctmp0"])
            self.dma("sp", ci_t, self.ssm_cT_d[l, 1, gp], "c_ci", w=["ctmp1"])
            frg, fig = self.spt[:, 11, gp:gp + 1], self.spt[:, 12, gp:gp + 1]
            self.ts(u1, ci_t, fig, ALU.mult, r=["ctmp1", K], w=["ctmp2"])
            self.stt(self.Cre[:, gp, :], cr_t, frg, u1, ALU.mult, ALU.subtract, r=["ctmp0", "ctmp2", K], w=CK[0:2])
            self.ts(u1, ci_t, frg, ALU.mult, r=["ctmp1", K] + CK[0:2], w=["ctmp2"])
            self.stt(u2, cr_t, fig, u1, ALU.mult, ALU.add, r=["ctmp0", "ctmp2", K], w=["ctmp3"])
            self.ts(self.Cim[:, gp, :], u2, -1.0, ALU.mult, r=["ctmp3"], w=CK[2:4])

    def rstd_from_bank(self, bank, dst, scale):
        self.act(dst, self.ps[:, bank, :], AF.Sqrt, r=[("ps", bank)], w=["rs"], bias=EPS, scale=scale)
        self.op("dve", lambda e: e.reciprocal(out=dst, in_=dst), r=["rs"], w=["rs"])

    def prenorm(self, l, gkind, halves):
        def f():
            for hf in halves:
                sbk = 4 + hf
                for c in range(16):
                    sq = self.sq[:, c % 2, :]
                    self.act(sq, self.xT[:, c, hs(hf)], AF.Square, r=[("x", c, hf)], w=[("sq", c % 2)])
                    self.mm(self.ps[:, sbk, :], self.ones[:, :], sq, start=(c == 0), stop=(c == 15),
                            r=["ones", ("sq", c % 2)], w=[("ps", sbk)])
                rs = self.rs[:, hf, :]
                self.rstd_from_bank(sbk, rs, 1.0 / D)
                for c in range(16):
                    self.stt(self.hT[:, c, hs(hf)], self.xT[:, c, hs(hf)], self.gv[:, l, gkind, c:c + 1], rs,
                             ALU.mult, ALU.mult, r=[("x", c, hf), "gv", "rs"], w=[("h", c, hf)])
        self.step(f)

    def inproj_pairs(self, wv, slot, jj, hf):
        return [(wv[:, kc, jj * 128:(jj + 1) * 128], self.hT[:, kc, hs(hf)], [("h", kc, hf)]) for kc in range(16)]

    def phase1_ssm(self, l, tile):
        SCR = self.SCR
        RC = self.RC
        uT = SCR[:, 0:4096].rearrange("p (c t) -> p c t", c=4)
        zT = RC[:, 8192:12288].rearrange("p (c t) -> p c t", c=4)
        stf = [SCR[:, 4096 + 4096 * i:8192 + 4096 * i].bitcast(F32).rearrange("p (c t) -> p c t", c=2) for i in range(2)]
        stb = [SCR[:, 12288 + 2048 * i:14336 + 2048 * i].rearrange("p (c t) -> p c t", c=2) for i in range(2)]
        CK = [("ys", 12), ("ys", 13), ("ys", 14), ("ys", 15)]
        for t in range(2):
            def f(slot, wv, t=t):
                for jj in range(2):
                    cb = 2 * t + jj
                    for hf in (0, 1):
                        bk = self.nb()
                        self.gemm(bk, self.inproj_pairs(wv, slot, jj, hf), self.wkeys(slot))
                        self.act(uT[:, cb, hs(hf)], self.ps[:, bk, :], AF.Copy, r=[("ps", bk)], w=[("scr", "uT", cb)])
            self.wstep(self.w_in[l, :, 256 * t:256 * t + 256], 16, 256, f)

        def g():
            self.ssm_ctables(l)
            for pr_ in range(8):
                gps = (2 * pr_, 2 * pr_ + 1)
                cb = pr_ // 2
                chains = []
                for bi, gp in enumerate(gps):
                    sk = ("stf", bi)
                    self.dma("pool", self.Bt[:, bi, 0, :], self.ssm_bT_d[l, 0, gp], ("bt", bi, 0), w=[("bt", bi, 0)])
                    self.dma("pool", self.Bt[:, bi, 1, :], self.ssm_bT_d[l, 1, gp], ("bt", bi, 1), w=[("bt", bi, 1)])
                    for ri in range(2):
                        for hf in (0, 1):
                            bk = 4 + ri * 2 + hf
                            self.mm(self.ps[:, bk, :], self.Bt[:, bi, ri, :], uT[:, cb, hs(hf)], True, True,
                                    r=[("bt", bi, ri), ("scr", "uT", cb)], w=[("ps", bk)])
                            self.act(stf[bi][:, ri, hs(hf)], self.ps[:, bk, :], AF.Copy, r=[("ps", bk)], w=[sk])
                    sr, si = stf[bi][:, 0, :], stf[bi][:, 1, :]
                    ops_ = []

                    def cmadd(a_r, a_i, b_r, b_i, d, extra=(), gp=gp, sk=sk, ops_=ops_, a_ri=None, b_ri=None):
                        rk = [sk, "P"] + list(extra)
                        p_r, p_i, p_n = self.Pr[:, d, gp:gp + 1], self.Pi[:, d, gp:gp + 1], self.Pn[:, d, gp:gp + 1]
                        if False and a_ri is not None:
                            ops_.append((a_ri, b_ri, p_r, rk, sk))
                        else:
                            ops_.append((a_r, b_r, p_r, rk, sk))
                            ops_.append((a_i, b_i, p_r, rk, sk))
                        ops_.append((a_r, b_i, p_n, rk, sk))
                        ops_.append((a_i, b_r, p_i, rk, sk))

                    if tile > 0:
                        cmadd(sr[:, 0:1], si[:, 0:1], self.carry[:, gp, 0:1], self.carry[:, gp, 1:2], 0, extra=["carry"])
                    for d in range(10):
                        s_ = 1 << d
                        A = slice(2 * s_ - 1, TT, 2 * s_)
                        Bv = slice(s_ - 1, TT, 2 * s_)
                        cmadd(sr[:, A], si[:, A], sr[:, Bv], si[:, Bv], d, a_ri=stf[bi][:, :, A], b_ri=stf[bi][:, :, Bv])
                    for d in range(8, -1, -1):
                        s_ = 1 << d
                        Bv = slice(2 * s_ - 1, TT - s_, 2 * s_)
                        A = slice(3 * s_ - 1, TT, 2 * s_)
                        cmadd(sr[:, A], si[:, A], sr[:, Bv], si[:, Bv], d, a_ri=stf[bi][:, :, A], b_ri=stf[bi][:, :, Bv])
                    chains.append(ops_)
                for k in range(max(len(c) for c in chains)):
                    for c in chains:
                        if k < len(c):
                            a, b, p, rk, sk = c[k]
                            self.stt(a, b, p, a, ALU.mult, ALU.add, r=rk, w=[sk])
                for bi, gp in enumerate(gps):
                    sk = ("stf", bi)
                    bk_ = ("stb", bi)
                    sr, si = stf[bi][:, 0, :], stf[bi][:, 1, :]
                    self.cp(self.carry[:, gp, 0:1], sr[:, TT - 1:TT], r=[sk], w=["carry"])
                    self.cp(self.carry[:, gp, 1:2], si[:, TT - 1:TT], r=[sk], w=["carry"])
                    self.act(stb[bi][:, 0, :], sr, AF.Copy, r=[sk], w=[bk_])
                    self.act(stb[bi][:, 1, :], si, AF.Copy, r=[sk], w=[bk_])
                    for hf in (0, 1):
                        yb = hf
                        self.mm(self.ps[:, yb, :], self.Cre[:, gp, :], stb[bi][:, 0, hs(hf)], start=(gp % 4 == 0), stop=False,
                                r=CK + [bk_], w=[("ps", yb)])
                        self.mm(self.ps[:, yb, :], self.Cim[:, gp, :], stb[bi][:, 1, hs(hf)], start=False, stop=(gp % 4 == 3),
                                r=CK + [bk_], w=[("ps", yb)])
                if pr_ % 2 == 1:
                    for hf in (0, 1):
                        tf = self.tmpf[:, hf, :]
                        self.stt(tf, uT[:, cb, hs(hf)], self.pvec[:, l, 0, cb:cb + 1], self.ps[:, hf, :], ALU.mult, ALU.add,
                                 r=[("scr", "uT", cb), "pvec", ("ps", hf)], w=[("tmpf", hf)])
                        self.act(zT[:, cb, hs(hf)], tf, AF.Gelu, r=[("tmpf", hf)], w=[("ys", 8 + cb)])
            self.bank_rr = 2
        self.step(g)

        def glu(slot, wv):
            for j in range(4):
                for hf in (0, 1):
                    b1, b2 = self.nb(), self.nb()
                    zk = lambda kc: [("ys", 8 + kc)]
                    self.gemm(b1, [(wv[:, kc, j * 128:(j + 1) * 128], zT[:, kc, hs(hf)], zk(kc)) for kc in range(4)], self.wkeys(slot))
                    self.gemm(b2, [(wv[:, kc, 512 + j * 128:512 + (j + 1) * 128], zT[:, kc, hs(hf)], zk(kc)) for kc in range(4)], self.wkeys(slot))
                    tf = self.tmpf[:, hf, :]
                    self.act(tf, self.ps[:, b2, :], AF.Sigmoid, r=[("ps", b2)], w=[("tmpf", hf)])
                    self.tt(self.ysT[:, j, hs(hf)], self.ps[:, b1, :], tf, ALU.mult, r=[("ps", b1), ("tmpf", hf)], w=[("ys", j)])
        self.wstep(self.w_glu[l], 4, 1024, glu)

    def phase1_pool(self, l, tile):
        SCR = self.SCR
        PW = 15 + TT
        Ub = SCR[:, 0:2 * PW].bitcast(F32)
        Ab = SCR[:, 2 * PW + 2:4 * PW + 2].bitcast(F32)
        Bb = SCR[:, 4 * PW + 4:6 * PW + 4].bitcast(F32)
        pooled = SCR[:, 6 * PW + 6:6 * PW + 6 + TT]
        for t in range(2):
            def f(slot, wv, t=t):
                for jj in range(2):
                    gi = 2 * t + jj
                    w = 2 << gi
                    for hf in (0, 1):
                        bk = self.nb()
                        self.gemm(bk, self.inproj_pairs(wv, slot, jj, hf), self.wkeys(slot))
                        self.act(Ub[:, 15 + hf * H:15 + (hf + 1) * H], self.ps[:, bk, :], AF.Copy, r=[("ps", bk)], w=["pU"])
                    if tile == 0:
                        self.op("dve", lambda e: e.memset(Ub[:, 0:15], 0.0), w=["pU"])
                    else:
                        self.cp(Ub[:, 0:15], self.poolc[:, gi, :], r=["poolc"], w=["pU"])
                    src, srck = Ub, "pU"
                    for k in range(gi + 1):
                        dst, dk = (Ab, "pA") if k % 2 == 0 else (Bb, "pB")
                        lo = (2 << k) - 1
                        sh = 1 << k
                        self.tt(dst[:, lo:PW], src[:, lo:PW], src[:, lo - sh:PW - sh], ALU.add, r=[srck], w=[dk])
                        src, srck = dst, dk
                    self.stt(pooled[:, :], src[:, 15:PW], 1.0 / w, Ub[:, 15:PW], ALU.mult, ALU.subtract, r=[srck, "pU"], w=["pooled"])
                    if tile == 0:
                        tf = self.tmpf[:, 0, 0:15]
                        self.tt(tf, src[:, 15:30], self.invc[:, gi, :], ALU.mult, r=[srck, "invc"], w=[("tmpf", 0)])
                        self.tt(pooled[:, 0:15], tf, Ub[:, 15:30], ALU.subtract, r=[("tmpf", 0), "pU"], w=["pooled"])
                    self.cp(self.poolc[:, gi, :], Ub[:, PW - 15:PW], r=["pU"], w=["poolc"])
                    for hf in (0, 1):
                        bk = self.nb()
                        self.mm(self.ps[:, bk, :], self.poolw[:, gi, :], pooled[:, hs(hf)], True, True,
                                r=["poolw", "pooled"], w=[("ps", bk)])
                        self.act(self.ysT[:, 4 + gi, hs(hf)], self.ps[:, bk, :], AF.Identity, r=[("ps", bk), "pvec"],
                                 w=[("ys", 4 + gi)], scale=self.pvec[:, l, 1, gi:gi + 1])
            self.wstep(self.w_in[l, :, OFF_POOL + 256 * t:OFF_POOL + 256 * t + 256], 16, 256, f)

    def phase1_conv(self, l, tile):
        SCR = self.SCR
        VW = 30 + TT
        vp = [SCR[:, 0:2 * VW].bitcast(F32), SCR[:, 2 * VW + 4:4 * VW + 4].bitcast(F32)]
        base = 4 * VW + 8
        acc = SCR[:, base:base + 8192].bitcast(F32).rearrange("p (c t) -> p c t", c=4)
        for j in range(4):
            def fc(slot, wv, j=j):
                vb = vp[j % 2]
                vk = ("vp", j % 2)
                wk_ = self.wkeys(slot)
                for hf in (0, 1):
                    b1, b2 = self.nb(), self.nb()
                    self.gemm(b1, [(wv[:, kc, :], self.hT[:, kc, hs(hf)], [("h", kc, hf)]) for kc in range(16)], wk_)
                    self.gemm(b2, [(wv[:, 16 + kc, :], self.hT[:, kc, hs(hf)], [("h", kc, hf)]) for kc in range(16)], wk_)
                    tf = self.tmpf[:, hf, :]
                    self.act(tf, self.ps[:, b2, :], AF.Sigmoid, r=[("ps", b2)], w=[("tmpf", hf)])
                    self.tt(vb[:, 30 + hf * H:30 + (hf + 1) * H], self.ps[:, b1, :], tf, ALU.mult,
                            r=[("ps", b1), ("tmpf", hf)], w=[vk])
                if tile == 0:
                    self.op("dve", lambda e, vb=vb: e.memset(vb[:, 0:30], 0.0), w=[vk])
                else:
                    self.cp(vb[:, 0:30], self.convc[:, j, :], r=["convc"], w=[vk])
                self.cp(self.convc[:, j, :], vb[:, VW - 30:VW], r=[vk], w=["convc"])
                a = acc[:, j, :]
                self.ts(a, vb[:, 0:TT], self.convw[:, l, j, 0:1], ALU.mult, r=[vk, "convw", "pvec"], w=[("acc", j)],
                        s2=self.pvec[:, l, 2, j:j + 1], op1=ALU.add)
                for k in range(1, 31):
                    self.stt(a, vb[:, k:k + TT], self.convw[:, l, j, k:k + 1], a, ALU.mult, ALU.add,
                             r=[vk, "convw", ("acc", j)], w=[("acc", j)])
            c0 = OFF_CONV + 128 * j
            self.wstep([(self.w_in[l, :, c0:c0 + 128], 16), (self.w_in[l, :, c0 + 512:c0 + 640], 16)], 32, 128, fc)

        def ln():
            for hf in (0, 1):
                b1, b2 = 4, 5
                for j in range(4):
                    q1 = self.sq[:, 0, :]
                    q2 = self.sq[:, 1, :]
                    self.act(q1, acc[:, j, hs(hf)], AF.Copy, r=[("acc", j)], w=[("sq", 0)])
                    self.act(q2, acc[:, j, hs(hf)], AF.Square, r=[("acc", j)], w=[("sq", 1)])
                    self.mm(self.ps[:, b1, :], self.ones[:, :], q1, start=(j == 0), stop=(j == 3), r=["ones", ("sq", 0)], w=[("ps", b1)])
                    self.mm(self.ps[:, b2, :], self.ones[:, :], q2, start=(j == 0), stop=(j == 3), r=["ones", ("sq", 1)], w=[("ps", b2)])
                mean = self.rs[:, 0, :]
                rstd = self.rs[:, 1, :]
                tf = self.tmpf[:, 0, :]
                self.act(mean, self.ps[:, b1, :], AF.Copy, r=[("ps", b1)], w=["rs"], scale=1.0 / 512.0)
                self.tt(tf, mean, mean, ALU.mult, r=["rs"], w=[("tmpf", 0)])
                self.stt(tf, self.ps[:, b2, :], 1.0 / 512.0, tf, ALU.mult, ALU.subtract, r=[("ps", b2), ("tmpf", 0)], w=[("tmpf", 0)])
                self.act(rstd, tf, AF.Sqrt, r=[("tmpf", 0)], w=["rs"], bias=EPS, scale=1.0)
                self.op("dve", lambda e, rstd=rstd: e.reciprocal(out=rstd, in_=rstd), r=["rs"], w=["rs"])
                for j in range(4):
                    t2 = self.tmpf[:, 1, :]
                    self.tt(t2, acc[:, j, hs(hf)], mean, ALU.subtract, r=[("acc", j), "rs"], w=[("tmpf", 1)])
                    self.tt(t2, t2, rstd, ALU.mult, r=[("tmpf", 1), "rs"], w=[("tmpf", 1)])
                    self.act(self.ysT[:, 8 + j, hs(hf)], t2, AF.Silu, r=[("tmpf", 1), "pvec"], w=[("ys", 8 + j)],
                             scale=self.pvec[:, l, 3, j:j + 1], bias=self.pvec[:, l, 4, j:j + 1])
        self.step(ln)

    def phase1_gmlp(self, l, tile):
        SCR = self.SCR
        ug = SCR[:, 0:4096].rearrange("p (c t) -> p c t", c=4)
        vg = SCR[:, 4096:12288].bitcast(F32).rearrange("p (c t) -> p c t", c=8)
        vn = SCR[:, 12288:16384].rearrange("p (c t) -> p c t", c=8)
        st = self.st4
        for t in range(2):
            def fu(slot, wv, t=t):
                for jj in range(2):
                    j = 2 * t + jj
                    for hf in (0, 1):
                        bk = self.nb()
                        self.gemm(bk, self.inproj_pairs(wv, slot, jj, hf), self.wkeys(slot))
                        self.act(ug[:, j, hs(hf)], self.ps[:, bk, :], AF.Gelu, r=[("ps", bk)], w=[("scr", "ug", j)])
            self.wstep(self.w_in[l, :, OFF_GMLP + 256 * t:OFF_GMLP + 256 * t + 256], 16, 256, fu)
        for ch in range(2):
            def fv(slot, wv, ch=ch):
                for tt_ in range(8):
                    bk = self.nb()
                    pairs = [(self.hT[:, kc, tt_ * 128:(tt_ + 1) * 128], wv[:, kc, :], [("h", kc, tt_ // 4)]) for kc in range(16)]
                    self.gemm(bk, pairs, self.wkeys(slot), n=256)
                    self.act(vg[:, tt_, ch * 256:(ch + 1) * 256], self.ps[:, bk, 0:256], AF.Gelu, r=[("ps", bk)],
                             w=[("scr", "vg", tt_)], accum_out=st[:, tt_, ch:ch + 1])
            self.wstep(self.w_in[l, :, OFF_GMLP + 512 + 256 * ch:OFF_GMLP + 512 + 256 * ch + 256], 16, 256, fv)

        def g():
            for tt_ in range(8):
                R = [("scr", "vg", tt_), "st4"]
                W = ["st4"]
                s1, s2, mean, var, rstd, nb_ = [st[:, tt_, i:i + 1] for i in range(2, 8)]
                tf = self.tmpf[:, tt_ % 2, :]
                tk = ("tmpf", tt_ % 2)
                self.act(tf, vg[:, tt_, :], AF.Square, r=R, w=[tk, "st4"], accum_out=s2)
                self.tt(s1, st[:, tt_, 0:1], st[:, tt_, 1:2], ALU.add, r=R, w=W)
                self.ts(mean, s1, 1.0 / 512.0, ALU.mult, r=R, w=W)
                self.tt(var, mean, mean, ALU.mult, r=R, w=W)
                self.stt(var, s2, 1.0 / 512.0, var, ALU.mult, ALU.subtract, r=R, w=W)
                self.act(rstd, var, AF.Sqrt, r=R, w=W, bias=EPS, scale=1.0)
                self.op("dve", lambda e, rstd=rstd: e.reciprocal(out=rstd, in_=rstd), r=R, w=W)
                self.stt(nb_, mean, -1.0, rstd, ALU.mult, ALU.mult, r=R, w=W)
                self.act(tf, vg[:, tt_, :], AF.Identity, r=R + [tk], w=[tk], scale=rstd, bias=nb_)
                self.tt(tf, tf, self.gln[:, 0, :], ALU.mult, r=[tk, "gln"], w=[tk])
                self.tt(vn[:, tt_, :], tf, self.gln[:, 1, :], ALU.add, r=[tk, "gln"], w=[("scr", "vn", tt_)])
            for hd in range(4):
                for hf in (0, 1):
                    bk = self.nb()
                    for q in range(4):
                        tt_ = hf * 4 + q
                        self.mm(self.ps[:, bk, q * 128:(q + 1) * 128], vn[:, tt_, hd * 128:(hd + 1) * 128], self.wsT[:, hd, :],
                                True, True, r=[("scr", "vn", tt_), "wsT"], w=[("ps", bk)])
                    tf = self.tmpf[:, hf, :]
                    tk = ("tmpf", hf)
                    self.tt(tf.rearrange("p (q t) -> p q t", q=4), self.ps[:, bk, :].rearrange("p (q t) -> p q t", q=4),
                            self.gln[:, 2, hd * 128:(hd + 1) * 128].unsqueeze(1).to_broadcast([128, 4, 128]), ALU.add,
                            r=[("ps", bk), "gln"], w=[tk])
                    self.tt(self.ysT[:, 12 + hd, hs(hf)], tf, ug[:, hd, hs(hf)], ALU.mult, r=[tk, ("scr", "ug", hd)], w=[("ys", 12 + hd)])
        self.step(g)

    def phase2_merge(self, l):
        for j in range(16):
            for k in range(4):
                def fg(slot, wv, j=j, k=k):
                    wk_ = self.wkeys(slot)
                    for hf in (0, 1):
                        bg, by = self.nb(), self.nb()
                        ab = 4 + hf
                        self.gemm(bg, [(wv[:, kc, :], self.hT[:, kc, hs(hf)], [("h", kc, hf)]) for kc in range(16)], wk_)
                        self.gemm(by, [(wv[:, 16 + kc, :], self.ysT[:, 4 * k + kc, hs(hf)], [("ys", 4 * k + kc)]) for kc in range(4)], wk_)
                        sg = self.rs[:, hf, :]
                        self.act(sg, self.ps[:, bg, :], AF.Sigmoid, r=[("ps", bg)], w=[("rsg", hf)])
                        if k == 0:
                            self.tt(self.ps[:, ab, :], self.ps[:, by, :], sg, ALU.mult, r=[("ps", by), ("rsg", hf)], w=[("ps", ab)])
                        else:
                            self.tt(sg, self.ps[:, by, :], sg, ALU.mult, r=[("ps", by), ("rsg", hf)], w=[("rsg", hf)])
                            if k < 3:
                                self.tt(self.ps[:, ab, :], self.ps[:, ab, :], sg, ALU.add, r=[("rsg", hf), ("ps", ab)], w=[("ps", ab)])
                            else:
                                self.tt(self.mgT[:, j, hs(hf)], self.ps[:, ab, :], sg, ALU.add, r=[("rsg", hf), ("ps", ab)], w=[("mg", j)])
                c0 = OFF_GATE + k * D + 128 * j
                self.wstep([(self.w_in[l, :, c0:c0 + 128], 16), (self.w_branch[l, 512 * k:512 * k + 512, 128 * j:128 * j + 128], 4)],
                           20, 128, fg)

    def norm_resid(self, l, gkind, hf, src_chunks, src_keys):
        rs = self.rs[:, hf, :]
        self.rstd_from_bank(6, rs, 1.0 / D)
        for c in range(16):
            tf = self.tmpf[:, c % 2, :]
            tk = ("tmpf", c % 2)
            self.stt(tf, src_chunks[c], self.gv[:, l, gkind, c:c + 1], rs, ALU.mult, ALU.mult, r=[src_keys[c], "gv", "rs"], w=[tk])
            self.tt(self.xT[:, c, hs(hf)], self.xT[:, c, hs(hf)], tf, ALU.add, r=[tk, ("x", c, hf)], w=[("x", c, hf)])

    def phase3_wo(self, l, hf):
        for j in range(16):
            def f(slot, wv, j=j):
                bk = self.nb()
                self.gemm(bk, [(wv[:, kc, :], self.mgT[:, kc, hs(hf)], [("mg", kc)]) for kc in range(16)], self.wkeys(slot))
                self.act(self.mixF[:, j, :], self.ps[:, bk, :], AF.Copy, r=[("ps", bk)], w=[("ys", j)])
                sq = self.sq[:, j % 2, :]
                self.act(sq, self.ps[:, bk, :], AF.Square, r=[("ps", bk)], w=[("sq", j % 2)])
                self.mm(self.ps[:, 6, :], self.ones[:, :], sq, start=(j == 0), stop=(j == 15), r=["ones", ("sq", j % 2)], w=[("ps", 6)])
            self.wstep(self.w_o[l, :, 128 * j:128 * j + 128], 16, 128, f)
        self.step(lambda: self.norm_resid(l, 1, hf, [self.mixF[:, c, :] for c in range(16)], [("ys", c) for c in range(16)]))

    def phase4_ffn(self, l, hf):
        self.prenorm(l, 2, (hf,))
        oh = 1 - hf
        fS = [self.hT[:, c, hs(oh)] for c in range(16)]
        for j in range(64):
            def f1(slot, wv, j=j):
                bk = self.nb()
                self.gemm(bk, [(wv[:, kc, :], self.hT[:, kc, hs(hf)], [("h", kc, hf)]) for kc in range(16)], self.wkeys(slot))
                tf = self.tmpf[:, j % 2, :]
                tk = ("tmpf", j % 2)
                self.act(tf, self.ps[:, bk, :], AF.Relu, r=[("ps", bk)], w=[tk])
                key = ("ys", j // 2) if j < 32 else ("mg", (j - 32) // 2)
                self.tt(self.f1T[:, j, :], tf, tf, ALU.mult, r=[tk], w=[key])
            self.wstep(self.w_ff1[l, :, 128 * j:128 * j + 128], 16, 128, f1)
        for j in range(16):
            for q in range(4):
                def f2(slot, wv, j=j, q=q):
                    bk = j % 2
                    pairs = []
                    for kc in range(16):
                        fc = q * 16 + kc
                        key = ("ys", fc // 2) if fc < 32 else ("mg", (fc - 32) // 2)
                        pairs.append((wv[:, kc, :], self.f1T[:, fc, :], [key]))
                    self.gemm(bk, pairs, self.wkeys(slot), first=(q == 0), last=(q == 3))
                    if q == 3:
                        self.act(fS[j], self.ps[:, bk, :], AF.Copy, r=[("ps", bk)], w=[("h", j, oh)])
                        sq = self.sq[:, j % 2, :]
                        self.act(sq, self.ps[:, bk, :], AF.Square, r=[("ps", bk)], w=[("sq", j % 2)])
                        self.mm(self.ps[:, 6, :], self.ones[:, :], sq, start=(j == 0), stop=(j == 15),
                                r=["ones", ("sq", j % 2)], w=[("ps", 6)])
                    self.bank_rr = 2
                self.wstep(self.w_ff2[l, 2048 * q:2048 * q + 2048, 128 * j:128 * j + 128], 16, 128, f2)
        self.step(lambda: self.norm_resid(l, 3, hf, fS, [("h", c, oh) for c in range(16)]))


_NC_CACHE = {}


def _host_layout(inp, wt_tensors):
    f = lambda a: np.ascontiguousarray(np.asarray(a, dtype=np.float32))
    x = f(inp["x"])
    per_core = []
    for b in range(NB):
        xb = x[b].reshape(NT, TT, 16, 128).transpose(0, 3, 2, 1)
        per_core.append(f(xb))
    sh = {}
    sh["pool_w"] = f(inp["pool_w"]).reshape(L, 512, 128)
    raw = {"w_in": inp["w_in"], "w_glu": inp["ssm_w_glu"], "w_branch": np.asarray(inp["w_branch"]).reshape(L, 2048, D),
           "w_o": inp["w_o"], "w_ff1": inp["w_ff1"], "w_ff2": inp["w_ff2"]}
    for name, _ap, used, recs in wt_tensors:
        buf = np.zeros((128, WCAP), np.float32)
        for (parts, ncols), off, n in recs:
            o = off
            for (wname, l_, r0, nrows, c0, nc_), kk in parts:
                assert nc_ == ncols and nrows == kk * 128
                a = np.asarray(raw[wname][l_][r0:r0 + nrows, c0:c0 + ncols], dtype=np.float32)
                buf[:, o:o + kk * ncols] = a.reshape(kk, 128, ncols).transpose(1, 0, 2).reshape(128, kk * ncols)
                o += kk * ncols
        sh[name] = buf
    gv = np.stack([f(inp[k]) for k in ("g_pre_mix", "g_post_mix", "g_pre_mlp", "g_post_mlp")], axis=1)
    sh["gv"] = f(gv.reshape(L, 4, 16, 128).transpose(3, 0, 1, 2))
    a_re, a_im, ldt = f(inp["ssm_a_re"]), f(inp["ssm_a_im"]), f(inp["ssm_log_dt"])
    ldt_b = np.broadcast_to(ldt[:, :, None], a_re.shape)
    a3 = np.stack([a_re, a_im, ldt_b], axis=1)
    a3 = a3.reshape(L, 3, 16, 2, 64).transpose(3, 4, 0, 1, 2)
    sh["ssm_a"] = f(a3.reshape(128, L, 3, 16))
    bT = np.zeros((L, 2, 16, 128, 128), np.float32)
    cT = np.zeros((L, 2, 16, 128, 128), np.float32)
    br, bi = f(inp["ssm_b_re"]), f(inp["ssm_b_im"])
    cr, ci = f(inp["ssm_c_re"]), f(inp["ssm_c_im"])
    for gp in range(16):
        for g2 in range(2):
            g = 2 * gp + g2
            g8 = g % 8
            for ri, (bb, cc) in enumerate(((br, cr), (bi, ci))):
                bT[:, ri, gp, g8 * 16:(g8 + 1) * 16, g2 * 64:(g2 + 1) * 64] = bb[:, g].transpose(0, 2, 1)
                cT[:, ri, gp, g2 * 64:(g2 + 1) * 64, g8 * 16:(g8 + 1) * 16] = cc[:, g].transpose(0, 2, 1)
    sh["ssm_bT"] = bT
    sh["ssm_cT"] = cT
    pv = np.stack([f(inp[k]) for k in ("ssm_d", "pool_scale", "conv_b", "conv_ln_g", "conv_ln_b")], axis=1)
    sh["pvec"] = f(pv.reshape(L, 5, 4, 128).transpose(3, 0, 1, 2))
    cw = f(inp["conv_w"])
    sh["convw"] = f(cw.reshape(L, 31, 4, 128).transpose(3, 0, 2, 1))
    sh["gln"] = f(np.stack([f(inp["gmlp_ln_g"]), f(inp["gmlp_ln_b"]), f(inp["gmlp_bs"]).reshape(L, 512)], axis=1))
    ws = f(inp["gmlp_ws"])
    sh["wsT"] = f(ws.transpose(0, 3, 1, 2))
    s_idx = np.arange(128)[:, None]
    t_idx = np.arange(128)[None, :]
    sh["mask"] = (t_idx >= s_idx).astype(np.float32)
    inv = np.zeros((128, 4, 15), np.float32)
    for gi in range(4):
        w = 2 << gi
        inv[:, gi, :] = 1.0 / np.minimum(np.arange(15) + 1, w)
    sh["invcnt"] = inv
    return per_core, sh


def kernel(**inputs):
    key = "full"
    if key not in _NC_CACHE:
        b = Builder(layers=list(range(L)))
        _NC_CACHE[key] = (b.build(), b.wt_tensors)
    nc, wt_tensors = _NC_CACHE[key]
    per_core, sh = _host_layout(inputs, wt_tensors)
    real = {0: 0, 1: 1, 4: 2, 5: 3}
    zsh = {}
    for k, v in sh.items():
        zsh[k] = np.zeros_like(v) if k.startswith("wt") else v
    zx = np.zeros_like(per_core[0])
    in_maps = []
    for c in range(8):
        if c in real:
            m = dict(sh)
            m["xin"] = per_core[real[c]]
        else:
            m = dict(zsh)
            m["xin"] = zx
        in_maps.append(m)
    res = run_bass_kernel_spmd(nc, in_maps, core_ids=list(range(8)))
    out = np.empty((NB, SEQ, D), np.float32)
    for c, b in real.items():
        y = np.asarray(res.results[c]["yout"])
        out[b] = y.transpose(0, 3, 2, 1).reshape(SEQ, D)
    return out
```

```python
import contextlib
import math
import numpy as np
import concourse.bass as bass
import concourse.mybir as mybir
from concourse.bass_utils import run_bass_kernel_spmd

F32 = mybir.dt.float32
BF16 = mybir.dt.bfloat16
AF = mybir.ActivationFunctionType
ALU = mybir.AluOpType

D = 2048
SEQ = 2048
NB = 4
L = 2
TT = 1024
NT = SEQ // TT
H = 512
IN_COLS = 11264
OFF_POOL, OFF_CONV, OFF_GMLP, OFF_GATE = 512, 1024, 2048, 3072
DFF = 8192
EPS = 1e-6
NUNIT = 4
UNIT = 2048
WSLOT = 2 * UNIT
WCAP = 294912


class Sched:
    def __init__(self, samesync=True):
        self.ops = []
        self.last_w = {}
        self.readers = {}
        self.last_dma = {}
        self.samesync = samesync

    def add(self, eng, fn, reads=(), writes=(), dma_key=None, deps=()):
        idx = len(self.ops)
        d = set(deps)
        for k in reads:
            w = self.last_w.get(k)
            if w is not None:
                d.add(w)
        for k in writes:
            w = self.last_w.get(k)
            if w is not None:
                d.add(w)
            for r in self.readers.get(k, {}).values():
                d.add(r)
        if dma_key is not None:
            p = self.last_dma.get(dma_key)
            if p is not None:
                d.add(p)
            self.last_dma[dma_key] = idx
        d.discard(idx)
        stream = ("dma", idx) if dma_key is not None else eng
        for k in reads:
            self.readers.setdefault(k, {})[stream] = idx
        for k in writes:
            self.last_w[k] = idx
            self.readers[k] = {}
        self.ops.append(dict(eng=eng, fn=fn, deps=sorted(d), dma_key=dma_key, observed=False, waits=[]))
        return idx

    def finalize(self):
        ops = self.ops
        known = {e: {} for e in ("pe", "act", "dve", "pool", "sp")}
        snaps = [None] * len(ops)
        for i, op in enumerate(ops):
            e = op["eng"]
            kn = known[e]
            best = {}
            for j in op["deps"]:
                pj = ops[j]
                if pj["dma_key"] is not None:
                    key = ("dma", pj["dma_key"])
                else:
                    key = pj["eng"]
                    if key == e and op["dma_key"] is None and (not self.samesync or e == "pe"):
                        continue
                if kn.get(key, -1) >= j:
                    continue
                if best.get(key, -1) < j:
                    best[key] = j
            for key, j in sorted(best.items(), key=lambda kv: kv[1]):
                if kn.get(key, -1) >= j:
                    continue
                ops[j]["observed"] = True
                op["waits"].append(j)
                kn[key] = j
                for k2, v2 in snaps[j].items():
                    if kn.get(k2, -1) < v2:
                        kn[k2] = v2
            snaps[i] = dict(kn)
        cnt = {}
        for op in ops:
            if op["dma_key"] is not None:
                key = ("dma", op["dma_key"])
                cnt[key] = cnt.get(key, 0) + 16
                op["token"] = (key, cnt[key])
            elif op["observed"]:
                key = op["eng"]
                cnt[key] = cnt.get(key, 0) + 1
                op["token"] = (key, cnt[key])
            else:
                op["token"] = None
        self.sem_keys = sorted(cnt.keys(), key=str)
        return self

    def stats(self):
        from collections import Counter
        c = Counter(op["eng"] for op in self.ops)
        return dict(ops=dict(c), waits=sum(len(op["waits"]) for op in self.ops),
                    observed=sum(1 for op in self.ops if op["observed"]), sems=len(self.sem_keys))

    def run(self, engine_obj, ename, sems):
        ops = self.ops
        for op in ops:
            if op["eng"] != ename:
                continue
            for j in op["waits"]:
                key, val = ops[j]["token"]
                engine_obj.wait_ge(sems[key], val)
            ins = op["fn"](engine_obj)
            if op["token"] is not None and ins is not None:
                key, val = op["token"]
                ins.then_inc(sems[key], 16 if isinstance(key, tuple) else 1)


def hs(hf):
    return slice(hf * H, (hf + 1) * H)


class WSpec:
    def __init__(self, name, K, N):
        self.name, self.K, self.N = name, K, N

    def __getitem__(self, idx):
        if not isinstance(idx, tuple):
            idx = (idx, slice(None), slice(None))
        l, rs, cs = idx
        r0, r1, _ = rs.indices(self.K)
        c0, c1, _ = cs.indices(self.N)
        return (self.name, l, r0, r1 - r0, c0, c1 - c0)


class Builder:
    def __init__(self, layers, debug=None, ntiles=NT, phases="spcgm34"):
        self.ntiles = ntiles
        self.phases = phases
        self.layers = layers
        self.debug = debug
        self.S = Sched()
        self.steps = []
        self.bank_rr = 0

    def op(self, eng, fn, r=(), w=()):
        return self.S.add(eng, fn, reads=r, writes=w)

    def dma(self, eng, out, in_, key, r=(), w=()):
        return self.S.add(eng, lambda e: e.dma_start(out=out, in_=in_), reads=r, writes=w, dma_key=key)

    def mm(self, out, lhsT, rhs, start, stop, r, w):
        return self.op("pe", lambda e: e.matmul(out, lhsT, rhs, start=start, stop=stop), r=r, w=w)

    def act(self, out, in_, func, r, w, bias=None, scale=None, accum_out=None):
        kw = {}
        if bias is not None:
            kw["bias"] = bias
        if scale is not None:
            kw["scale"] = scale
        if accum_out is not None:
            kw["accum_out"] = accum_out
        return self.op("act", lambda e: e.activation(out=out, in_=in_, func=func, **kw), r=r, w=w)

    def tt(self, out, in0, in1, op_, r, w, eng="dve"):
        return self.op(eng, lambda e: e.tensor_tensor(out=out, in0=in0, in1=in1, op=op_), r=r, w=w)

    def ts(self, out, in0, s1, op0, r, w, s2=None, op1=None, eng="dve"):
        if op1 is None:
            return self.op(eng, lambda e: e.tensor_scalar(out=out, in0=in0, scalar1=s1, scalar2=None, op0=op0), r=r, w=w)
        return self.op(eng, lambda e: e.tensor_scalar(out=out, in0=in0, scalar1=s1, scalar2=s2, op0=op0, op1=op1), r=r, w=w)

    def stt(self, out, in0, scalar, in1, op0, op1, r, w):
        return self.op("dve", lambda e: e.scalar_tensor_tensor(out=out, in0=in0, scalar=scalar, in1=in1, op0=op0, op1=op1), r=r, w=w)

    def cp(self, out, in_, r, w, eng="dve"):
        return self.op(eng, lambda e: e.tensor_copy(out=out, in_=in_), r=r, w=w)

    def nb(self):
        b = self.bank_rr
        self.bank_rr = (self.bank_rr + 1) % 4
        return b

    def wstep(self, src, kc, ncols, fn):
        parts = tuple(src) if isinstance(src, list) else ((src, kc),)
        n = sum(k for _, k in parts) * ncols
        assert n <= WSLOT
        key = (parts, ncols)
        if key not in self.cat:
            if not self.wt_tensors or self.wt_tensors[-1][2] + n > WCAP:
                name = "wt%d" % len(self.wt_tensors)
                self.wt_tensors.append([name, self._dr(name, [128, WCAP]), 0, []])
            tns = self.wt_tensors[-1]
            self.cat[key] = (len(self.wt_tensors) - 1, tns[2], n)
            tns[3].append((key, tns[2], n))
            tns[2] += n
        self.steps.append((self.cat[key], parts, ncols, fn))

    def step(self, fn):
        self.steps.append((None, None, 0, fn))

    def wkeys(self, slot):
        u0, cnt = slot
        return [("w", u0 + i) for i in range(cnt)]

    def run_steps(self):
        steps = self.steps
        tiles = [i for i, s in enumerate(steps) if s[0] is not None]
        owner = [-1] * NUNIT
        loaded = {}
        st = dict(next=0, ptr=0)

        def try_issue(cur):
            while st["next"] < len(tiles):
                i = tiles[st["next"]]
                (ti, off, n), parts, ncols, _ = steps[i]
                cnt = 1 if n <= UNIT else 2
                p = st["ptr"]
                if cnt == 2 and p % 2 == 1:
                    p = (p + 1) % NUNIT
                if p + cnt > NUNIT:
                    p = 0
                if any(owner[p + k] >= cur for k in range(cnt)):
                    return
                kct = sum(kk for _, kk in parts)
                dst = self.wring[:, p * UNIT:p * UNIT + n]
                self.dma("pool", dst, self.wt_tensors[ti][1][:, off:off + n], key=("w", p), w=[("w", p + k) for k in range(cnt)])
                for k in range(cnt):
                    owner[p + k] = i
                loaded[i] = ((p, cnt), dst.rearrange("p (c n) -> p c n", c=kct))
                st["ptr"] = (p + cnt) % NUNIT
                st["next"] += 1

        for i, s in enumerate(steps):
            try_issue(i)
            if s[0] is not None:
                assert i in loaded
                owner_save = None
                s[3](*loaded[i])
            else:
                s[3]()
        self.steps = []

    def gemm(self, bank, pairs, rkeys, first=True, last=True, n=H):
        np_ = len(pairs)
        for i, (lt, rh, rk) in enumerate(pairs):
            self.mm(self.ps[:, bank, 0:n], lt, rh, start=(first and i == 0), stop=(last and i == np_ - 1),
                    r=list(rkeys) + list(rk), w=[("ps", bank)])

    def build(self):
        nc = bass.Bass("TRN2", target_bir_lowering=False)
        self.nc = nc
        es = contextlib.ExitStack()
        self.es = es
        dr = lambda name, shape, kind="ExternalInput": nc.dram_tensor(name, list(shape), F32, kind=kind).ap()
        self.xin = dr("xin", [NT, 128, 16, TT])
        self.yout = dr("yout", [NT, 128, 16, TT], "ExternalOutput")
        self.xs = self.yout
        self.w_in = WSpec("w_in", D, IN_COLS)
        self.w_glu = WSpec("w_glu", 512, 1024)
        self.pool_w = dr("pool_w", [L, 512, 128])
        self.w_branch = WSpec("w_branch", 2048, D)
        self.w_o = WSpec("w_o", D, D)
        self.w_ff1 = WSpec("w_ff1", D, DFF)
        self.w_ff2 = WSpec("w_ff2", DFF, D)
        self.cat = {}
        self.wt_tensors = []
        self._dr = dr
        self.gv_d = dr("gv", [128, L, 4, 16])
        self.ssm_a_d = dr("ssm_a", [128, L, 3, 16])
        self.ssm_bT_d = dr("ssm_bT", [L, 2, 16, 128, 128])
        self.ssm_cT_d = dr("ssm_cT", [L, 2, 16, 128, 128])
        self.pvec_d = dr("pvec", [128, L, 5, 4])
        self.convw_d = dr("convw", [128, L, 4, 31])
        self.gln_d = dr("gln", [L, 3, 512])
        self.wsT_d = dr("wsT", [L, 128, 4, 128])
        self.mask_d = dr("mask", [128, 128])
        self.invcnt_d = dr("invcnt", [128, 4, 15])
        self.ident_d = dr("ident", [128, 128])
        if self.debug:
            self.dbg = dr("dbg", [128, 16, TT], "ExternalOutput")

        sb = lambda name, shape, dt=F32: es.enter_context(nc.sbuf_tensor(name, list(shape), dt))
        with es:
            self.xT = sb("xT", [128, 16, TT])
            self.hT = sb("hT", [128, 16, TT], BF16)
            self.RC = sb("RC", [128, 32768], BF16)
            self.wring = sb("wring", [128, NUNIT * UNIT], BF16)
            self.ps = es.enter_context(nc.psum_tensor("ps", [128, 8, H], F32))
            RC = self.RC
            self.ysT = RC[:, 0:16384].rearrange("p (c t) -> p c t", c=16)
            self.mgT = RC[:, 16384:32768].rearrange("p (c t) -> p c t", c=16)
            self.mixF = RC[:, 0:16384].bitcast(F32).rearrange("p (c t) -> p c t", c=16)
            self.f1T = RC[:, :].rearrange("p (c t) -> p c t", c=64)
            SCR = RC[:, 16384:32768]
            self.SCR = SCR
            self.ones = sb("ones", [128, 128], BF16)
            self.gv = sb("gvs", [128, L, 4, 16])
            self.pvec = sb("pvecs", [128, L, 5, 4])
            self.convw = sb("convws", [128, L, 4, 31])
            self.ssma = sb("ssmas", [128, 3, 16])
            self.spt = sb("spt", [128, 16, 16])
            self.Pr = sb("Pr", [128, 10, 16])
            self.Pi = sb("Pi", [128, 10, 16])
            self.Pn = sb("Pn", [128, 10, 16])
            self.Cre = RC[:, 12288:14336].rearrange("p (c t) -> p c t", c=16)
            self.Cim = RC[:, 14336:16384].rearrange("p (c t) -> p c t", c=16)
            self.ctmp = sb("ctmp", [128, 4, 128])
            self.Bt = sb("Bt", [128, 2, 2, 128], BF16)
            self.carry = sb("carry", [128, 16, 2])
            self.poolc = sb("poolc", [128, 4, 15])
            self.convc = sb("convc", [128, 4, 30])
            self.identb = sb("identb", [128, 128], BF16)
            self.gln = sb("glns", [128, 3, 512])
            self.wsT = sb("wsTs", [128, 4, 128], BF16)
            self.maskt = sb("maskt", [128, 128])
            self.invc = sb("invc", [128, 4, 15])
            self.poolw = sb("poolw", [128, 4, 128], BF16)
            self.rs = sb("rs", [128, 2, H])
            self.tmpf = sb("tmpf", [128, 2, H])
            self.sq = sb("sq", [128, 2, H], BF16)
            self.st4 = sb("st4", [128, 8, 8])

            self.emit_all()

            self.S.finalize()
            print("sched:", self.S.stats(), "sbuf_remaining", nc.sbuf_bytes_remaining, flush=True)
            sems = {k: es.enter_context(nc.semaphore("sem%d" % i)) for i, k in enumerate(self.S.sem_keys)}
            block = es.enter_context(nc.Block())
            S = self.S

            @block.tensor
            def _(e):
                S.run(e, "pe", sems)

            @block.scalar
            def _(e):
                S.run(e, "act", sems)

            @block.vector
            def _(e):
                S.run(e, "dve", sems)

            @block.gpsimd
            def _(e):
                S.run(e, "pool", sems)

            @block.sync
            def _(e):
                S.run(e, "sp", sems)
        return nc

    def emit_all(self):
        self.op("dve", lambda e: e.memset(self.ones[:, :], 1.0), w=["ones"])
        self.dma("sp", self.gv[:], self.gv_d, "c_gv", w=["gv"])
        self.dma("sp", self.pvec[:], self.pvec_d, "c_pvec", w=["pvec"])
        self.dma("sp", self.convw[:], self.convw_d, "c_convw", w=["convw"])
        self.dma("sp", self.maskt[:], self.mask_d, "c_mask", w=["mask"])
        self.dma("sp", self.invc[:], self.invcnt_d, "c_invc", w=["invc"])
        self.dma("pool", self.identb[:], self.ident_d, "c_ident", w=["identb"])
        fin = []
        nl = len(self.layers)
        passes = [(li, l, tile) for li, l in enumerate(self.layers) for tile in range(self.ntiles)]

        def xload(pi, hf):
            li, l, tile = passes[pi]
            src = self.xin if li == 0 else self.xs
            for q in range(2):
                self.dma("sp", self.xT[:, 8 * q:8 * q + 8, hs(hf)], src[tile, :, 8 * q:8 * q + 8, hs(hf)], "xload%d_%d" % (hf, q),
                         r=[("xs", tile, hf)] if li > 0 else [], w=[("x", c, hf) for c in range(8 * q, 8 * q + 8)])

        def xstore(pi, hf):
            li, l, tile = passes[pi]
            dst = self.yout if li == nl - 1 else self.xs
            for q in range(2):
                d_ = self.dma("sp", dst[tile, :, 8 * q:8 * q + 8, hs(hf)], self.xT[:, 8 * q:8 * q + 8, hs(hf)], "xstore%d_%d" % (hf, q),
                              r=[("x", c, hf) for c in range(8 * q, 8 * q + 8)], w=[("xs", tile, hf)])
                if li == nl - 1:
                    fin.append(d_)

        def after_half(pi, hf):
            xstore(pi, hf)
            if pi + 1 < len(passes):
                xload(pi + 1, hf)

        xload(0, 0)
        xload(0, 1)
        for pi, (li, l, tile) in enumerate(passes):
            if tile == 0:
                self.layer_setup(l)
            self.prenorm(l, 0, (0, 1))
            P = self.phases
            if "s" in P:
                self.phase1_ssm(l, tile)
            if "p" in P:
                self.phase1_pool(l, tile)
            if "c" in P:
                self.phase1_conv(l, tile)
            if "g" in P:
                self.phase1_gmlp(l, tile)
            if "m" in P:
                self.phase2_merge(l)
            if "3" in P:
                for hf in (0, 1):
                    self.phase3_wo(l, hf)
            for hf in (0, 1):
                if "4" in P:
                    self.phase4_ffn(l, hf)
                self.step(lambda pi=pi, hf=hf: after_half(pi, hf))
            self.run_steps()
        self.S.add("sp", lambda e: None, deps=fin)

    def dump_debug(self):
        pass

    def layer_setup(self, l):
        t = lambda i: self.spt[:, i, :]
        K = "spt"
        self.dma("sp", self.ssma[:], self.ssm_a_d[:, l, :, :], "c_ssma", w=["ssma"])
        self.dma("sp", self.gln[:, 0, :], self.gln_d[l, 0:1, :].partition_broadcast(128), "c_gln0", w=["gln"])
        self.dma("sp", self.gln[:, 1, :], self.gln_d[l, 1:2, :].partition_broadcast(128), "c_gln1", w=["gln"])
        self.dma("sp", self.gln[:, 2, :], self.gln_d[l, 2:3, :].partition_broadcast(128), "c_gln2", w=["gln"])
        CT = ["ctmp0", "ctmp1", "ctmp2", "ctmp3"]
        self.dma("sp", self.ctmp[:], self.wsT_d[l], "c_wsf", w=CT)
        self.dma("pool", self.poolw[:], self.pool_w[l].rearrange("(g p) d -> p g d", p=128), "c_poolw", w=["poolw"])
        for hd in range(4):
            self.tt(self.wsT[:, hd, :], self.ctmp[:, hd, :], self.maskt[:, :], ALU.mult, r=CT + ["mask"], w=["wsT"])
        ar, ai, ldt = self.ssma[:, 0, :], self.ssma[:, 1, :], self.ssma[:, 2, :]
        dt, mag, ang, c, s, t1, t2, lr, li_, den, lm1, fr, fi, t3 = [t(i) for i in range(14)]
        R, W = ["ssma", K], [K]
        self.act(dt, ldt, AF.Exp, r=R, w=W)
        self.tt(t1, ar, dt, ALU.mult, r=R, w=W)
        self.act(mag, t1, AF.Exp, r=R, w=W)
        self.tt(ang, ai, dt, ALU.mult, r=R, w=W)
        self.act(s, ang, AF.Sin, r=R, w=W, scale=1.0 / 32.0)
        self.ts(t2, ang, 1.0 / 32.0, ALU.mult, r=R, w=W, s2=math.pi / 2.0, op1=ALU.add)
        self.act(c, t2, AF.Sin, r=R, w=W)
        for _ in range(5):
            self.tt(t1, c, c, ALU.mult, r=R, w=W)
            self.tt(t2, s, s, ALU.mult, r=R, w=W)
            self.stt(s, c, 2.0, s, ALU.mult, ALU.mult, r=R, w=W)
            self.tt(c, t1, t2, ALU.subtract, r=R, w=W)
        self.tt(lr, mag, c, ALU.mult, r=R, w=W)
        self.tt(li_, mag, s, ALU.mult, r=R, w=W)
        self.tt(t1, ar, ar, ALU.mult, r=R, w=W)
        self.tt(t2, ai, ai, ALU.mult, r=R, w=W)
        self.tt(den, t1, t2, ALU.add, r=R, w=W)
        self.op("dve", lambda e: e.reciprocal(out=den, in_=den), r=R, w=W)
        self.ts(lm1, lr, -1.0, ALU.add, r=R, w=W)
        self.tt(t1, lm1, ar, ALU.mult, r=R, w=W)
        self.tt(t2, li_, ai, ALU.mult, r=R, w=W)
        self.tt(t1, t1, t2, ALU.add, r=R, w=W)
        self.tt(fr, t1, den, ALU.mult, r=R, w=W)
        self.tt(t1, li_, ar, ALU.mult, r=R, w=W)
        self.tt(t2, lm1, ai, ALU.mult, r=R, w=W)
        self.tt(t1, t1, t2, ALU.subtract, r=R, w=W)
        self.tt(fi, t1, den, ALU.mult, r=R, w=W)
        RP, WP = [K, "P"], ["P"]
        self.cp(self.Pr[:, 0, :], lr, r=RP, w=WP)
        self.cp(self.Pi[:, 0, :], li_, r=RP, w=WP)
        for d in range(9):
            pr, pi = self.Pr[:, d, :], self.Pi[:, d, :]
            self.tt(t1, pr, pr, ALU.mult, r=RP, w=[K])
            self.tt(t2, pi, pi, ALU.mult, r=RP, w=[K])
            self.tt(self.Pr[:, d + 1, :], t1, t2, ALU.subtract, r=RP, w=WP)
            self.stt(self.Pi[:, d + 1, :], pr, 2.0, pi, ALU.mult, ALU.mult, r=RP, w=WP)
        self.ts(self.Pn[:, :, :], self.Pi[:, :, :], -1.0, ALU.mult, r=RP, w=WP)

    def ssm_ctables(self, l):
        K = "spt"
        CK = [("ys", 12), ("ys", 13), ("ys", 14), ("ys", 15)]
        for gp in range(16):
            cr_t, ci_t, u1, u2 = [self.ctmp[:, i, :] for i in range(4)]
            self.dma("sp", cr_t, self.ssm_cT_d[l, 0, gp], "c_cr", w=["ctmp0"])
            self.dma("sp", ci_t, self.ssm_cT_d[l, 1, gp], "c_ci", w=["ctmp1"])
            frg, fig = self.spt[:, 11, gp:gp + 1], self.spt[:, 12, gp:gp + 1]
            self.ts(u1, ci_t, fig, ALU.mult, r=["ctmp1", K], w=["ctmp2"])
            self.stt(self.Cre[:, gp, :], cr_t, frg, u1, ALU.mult, ALU.subtract, r=["ctmp0", "ctmp2", K], w=CK[0:2])
            self.ts(u1, ci_t, frg, ALU.mult, r=["ctmp1", K] + CK[0:2], w=["ctmp2"])
            self.stt(u2, cr_t, fig, u1, ALU.mult, ALU.add, r=["ctmp0", "ctmp2", K], w=["ctmp3"])
            self.ts(self.Cim[:, gp, :], u2, -1.0, ALU.mult, r=["ctmp3"], w=CK[2:4])

    def rstd_from_bank(self, bank, dst, scale):
        self.act(dst, self.ps[:, bank, :], AF.Sqrt, r=[("ps", bank)], w=["rs"], bias=EPS, scale=scale)
        self.op("dve", lambda e: e.reciprocal(out=dst, in_=dst), r=["rs"], w=["rs"])

    def prenorm(self, l, gkind, halves):
        def f():
            for hf in halves:
                sbk = 4 + hf
                for c in range(16):
                    sq = self.sq[:, c % 2, :]
                    self.act(sq, self.xT[:, c, hs(hf)], AF.Square, r=[("x", c, hf)], w=[("sq", c % 2)])
                    self.mm(self.ps[:, sbk, :], self.ones[:, :], sq, start=(c == 0), stop=(c == 15),
                            r=["ones", ("sq", c % 2)], w=[("ps", sbk)])
                rs = self.rs[:, hf, :]
                self.rstd_from_bank(sbk, rs, 1.0 / D)
                for c in range(16):
                    self.stt(self.hT[:, c, hs(hf)], self.xT[:, c, hs(hf)], self.gv[:, l, gkind, c:c + 1], rs,
                             ALU.mult, ALU.mult, r=[("x", c, hf), "gv", "rs"], w=[("h", c, hf)])
        self.step(f)

    def inproj_pairs(self, wv, slot, jj, hf):
        return [(wv[:, kc, jj * 128:(jj + 1) * 128], self.hT[:, kc, hs(hf)], [("h", kc, hf)]) for kc in range(16)]

    def phase1_ssm(self, l, tile):
        SCR = self.SCR
        RC = self.RC
        uT = SCR[:, 0:4096].rearrange("p (c t) -> p c t", c=4)
        zT = RC[:, 8192:12288].rearrange("p (c t) -> p c t", c=4)
        stf = [SCR[:, 4096 + 4096 * i:8192 + 4096 * i].bitcast(F32).rearrange("p (c t) -> p c t", c=2) for i in range(2)]
        stb = [SCR[:, 12288 + 2048 * i:14336 + 2048 * i].rearrange("p (c t) -> p c t", c=2) for i in range(2)]
        CK = [("ys", 12), ("ys", 13), ("ys", 14), ("ys", 15)]
        for t in range(2):
            def f(slot, wv, t=t):
                for jj in range(2):
                    cb = 2 * t + jj
                    for hf in (0, 1):
                        bk = self.nb()
                        self.gemm(bk, self.inproj_pairs(wv, slot, jj, hf), self.wkeys(slot))
                        self.act(uT[:, cb, hs(hf)], self.ps[:, bk, :], AF.Copy, r=[("ps", bk)], w=[("scr", "uT", cb)])
            self.wstep(self.w_in[l, :, 256 * t:256 * t + 256], 16, 256, f)

        def g():
            self.ssm_ctables(l)
            for pr_ in range(8):
                gps = (2 * pr_, 2 * pr_ + 1)
                cb = pr_ // 2
                chains = []
                for bi, gp in enumerate(gps):
                    sk = ("stf", bi)
                    self.dma("pool", self.Bt[:, bi, 0, :], self.ssm_bT_d[l, 0, gp], ("bt", bi, 0), w=[("bt", bi, 0)])
                    self.dma("pool", self.Bt[:, bi, 1, :], self.ssm_bT_d[l, 1, gp], ("bt", bi, 1), w=[("bt", bi, 1)])
                    for ri in range(2):
                        for hf in (0, 1):
                            bk = 4 + ri * 2 + hf
                            self.mm(self.ps[:, bk, :], self.Bt[:, bi, ri, :], uT[:, cb, hs(hf)], True, True,
                                    r=[("bt", bi, ri), ("scr", "uT", cb)], w=[("ps", bk)])
                            self.act(stf[bi][:, ri, hs(hf)], self.ps[:, bk, :], AF.Copy, r=[("ps", bk)], w=[sk])
                    sr, si = stf[bi][:, 0, :], stf[bi][:, 1, :]
                    ops_ = []

                    def cmadd(a_r, a_i, b_r, b_i, d, extra=(), gp=gp, sk=sk, ops_=ops_, a_ri=None, b_ri=None):
                        rk = [sk, "P"] + list(extra)
                        p_r, p_i, p_n = self.Pr[:, d, gp:gp + 1], self.Pi[:, d, gp:gp + 1], self.Pn[:, d, gp:gp + 1]
                        if False and a_ri is not None:
                            ops_.append((a_ri, b_ri, p_r, rk, sk))
                        else:
                            ops_.append((a_r, b_r, p_r, rk, sk))
                            ops_.append((a_i, b_i, p_r, rk, sk))
                        ops_.append((a_r, b_i, p_n, rk, sk))
                        ops_.append((a_i, b_r, p_i, rk, sk))

                    if tile > 0:
                        cmadd(sr[:, 0:1], si[:, 0:1], self.carry[:, gp, 0:1], self.carry[:, gp, 1:2], 0, extra=["carry"])
                    for d in range(10):
                        s_ = 1 << d
                        A = slice(2 * s_ - 1, TT, 2 * s_)
                        Bv = slice(s_ - 1, TT, 2 * s_)
                        cmadd(sr[:, A], si[:, A], sr[:, Bv], si[:, Bv], d, a_ri=stf[bi][:, :, A], b_ri=stf[bi][:, :, Bv])
                    for d in range(8, -1, -1):
                        s_ = 1 << d
                        Bv = slice(2 * s_ - 1, TT - s_, 2 * s_)
                        A = slice(3 * s_ - 1, TT, 2 * s_)
                        cmadd(sr[:, A], si[:, A], sr[:, Bv], si[:, Bv], d, a_ri=stf[bi][:, :, A], b_ri=stf[bi][:, :, Bv])
                    chains.append(ops_)
                for k in range(max(len(c) for c in chains)):
                    for c in chains:
                        if k < len(c):
                            a, b, p, rk, sk = c[k]
                            self.stt(a, b, p, a, ALU.mult, ALU.add, r=rk, w=[sk])
                for bi, gp in enumerate(gps):
                    sk = ("stf", bi)
                    bk_ = ("stb", bi)
                    sr, si = stf[bi][:, 0, :], stf[bi][:, 1, :]
                    self.cp(self.carry[:, gp, 0:1], sr[:, TT - 1:TT], r=[sk], w=["carry"])
                    self.cp(self.carry[:, gp, 1:2], si[:, TT - 1:TT], r=[sk], w=["carry"])
                    self.act(stb[bi][:, 0, :], sr, AF.Copy, r=[sk], w=[bk_])
                    self.act(stb[bi][:, 1, :], si, AF.Copy, r=[sk], w=[bk_])
                    for hf in (0, 1):
                        yb = hf
                        self.mm(self.ps[:, yb, :], self.Cre[:, gp, :], stb[bi][:, 0, hs(hf)], start=(gp % 4 == 0), stop=False,
                                r=CK + [bk_], w=[("ps", yb)])
                        self.mm(self.ps[:, yb, :], self.Cim[:, gp, :], stb[bi][:, 1, hs(hf)], start=False, stop=(gp % 4 == 3),
                                r=CK + [bk_], w=[("ps", yb)])
                if pr_ % 2 == 1:
                    for hf in (0, 1):
                        tf = self.tmpf[:, hf, :]
                        self.stt(tf, uT[:, cb, hs(hf)], self.pvec[:, l, 0, cb:cb + 1], self.ps[:, hf, :], ALU.mult, ALU.add,
                                 r=[("scr", "uT", cb), "pvec", ("ps", hf)], w=[("tmpf", hf)])
                        self.act(zT[:, cb, hs(hf)], tf, AF.Gelu, r=[("tmpf", hf)], w=[("ys", 8 + cb)])
            self.bank_rr = 2
        self.step(g)

        def glu(slot, wv):
            for j in range(4):
                for hf in (0, 1):
                    b1, b2 = self.nb(), self.nb()
                    zk = lambda kc: [("ys", 8 + kc)]
                    self.gemm(b1, [(wv[:, kc, j * 128:(j + 1) * 128], zT[:, kc, hs(hf)], zk(kc)) for kc in range(4)], self.wkeys(slot))
                    self.gemm(b2, [(wv[:, kc, 512 + j * 128:512 + (j + 1) * 128], zT[:, kc, hs(hf)], zk(kc)) for kc in range(4)], self.wkeys(slot))
                    tf = self.tmpf[:, hf, :]
                    self.act(tf, self.ps[:, b2, :], AF.Sigmoid, r=[("ps", b2)], w=[("tmpf", hf)])
                    self.tt(self.ysT[:, j, hs(hf)], self.ps[:, b1, :], tf, ALU.mult, r=[("ps", b1), ("tmpf", hf)], w=[("ys", j)])
        self.wstep(self.w_glu[l], 4, 1024, glu)

    def phase1_pool(self, l, tile):
        SCR = self.SCR
        PW = 15 + TT
        Ub = SCR[:, 0:2 * PW].bitcast(F32)
        Ab = SCR[:, 2 * PW + 2:4 * PW + 2].bitcast(F32)
        Bb = SCR[:, 4 * PW + 4:6 * PW + 4].bitcast(F32)
        pooled = SCR[:, 6 * PW + 6:6 * PW + 6 + TT]
        for t in range(2):
            def f(slot, wv, t=t):
                for jj in range(2):
                    gi = 2 * t + jj
                    w = 2 << gi
                    for hf in (0, 1):
                        bk = self.nb()
                        self.gemm(bk, self.inproj_pairs(wv, slot, jj, hf), self.wkeys(slot))
                        self.act(Ub[:, 15 + hf * H:15 + (hf + 1) * H], self.ps[:, bk, :], AF.Copy, r=[("ps", bk)], w=["pU"])
                    if tile == 0:
                        self.op("dve", lambda e: e.memset(Ub[:, 0:15], 0.0), w=["pU"])
                    else:
                        self.cp(Ub[:, 0:15], self.poolc[:, gi, :], r=["poolc"], w=["pU"])
                    src, srck = Ub, "pU"
                    for k in range(gi + 1):
                        dst, dk = (Ab, "pA") if k % 2 == 0 else (Bb, "pB")
                        lo = (2 << k) - 1
                        sh = 1 << k
                        self.tt(dst[:, lo:PW], src[:, lo:PW], src[:, lo - sh:PW - sh], ALU.add, r=[srck], w=[dk])
                        src, srck = dst, dk
                    self.stt(pooled[:, :], src[:, 15:PW], 1.0 / w, Ub[:, 15:PW], ALU.mult, ALU.subtract, r=[srck, "pU"], w=["pooled"])
                    if tile == 0:
                        tf = self.tmpf[:, 0, 0:15]
                        self.tt(tf, src[:, 15:30], self.invc[:, gi, :], ALU.mult, r=[srck, "invc"], w=[("tmpf", 0)])
                        self.tt(pooled[:, 0:15], tf, Ub[:, 15:30], ALU.subtract, r=[("tmpf", 0), "pU"], w=["pooled"])
                    self.cp(self.poolc[:, gi, :], Ub[:, PW - 15:PW], r=["pU"], w=["poolc"])
                    for hf in (0, 1):
                        bk = self.nb()
                        self.mm(self.ps[:, bk, :], self.poolw[:, gi, :], pooled[:, hs(hf)], True, True,
                                r=["poolw", "pooled"], w=[("ps", bk)])
                        self.act(self.ysT[:, 4 + gi, hs(hf)], self.ps[:, bk, :], AF.Identity, r=[("ps", bk), "pvec"],
                                 w=[("ys", 4 + gi)], scale=self.pvec[:, l, 1, gi:gi + 1])
            self.wstep(self.w_in[l, :, OFF_POOL + 256 * t:OFF_POOL + 256 * t + 256], 16, 256, f)

    def phase1_conv(self, l, tile):
        SCR = self.SCR
        VW = 30 + TT
        vp = [SCR[:, 0:2 * VW].bitcast(F32), SCR[:, 2 * VW + 4:4 * VW + 4].bitcast(F32)]
        base = 4 * VW + 8
        acc = SCR[:, base:base + 8192].bitcast(F32).rearrange("p (c t) -> p c t", c=4)
        for j in range(4):
            def fc(slot, wv, j=j):
                vb = vp[j % 2]
                vk = ("vp", j % 2)
                wk_ = self.wkeys(slot)
                for hf in (0, 1):
                    b1, b2 = self.nb(), self.nb()
                    self.gemm(b1, [(wv[:, kc, :], self.hT[:, kc, hs(hf)], [("h", kc, hf)]) for kc in range(16)], wk_)
                    self.gemm(b2, [(wv[:, 16 + kc, :], self.hT[:, kc, hs(hf)], [("h", kc, hf)]) for kc in range(16)], wk_)
                    tf = self.tmpf[:, hf, :]
                    self.act(tf, self.ps[:, b2, :], AF.Sigmoid, r=[("ps", b2)], w=[("tmpf", hf)])
                    self.tt(vb[:, 30 + hf * H:30 + (hf + 1) * H], self.ps[:, b1, :], tf, ALU.mult,
                            r=[("ps", b1), ("tmpf", hf)], w=[vk])
                if tile == 0:
                    self.op("dve", lambda e, vb=vb: e.memset(vb[:, 0:30], 0.0), w=[vk])
                else:
                    self.cp(vb[:, 0:30], self.convc[:, j, :], r=["convc"], w=[vk])
                self.cp(self.convc[:, j, :], vb[:, VW - 30:VW], r=[vk], w=["convc"])
                a = acc[:, j, :]
                self.ts(a, vb[:, 0:TT], self.convw[:, l, j, 0:1], ALU.mult, r=[vk, "convw", "pvec"], w=[("acc", j)],
                        s2=self.pvec[:, l, 2, j:j + 1], op1=ALU.add)
                for k in range(1, 31):
                    self.stt(a, vb[:, k:k + TT], self.convw[:, l, j, k:k + 1], a, ALU.mult, ALU.add,
                             r=[vk, "convw", ("acc", j)], w=[("acc", j)])
            c0 = OFF_CONV + 128 * j
            self.wstep([(self.w_in[l, :, c0:c0 + 128], 16), (self.w_in[l, :, c0 + 512:c0 + 640], 16)], 32, 128, fc)

        def ln():
            for hf in (0, 1):
                b1, b2 = 4, 5
                for j in range(4):
                    q1 = self.sq[:, 0, :]
                    q2 = self.sq[:, 1, :]
                    self.act(q1, acc[:, j, hs(hf)], AF.Copy, r=[("acc", j)], w=[("sq", 0)])
                    self.act(q2, acc[:, j, hs(hf)], AF.Square, r=[("acc", j)], w=[("sq", 1)])
                    self.mm(self.ps[:, b1, :], self.ones[:, :], q1, start=(j == 0), stop=(j == 3), r=["ones", ("sq", 0)], w=[("ps", b1)])
                    self.mm(self.ps[:, b2, :], self.ones[:, :], q2, start=(j == 0), stop=(j == 3), r=["ones", ("sq", 1)], w=[("ps", b2)])
                mean = self.rs[:, 0, :]
                rstd = self.rs[:, 1, :]
                tf = self.tmpf[:, 0, :]
                self.act(mean, self.ps[:, b1, :], AF.Copy, r=[("ps", b1)], w=["rs"], scale=1.0 / 512.0)
                self.tt(tf, mean, mean, ALU.mult, r=["rs"], w=[("tmpf", 0)])
                self.stt(tf, self.ps[:, b2, :], 1.0 / 512.0, tf, ALU.mult, ALU.subtract, r=[("ps", b2), ("tmpf", 0)], w=[("tmpf", 0)])
                self.act(rstd, tf, AF.Sqrt, r=[("tmpf", 0)], w=["rs"], bias=EPS, scale=1.0)
                self.op("dve", lambda e, rstd=rstd: e.reciprocal(out=rstd, in_=rstd), r=["rs"], w=["rs"])
                for j in range(4):
                    t2 = self.tmpf[:, 1, :]
                    self.tt(t2, acc[:, j, hs(hf)], mean, ALU.subtract, r=[("acc", j), "rs"], w=[("tmpf", 1)])
                    self.tt(t2, t2, rstd, ALU.mult, r=[("tmpf", 1), "rs"], w=[("tmpf", 1)])
                    self.act(self.ysT[:, 8 + j, hs(hf)], t2, AF.Silu, r=[("tmpf", 1), "pvec"], w=[("ys", 8 + j)],
                             scale=self.pvec[:, l, 3, j:j + 1], bias=self.pvec[:, l, 4, j:j + 1])
        self.step(ln)

    def phase1_gmlp(self, l, tile):
        SCR = self.SCR
        ug = SCR[:, 0:4096].rearrange("p (c t) -> p c t", c=4)
        vg = SCR[:, 4096:12288].bitcast(F32).rearrange("p (c t) -> p c t", c=8)
        vn = SCR[:, 12288:16384].rearrange("p (c t) -> p c t", c=8)
        st = self.st4
        for t in range(2):
            def fu(slot, wv, t=t):
                for jj in range(2):
                    j = 2 * t + jj
                    for hf in (0, 1):
                        bk = self.nb()
                        self.gemm(bk, self.inproj_pairs(wv, slot, jj, hf), self.wkeys(slot))
                        self.act(ug[:, j, hs(hf)], self.ps[:, bk, :], AF.Gelu, r=[("ps", bk)], w=[("scr", "ug", j)])
            self.wstep(self.w_in[l, :, OFF_GMLP + 256 * t:OFF_GMLP + 256 * t + 256], 16, 256, fu)
        for ch in range(2):
            def fv(slot, wv, ch=ch):
                for tt_ in range(8):
                    bk = self.nb()
                    pairs = [(self.hT[:, kc, tt_ * 128:(tt_ + 1) * 128], wv[:, kc, :], [("h", kc, tt_ // 4)]) for kc in range(16)]
                    self.gemm(bk, pairs, self.wkeys(slot), n=256)
                    self.act(vg[:, tt_, ch * 256:(ch + 1) * 256], self.ps[:, bk, 0:256], AF.Gelu, r=[("ps", bk)],
                             w=[("scr", "vg", tt_)], accum_out=st[:, tt_, ch:ch + 1])
            self.wstep(self.w_in[l, :, OFF_GMLP + 512 + 256 * ch:OFF_GMLP + 512 + 256 * ch + 256], 16, 256, fv)

        def g():
            for tt_ in range(8):
                R = [("scr", "vg", tt_), "st4"]
                W = ["st4"]
                s1, s2, mean, var, rstd, nb_ = [st[:, tt_, i:i + 1] for i in range(2, 8)]
                tf = self.tmpf[:, tt_ % 2, :]
                tk = ("tmpf", tt_ % 2)
                self.act(tf, vg[:, tt_, :], AF.Square, r=R, w=[tk, "st4"], accum_out=s2)
                self.tt(s1, st[:, tt_, 0:1], st[:, tt_, 1:2], ALU.add, r=R, w=W)
                self.ts(mean, s1, 1.0 / 512.0, ALU.mult, r=R, w=W)
                self.tt(var, mean, mean, ALU.mult, r=R, w=W)
                self.stt(var, s2, 1.0 / 512.0, var, ALU.mult, ALU.subtract, r=R, w=W)
                self.act(rstd, var, AF.Sqrt, r=R, w=W, bias=EPS, scale=1.0)
                self.op("dve", lambda e, rstd=rstd: e.reciprocal(out=rstd, in_=rstd), r=R, w=W)
                self.stt(nb_, mean, -1.0, rstd, ALU.mult, ALU.mult, r=R, w=W)
                self.act(tf, vg[:, tt_, :], AF.Identity, r=R + [tk], w=[tk], scale=rstd, bias=nb_)
                self.tt(tf, tf, self.gln[:, 0, :], ALU.mult, r=[tk, "gln"], w=[tk])
                self.tt(vn[:, tt_, :], tf, self.gln[:, 1, :], ALU.add, r=[tk, "gln"], w=[("scr", "vn", tt_)])
            for hd in range(4):
                for hf in (0, 1):
                    bk = self.nb()
                    for q in range(4):
                        tt_ = hf * 4 + q
                        self.mm(self.ps[:, bk, q * 128:(q + 1) * 128], vn[:, tt_, hd * 128:(hd + 1) * 128], self.wsT[:, hd, :],
                                True, True, r=[("scr", "vn", tt_), "wsT"], w=[("ps", bk)])
                    tf = self.tmpf[:, hf, :]
                    tk = ("tmpf", hf)
                    self.tt(tf.rearrange("p (q t) -> p q t", q=4), self.ps[:, bk, :].rearrange("p (q t) -> p q t", q=4),
                            self.gln[:, 2, hd * 128:(hd + 1) * 128].unsqueeze(1).to_broadcast([128, 4, 128]), ALU.add,
                            r=[("ps", bk), "gln"], w=[tk])
                    self.tt(self.ysT[:, 12 + hd, hs(hf)], tf, ug[:, hd, hs(hf)], ALU.mult, r=[tk, ("scr", "ug", hd)], w=[("ys", 12 + hd)])
        self.step(g)

    def phase2_merge(self, l):
        for j in range(16):
            for k in range(4):
                def fg(slot, wv, j=j, k=k):
                    wk_ = self.wkeys(slot)
                    for hf in (0, 1):
                        bg, by = self.nb(), self.nb()
                        ab = 4 + hf
                        self.gemm(bg, [(wv[:, kc, :], self.hT[:, kc, hs(hf)], [("h", kc, hf)]) for kc in range(16)], wk_)
                        self.gemm(by, [(wv[:, 16 + kc, :], self.ysT[:, 4 * k + kc, hs(hf)], [("ys", 4 * k + kc)]) for kc in range(4)], wk_)
                        sg = self.rs[:, hf, :]
                        self.act(sg, self.ps[:, bg, :], AF.Sigmoid, r=[("ps", bg)], w=[("rsg", hf)])
                        if k == 0:
                            self.tt(self.ps[:, ab, :], self.ps[:, by, :], sg, ALU.mult, r=[("ps", by), ("rsg", hf)], w=[("ps", ab)])
                        else:
                            self.tt(sg, self.ps[:, by, :], sg, ALU.mult, r=[("ps", by), ("rsg", hf)], w=[("rsg", hf)])
                            if k < 3:
                                self.tt(self.ps[:, ab, :], self.ps[:, ab, :], sg, ALU.add, r=[("rsg", hf), ("ps", ab)], w=[("ps", ab)])
                            else:
                                self.tt(self.mgT[:, j, hs(hf)], self.ps[:, ab, :], sg, ALU.add, r=[("rsg", hf), ("ps", ab)], w=[("mg", j)])
                c0 = OFF_GATE + k * D + 128 * j
                self.wstep([(self.w_in[l, :, c0:c0 + 128], 16), (self.w_branch[l, 512 * k:512 * k + 512, 128 * j:128 * j + 128], 4)],
                           20, 128, fg)

    def norm_resid(self, l, gkind, hf, src_chunks, src_keys):
        rs = self.rs[:, hf, :]
        self.rstd_from_bank(6, rs, 1.0 / D)
        for c in range(16):
            tf = self.tmpf[:, c % 2, :]
            tk = ("tmpf", c % 2)
            self.stt(tf, src_chunks[c], self.gv[:, l, gkind, c:c + 1], rs, ALU.mult, ALU.mult, r=[src_keys[c], "gv", "rs"], w=[tk])
            self.tt(self.xT[:, c, hs(hf)], self.xT[:, c, hs(hf)], tf, ALU.add, r=[tk, ("x", c, hf)], w=[("x", c, hf)])

    def phase3_wo(self, l, hf):
        for j in range(16):
            def f(slot, wv, j=j):
                bk = self.nb()
                self.gemm(bk, [(wv[:, kc, :], self.mgT[:, kc, hs(hf)], [("mg", kc)]) for kc in range(16)], self.wkeys(slot))
                self.act(self.mixF[:, j, :], self.ps[:, bk, :], AF.Copy, r=[("ps", bk)], w=[("ys", j)])
                sq = self.sq[:, j % 2, :]
                self.act(sq, self.ps[:, bk, :], AF.Square, r=[("ps", bk)], w=[("sq", j % 2)])
                self.mm(self.ps[:, 6, :], self.ones[:, :], sq, start=(j == 0), stop=(j == 15), r=["ones", ("sq", j % 2)], w=[("ps", 6)])
            self.wstep(self.w_o[l, :, 128 * j:128 * j + 128], 16, 128, f)
        self.step(lambda: self.norm_resid(l, 1, hf, [self.mixF[:, c, :] for c in range(16)], [("ys", c) for c in range(16)]))

    def phase4_ffn(self, l, hf):
        self.prenorm(l, 2, (hf,))
        oh = 1 - hf
        fS = [self.hT[:, c, hs(oh)] for c in range(16)]
        for j in range(64):
            def f1(slot, wv, j=j):
                bk = self.nb()
                self.gemm(bk, [(wv[:, kc, :], self.hT[:, kc, hs(hf)], [("h", kc, hf)]) for kc in range(16)], self.wkeys(slot))
                tf = self.tmpf[:, j % 2, :]
                tk = ("tmpf", j % 2)
                self.act(tf, self.ps[:, bk, :], AF.Relu, r=[("ps", bk)], w=[tk])
                key = ("ys", j // 2) if j < 32 else ("mg", (j - 32) // 2)
                self.tt(self.f1T[:, j, :], tf, tf, ALU.mult, r=[tk], w=[key])
            self.wstep(self.w_ff1[l, :, 128 * j:128 * j + 128], 16, 128, f1)
        for j in range(16):
            for q in range(4):
                def f2(slot, wv, j=j, q=q):
                    bk = j % 2
                    pairs = []
                    for kc in range(16):
                        fc = q * 16 + kc
                        key = ("ys", fc // 2) if fc < 32 else ("mg", (fc - 32) // 2)
                        pairs.append((wv[:, kc, :], self.f1T[:, fc, :], [key]))
                    self.gemm(bk, pairs, self.wkeys(slot), first=(q == 0), last=(q == 3))
                    if q == 3:
                        self.act(fS[j], self.ps[:, bk, :], AF.Copy, r=[("ps", bk)], w=[("h", j, oh)])
                        sq = self.sq[:, j % 2, :]
                        self.act(sq, self.ps[:, bk, :], AF.Square, r=[("ps", bk)], w=[("sq", j % 2)])
                        self.mm(self.ps[:, 6, :], self.ones[:, :], sq, start=(j == 0), stop=(j == 15),
                                r=["ones", ("sq", j % 2)], w=[("ps", 6)])
                    self.bank_rr = 2
                self.wstep(self.w_ff2[l, 2048 * q:2048 * q + 2048, 128 * j:128 * j + 128], 16, 128, f2)
        self.step(lambda: self.norm_resid(l, 3, hf, fS, [("h", c, oh) for c in range(16)]))


_NC_CACHE = {}


def _host_layout(inp, wt_tensors):
    f = lambda a: np.ascontiguousarray(np.asarray(a, dtype=np.float32))
    x = f(inp["x"])
    per_core = []
    for b in range(NB):
        xb = x[b].reshape(NT, TT, 16, 128).transpose(0, 3, 2, 1)
        per_core.append(f(xb))
    sh = {}
    sh["pool_w"] = f(inp["pool_w"]).reshape(L, 512, 128)
    raw = {"w_in": inp["w_in"], "w_glu": inp["ssm_w_glu"], "w_branch": np.asarray(inp["w_branch"]).reshape(L, 2048, D),
           "w_o": inp["w_o"], "w_ff1": inp["w_ff1"], "w_ff2": inp["w_ff2"]}
    for name, _ap, used, recs in wt_tensors:
        buf = np.zeros((128, WCAP), np.float32)
        for (parts, ncols), off, n in recs:
            o = off
            for (wname, l_, r0, nrows, c0, nc_), kk in parts:
                assert nc_ == ncols and nrows == kk * 128
                a = np.asarray(raw[wname][l_][r0:r0 + nrows, c0:c0 + ncols], dtype=np.float32)
                buf[:, o:o + kk * ncols] = a.reshape(kk, 128, ncols).transpose(1, 0, 2).reshape(128, kk * ncols)
                o += kk * ncols
        sh[name] = buf
    gv = np.stack([f(inp[k]) for k in ("g_pre_mix", "g_post_mix", "g_pre_mlp", "g_post_mlp")], axis=1)
    sh["gv"] = f(gv.reshape(L, 4, 16, 128).transpose(3, 0, 1, 2))
    a_re, a_im, ldt = f(inp["ssm_a_re"]), f(inp["ssm_a_im"]), f(inp["ssm_log_dt"])
    ldt_b = np.broadcast_to(ldt[:, :, None], a_re.shape)
    a3 = np.stack([a_re, a_im, ldt_b], axis=1)
    a3 = a3.reshape(L, 3, 16, 2, 64).transpose(3, 4, 0, 1, 2)
    sh["ssm_a"] = f(a3.reshape(128, L, 3, 16))
    bT = np.zeros((L, 2, 16, 128, 128), np.float32)
    cT = np.zeros((L, 2, 16, 128, 128), np.float32)
    br, bi = f(inp["ssm_b_re"]), f(inp["ssm_b_im"])
    cr, ci = f(inp["ssm_c_re"]), f(inp["ssm_c_im"])
    for gp in range(16):
        for g2 in range(2):
            g = 2 * gp + g2
            g8 = g % 8
            for ri, (bb, cc) in enumerate(((br, cr), (bi, ci))):
                bT[:, ri, gp, g8 * 16:(g8 + 1) * 16, g2 * 64:(g2 + 1) * 64] = bb[:, g].transpose(0, 2, 1)
                cT[:, ri, gp, g2 * 64:(g2 + 1) * 64, g8 * 16:(g8 + 1) * 16] = cc[:, g].transpose(0, 2, 1)
    sh["ssm_bT"] = bT
    sh["ssm_cT"] = cT
    pv = np.stack([f(inp[k]) for k in ("ssm_d", "pool_scale", "conv_b", "conv_ln_g", "conv_ln_b")], axis=1)
    sh["pvec"] = f(pv.reshape(L, 5, 4, 128).transpose(3, 0, 1, 2))
    cw = f(inp["conv_w"])
    sh["convw"] = f(cw.reshape(L, 31, 4, 128).transpose(3, 0, 2, 1))
    sh["gln"] = f(np.stack([f(inp["gmlp_ln_g"]), f(inp["gmlp_ln_b"]), f(inp["gmlp_bs"]).reshape(L, 512)], axis=1))
    ws = f(inp["gmlp_ws"])
    sh["wsT"] = f(ws.transpose(0, 3, 1, 2))
    s_idx = np.arange(128)[:, None]
    t_idx = np.arange(128)[None, :]
    sh["mask"] = (t_idx >= s_idx).astype(np.float32)
    inv = np.zeros((128, 4, 15), np.float32)
    for gi in range(4):
        w = 2 << gi
        inv[:, gi, :] = 1.0 / np.minimum(np.arange(15) + 1, w)
    sh["invcnt"] = inv
    sh["ident"] = np.eye(128, dtype=np.float32)
    return per_core, sh


def kernel(**inputs):
    key = "full"
    if key not in _NC_CACHE:
        b = Builder(layers=list(range(L)))
        _NC_CACHE[key] = (b.build(), b.wt_tensors)
    nc, wt_tensors = _NC_CACHE[key]
    per_core, sh = _host_layout(inputs, wt_tensors)
    real = {0: 0, 1: 1, 4: 2, 5: 3}
    zsh = {}
    for k, v in sh.items():
        zsh[k] = np.zeros_like(v) if k.startswith("wt") else v
    zx = np.zeros_like(per_core[0])
    in_maps = []
    for c in range(8):
        if c in real:
            m = dict(sh)
            m["xin"] = per_core[real[c]]
        else:
            m = dict(zsh)
            m["xin"] = zx
        in_maps.append(m)
    res = run_bass_kernel_spmd(nc, in_maps, core_ids=list(range(8)))
    out = np.empty((NB, SEQ, D), np.float32)
    for c, b in real.items():
        y = np.asarray(res.results[c]["yout"])
        out[b] = y.transpose(0, 3, 2, 1).reshape(SEQ, D)
    return out
```

```python
import contextlib
import math
import numpy as np
import concourse.bass as bass
import concourse.mybir as mybir
from concourse.bass_utils import run_bass_kernel_spmd

F32 = mybir.dt.float32
BF16 = mybir.dt.bfloat16
AF = mybir.ActivationFunctionType
ALU = mybir.AluOpType

D = 2048
SEQ = 2048
NB = 4
L = 2
TT = 1024
NT = SEQ // TT
H = 512
IN_COLS = 11264
OFF_POOL, OFF_CONV, OFF_GMLP, OFF_GATE = 512, 1024, 2048, 3072
DFF = 8192
EPS = 1e-6
NUNIT = 4
UNIT = 2048
WSLOT = 2 * UNIT
WCAP = 294912


class Sched:
    def __init__(self, samesync=True):
        self.ops = []
        self.last_w = {}
        self.readers = {}
        self.last_dma = {}
        self.samesync = samesync

    def add(self, eng, fn, reads=(), writes=(), dma_key=None, deps=()):
        idx = len(self.ops)
        d = set(deps)
        for k in reads:
            w = self.last_w.get(k)
            if w is not None:
                d.add(w)
        for k in writes:
            w = self.last_w.get(k)
            if w is not None:
                d.add(w)
            for r in self.readers.get(k, {}).values():
                d.add(r)
        if dma_key is not None:
            p = self.last_dma.get(dma_key)
            if p is not None:
                d.add(p)
            self.last_dma[dma_key] = idx
        d.discard(idx)
        stream = ("dma", idx) if dma_key is not None else eng
        for k in reads:
            self.readers.setdefault(k, {})[stream] = idx
        for k in writes:
            self.last_w[k] = idx
            self.readers[k] = {}
        self.ops.append(dict(eng=eng, fn=fn, deps=sorted(d), dma_key=dma_key, observed=False, waits=[]))
        return idx

    def finalize(self):
        ops = self.ops
        known = {e: {} for e in ("pe", "act", "dve", "pool", "sp")}
        snaps = [None] * len(ops)
        for i, op in enumerate(ops):
            e = op["eng"]
            kn = known[e]
            best = {}
            for j in op["deps"]:
                pj = ops[j]
                if pj["dma_key"] is not None:
                    key = ("dma", pj["dma_key"])
                else:
                    key = pj["eng"]
                    if key == e and op["dma_key"] is None and (not self.samesync or e == "pe"):
                        continue
                if kn.get(key, -1) >= j:
                    continue
                if best.get(key, -1) < j:
                    best[key] = j
            for key, j in sorted(best.items(), key=lambda kv: kv[1]):
                if kn.get(key, -1) >= j:
                    continue
                ops[j]["observed"] = True
                op["waits"].append(j)
                kn[key] = j
                for k2, v2 in snaps[j].items():
                    if kn.get(k2, -1) < v2:
                        kn[k2] = v2
            snaps[i] = dict(kn)
        cnt = {}
        for op in ops:
            if op["dma_key"] is not None:
                key = ("dma", op["dma_key"])
                cnt[key] = cnt.get(key, 0) + 16
                op["token"] = (key, cnt[key])
            elif op["observed"]:
                key = op["eng"]
                cnt[key] = cnt.get(key, 0) + 1
                op["token"] = (key, cnt[key])
            else:
                op["token"] = None
        self.sem_keys = sorted(cnt.keys(), key=str)
        return self

    def stats(self):
        from collections import Counter
        c = Counter(op["eng"] for op in self.ops)
        return dict(ops=dict(c), waits=sum(len(op["waits"]) for op in self.ops),
                    observed=sum(1 for op in self.ops if op["observed"]), sems=len(self.sem_keys))

    def run(self, engine_obj, ename, sems):
        ops = self.ops
        for op in ops:
            if op["eng"] != ename:
                continue
            for j in op["waits"]:
                key, val = ops[j]["token"]
                engine_obj.wait_ge(sems[key], val)
            ins = op["fn"](engine_obj)
            if op["token"] is not None and ins is not None:
                key, val = op["token"]
                ins.then_inc(sems[key], 16 if isinstance(key, tuple) else 1)


def hs(hf):
    return slice(hf * H, (hf + 1) * H)


class WSpec:
    def __init__(self, name, K, N):
        self.name, self.K, self.N = name, K, N

    def __getitem__(self, idx):
        if not isinstance(idx, tuple):
            idx = (idx, slice(None), slice(None))
        l, rs, cs = idx
        r0, r1, _ = rs.indices(self.K)
        c0, c1, _ = cs.indices(self.N)
        return (self.name, l, r0, r1 - r0, c0, c1 - c0)


class Builder:
    def __init__(self, layers, debug=None, ntiles=NT, phases="spcgm34"):
        self.ntiles = ntiles
        self.phases = phases
        self.layers = layers
        self.debug = debug
        self.S = Sched()
        self.steps = []
        self.bank_rr = 0

    def op(self, eng, fn, r=(), w=()):
        return self.S.add(eng, fn, reads=r, writes=w)

    def dma(self, eng, out, in_, key, r=(), w=()):
        return self.S.add(eng, lambda e: e.dma_start(out=out, in_=in_), reads=r, writes=w, dma_key=key)

    def mm(self, out, lhsT, rhs, start, stop, r, w):
        return self.op("pe", lambda e: e.matmul(out, lhsT, rhs, start=start, stop=stop), r=r, w=w)

    def act(self, out, in_, func, r, w, bias=None, scale=None, accum_out=None):
        kw = {}
        if bias is not None:
            kw["bias"] = bias
        if scale is not None:
            kw["scale"] = scale
        if accum_out is not None:
            kw["accum_out"] = accum_out
        return self.op("act", lambda e: e.activation(out=out, in_=in_, func=func, **kw), r=r, w=w)

    def tt(self, out, in0, in1, op_, r, w, eng="dve"):
        return self.op(eng, lambda e: e.tensor_tensor(out=out, in0=in0, in1=in1, op=op_), r=r, w=w)

    def ts(self, out, in0, s1, op0, r, w, s2=None, op1=None, eng="dve"):
        if op1 is None:
            return self.op(eng, lambda e: e.tensor_scalar(out=out, in0=in0, scalar1=s1, scalar2=None, op0=op0), r=r, w=w)
        return self.op(eng, lambda e: e.tensor_scalar(out=out, in0=in0, scalar1=s1, scalar2=s2, op0=op0, op1=op1), r=r, w=w)

    def stt(self, out, in0, scalar, in1, op0, op1, r, w):
        return self.op("dve", lambda e: e.scalar_tensor_tensor(out=out, in0=in0, scalar=scalar, in1=in1, op0=op0, op1=op1), r=r, w=w)

    def cp(self, out, in_, r, w, eng="dve"):
        return self.op(eng, lambda e: e.tensor_copy(out=out, in_=in_), r=r, w=w)

    def nb(self):
        b = self.bank_rr
        self.bank_rr = (self.bank_rr + 1) % 4
        return b

    def wstep(self, src, kc, ncols, fn):
        parts = tuple(src) if isinstance(src, list) else ((src, kc),)
        n = sum(k for _, k in parts) * ncols
        assert n <= WSLOT
        key = (parts, ncols)
        if key not in self.cat:
            if not self.wt_tensors or self.wt_tensors[-1][2] + n > WCAP:
                name = "wt%d" % len(self.wt_tensors)
                self.wt_tensors.append([name, self._dr(name, [128, WCAP]), 0, []])
            tns = self.wt_tensors[-1]
            self.cat[key] = (len(self.wt_tensors) - 1, tns[2], n)
            tns[3].append((key, tns[2], n))
            tns[2] += n
        self.steps.append((self.cat[key], parts, ncols, fn))

    def step(self, fn):
        self.steps.append((None, None, 0, fn))

    def wkeys(self, slot):
        u0, cnt = slot
        return [("w", u0 + i) for i in range(cnt)]

    def run_steps(self):
        steps = self.steps
        tiles = [i for i, s in enumerate(steps) if s[0] is not None]
        owner = [-1] * NUNIT
        loaded = {}
        st = dict(next=0, ptr=0)

        def try_issue(cur):
            while st["next"] < len(tiles):
                i = tiles[st["next"]]
                (ti, off, n), parts, ncols, _ = steps[i]
                cnt = 1 if n <= UNIT else 2
                p = st["ptr"]
                if cnt == 2 and p % 2 == 1:
                    p = (p + 1) % NUNIT
                if p + cnt > NUNIT:
                    p = 0
                if any(owner[p + k] >= cur for k in range(cnt)):
                    return
                kct = sum(kk for _, kk in parts)
                dst = self.wring[:, p * UNIT:p * UNIT + n]
                self.dma("pool", dst, self.wt_tensors[ti][1][:, off:off + n], key=("w", p), w=[("w", p + k) for k in range(cnt)])
                for k in range(cnt):
                    owner[p + k] = i
                loaded[i] = ((p, cnt), dst.rearrange("p (c n) -> p c n", c=kct))
                st["ptr"] = (p + cnt) % NUNIT
                st["next"] += 1

        for i, s in enumerate(steps):
            try_issue(i)
            if s[0] is not None:
                assert i in loaded
                owner_save = None
                s[3](*loaded[i])
            else:
                s[3]()
        self.steps = []

    def gemm(self, bank, pairs, rkeys, first=True, last=True, n=H):
        np_ = len(pairs)
        for i, (lt, rh, rk) in enumerate(pairs):
            self.mm(self.ps[:, bank, 0:n], lt, rh, start=(first and i == 0), stop=(last and i == np_ - 1),
                    r=list(rkeys) + list(rk), w=[("ps", bank)])

    def build(self):
        nc = bass.Bass("TRN2", target_bir_lowering=False)
        self.nc = nc
        es = contextlib.ExitStack()
        self.es = es
        dr = lambda name, shape, kind="ExternalInput": nc.dram_tensor(name, list(shape), F32, kind=kind).ap()
        self.xin = dr("xin", [NT, 128, 16, TT])
        self.yout = dr("yout", [NT, 128, 16, TT], "ExternalOutput")
        self.xs = self.yout
        self.w_in = WSpec("w_in", D, IN_COLS)
        self.w_glu = WSpec("w_glu", 512, 1024)
        self.pool_w = dr("pool_w", [L, 512, 128])
        self.w_branch = WSpec("w_branch", 2048, D)
        self.w_o = WSpec("w_o", D, D)
        self.w_ff1 = WSpec("w_ff1", D, DFF)
        self.w_ff2 = WSpec("w_ff2", DFF, D)
        self.cat = {}
        self.wt_tensors = []
        self._dr = dr
        self.gv_d = dr("gv", [128, L, 4, 16])
        self.ssm_a_d = dr("ssm_a", [128, L, 3, 16])
        self.ssm_bT_d = dr("ssm_bT", [L, 2, 16, 128, 128])
        self.ssm_cT_d = dr("ssm_cT", [L, 2, 16, 128, 128])
        self.pvec_d = dr("pvec", [128, L, 5, 4])
        self.convw_d = dr("convw", [128, L, 4, 31])
        self.gln_d = dr("gln", [L, 3, 512])
        self.wsT_d = dr("wsT", [L, 128, 4, 128])
        self.mask_d = dr("mask", [128, 128])
        self.invcnt_d = dr("invcnt", [128, 4, 15])
        self.ident_d = dr("ident", [128, 128])
        if self.debug:
            self.dbg = dr("dbg", [128, 16, TT], "ExternalOutput")

        sb = lambda name, shape, dt=F32: es.enter_context(nc.sbuf_tensor(name, list(shape), dt))
        with es:
            self.xT = sb("xT", [128, 16, TT])
            self.hT = sb("hT", [128, 16, TT], BF16)
            self.RC = sb("RC", [128, 32768], BF16)
            self.wring = sb("wring", [128, NUNIT * UNIT], BF16)
            self.ps = es.enter_context(nc.psum_tensor("ps", [128, 8, H], F32))
            RC = self.RC
            self.ysT = RC[:, 0:16384].rearrange("p (c t) -> p c t", c=16)
            self.mgT = RC[:, 16384:32768].rearrange("p (c t) -> p c t", c=16)
            self.mixF = RC[:, 0:16384].bitcast(F32).rearrange("p (c t) -> p c t", c=16)
            self.f1T = RC[:, :].rearrange("p (c t) -> p c t", c=64)
            SCR = RC[:, 16384:32768]
            self.SCR = SCR
            self.ones = sb("ones", [128, 128], BF16)
            self.gv = sb("gvs", [128, L, 4, 16])
            self.pvec = sb("pvecs", [128, L, 5, 4])
            self.convw = sb("convws", [128, L, 4, 31])
            self.ssma = sb("ssmas", [128, 3, 16])
            self.spt = sb("spt", [128, 16, 16])
            self.Pr = sb("Pr", [128, 10, 16])
            self.Pi = sb("Pi", [128, 10, 16])
            self.Pn = sb("Pn", [128, 10, 16])
            self.Cre = RC[:, 12288:14336].rearrange("p (c t) -> p c t", c=16)
            self.Cim = RC[:, 14336:16384].rearrange("p (c t) -> p c t", c=16)
            self.ctmp = sb("ctmp", [128, 4, 128])
            self.Bt = sb("Bt", [128, 2, 2, 128], BF16)
            self.carry = sb("carry", [128, 16, 2])
            self.poolc = sb("poolc", [128, 4, 15])
            self.convc = sb("convc", [128, 4, 30], BF16)
            self.identb = sb("identb", [128, 128], BF16)
            self.gln = sb("glns", [128, 3, 512])
            self.wsT = sb("wsTs", [128, 4, 128], BF16)
            self.maskt = sb("maskt", [128, 128])
            self.invc = sb("invc", [128, 4, 15])
            self.poolw = sb("poolw", [128, 4, 128], BF16)
            self.rs = sb("rs", [128, 2, H])
            self.tmpf = sb("tmpf", [128, 2, H])
            self.sq = sb("sq", [128, 2, H], BF16)
            self.st4 = sb("st4", [128, 8, 8])

            self.emit_all()

            self.S.finalize()
            print("sched:", self.S.stats(), "sbuf_remaining", nc.sbuf_bytes_remaining, flush=True)
            sems = {k: es.enter_context(nc.semaphore("sem%d" % i)) for i, k in enumerate(self.S.sem_keys)}
            block = es.enter_context(nc.Block())
            S = self.S

            @block.tensor
            def _(e):
                S.run(e, "pe", sems)

            @block.scalar
            def _(e):
                S.run(e, "act", sems)

            @block.vector
            def _(e):
                S.run(e, "dve", sems)

            @block.gpsimd
            def _(e):
                S.run(e, "pool", sems)

            @block.sync
            def _(e):
                S.run(e, "sp", sems)
        return nc

    def emit_all(self):
        self.op("dve", lambda e: e.memset(self.ones[:, :], 1.0), w=["ones"])
        self.dma("sp", self.gv[:], self.gv_d, "c_gv", w=["gv"])
        self.dma("sp", self.pvec[:], self.pvec_d, "c_pvec", w=["pvec"])
        self.dma("sp", self.convw[:], self.convw_d, "c_convw", w=["convw"])
        self.dma("sp", self.maskt[:], self.mask_d, "c_mask", w=["mask"])
        self.dma("sp", self.invc[:], self.invcnt_d, "c_invc", w=["invc"])
        self.dma("pool", self.identb[:], self.ident_d, "c_ident", w=["identb"])
        fin = []
        nl = len(self.layers)
        passes = [(li, l, tile) for li, l in enumerate(self.layers) for tile in range(self.ntiles)]

        def xload(pi, hf):
            li, l, tile = passes[pi]
            src = self.xin if li == 0 else self.xs
            for q in range(2):
                self.dma("sp", self.xT[:, 8 * q:8 * q + 8, hs(hf)], src[tile, :, 8 * q:8 * q + 8, hs(hf)], "xload%d_%d" % (hf, q),
                         r=[("xs", tile, hf)] if li > 0 else [], w=[("x", c, hf) for c in range(8 * q, 8 * q + 8)])

        def xstore(pi, hf):
            li, l, tile = passes[pi]
            dst = self.yout if li == nl - 1 else self.xs
            for q in range(2):
                d_ = self.dma("sp", dst[tile, :, 8 * q:8 * q + 8, hs(hf)], self.xT[:, 8 * q:8 * q + 8, hs(hf)], "xstore%d_%d" % (hf, q),
                              r=[("x", c, hf) for c in range(8 * q, 8 * q + 8)], w=[("xs", tile, hf)])
                if li == nl - 1:
                    fin.append(d_)

        def after_half(pi, hf):
            xstore(pi, hf)
            if pi + 1 < len(passes):
                xload(pi + 1, hf)

        xload(0, 0)
        xload(0, 1)
        for pi, (li, l, tile) in enumerate(passes):
            if tile == 0:
                self.layer_setup(l)
            self.prenorm(l, 0, (0, 1))
            P = self.phases
            if "s" in P:
                self.phase1_ssm(l, tile)
            if "p" in P:
                self.phase1_pool(l, tile)
            if "c" in P:
                self.phase1_conv(l, tile)
            if "g" in P:
                self.phase1_gmlp(l, tile)
            if "m" in P:
                self.phase2_merge(l)
            if "3" in P:
                for hf in (0, 1):
                    self.phase3_wo(l, hf)
            for hf in (0, 1):
                if "4" in P:
                    self.phase4_ffn(l, hf)
                self.step(lambda pi=pi, hf=hf: after_half(pi, hf))
            self.run_steps()
        self.S.add("sp", lambda e: None, deps=fin)

    def dump_debug(self):
        pass

    def layer_setup(self, l):
        t = lambda i: self.spt[:, i, :]
        K = "spt"
        self.dma("sp", self.ssma[:], self.ssm_a_d[:, l, :, :], "c_ssma", w=["ssma"])
        self.dma("sp", self.gln[:, 0, :], self.gln_d[l, 0:1, :].partition_broadcast(128), "c_gln0", w=["gln"])
        self.dma("sp", self.gln[:, 1, :], self.gln_d[l, 1:2, :].partition_broadcast(128), "c_gln1", w=["gln"])
        self.dma("sp", self.gln[:, 2, :], self.gln_d[l, 2:3, :].partition_broadcast(128), "c_gln2", w=["gln"])
        CT = ["ctmp0", "ctmp1", "ctmp2", "ctmp3"]
        self.dma("sp", self.ctmp[:], self.wsT_d[l], "c_wsf", w=CT)
        self.dma("pool", self.poolw[:], self.pool_w[l].rearrange("(g p) d -> p g d", p=128), "c_poolw", w=["poolw"])
        for hd in range(4):
            self.tt(self.wsT[:, hd, :], self.ctmp[:, hd, :], self.maskt[:, :], ALU.mult, r=CT + ["mask"], w=["wsT"])
        ar, ai, ldt = self.ssma[:, 0, :], self.ssma[:, 1, :], self.ssma[:, 2, :]
        dt, mag, ang, c, s, t1, t2, lr, li_, den, lm1, fr, fi, t3 = [t(i) for i in range(14)]
        R, W = ["ssma", K], [K]
        self.act(dt, ldt, AF.Exp, r=R, w=W)
        self.tt(t1, ar, dt, ALU.mult, r=R, w=W)
        self.act(mag, t1, AF.Exp, r=R, w=W)
        self.tt(ang, ai, dt, ALU.mult, r=R, w=W)
        self.act(s, ang, AF.Sin, r=R, w=W, scale=1.0 / 32.0)
        self.ts(t2, ang, 1.0 / 32.0, ALU.mult, r=R, w=W, s2=math.pi / 2.0, op1=ALU.add)
        self.act(c, t2, AF.Sin, r=R, w=W)
        for _ in range(5):
            self.tt(t1, c, c, ALU.mult, r=R, w=W)
            self.tt(t2, s, s, ALU.mult, r=R, w=W)
            self.stt(s, c, 2.0, s, ALU.mult, ALU.mult, r=R, w=W)
            self.tt(c, t1, t2, ALU.subtract, r=R, w=W)
        self.tt(lr, mag, c, ALU.mult, r=R, w=W)
        self.tt(li_, mag, s, ALU.mult, r=R, w=W)
        self.tt(t1, ar, ar, ALU.mult, r=R, w=W)
        self.tt(t2, ai, ai, ALU.mult, r=R, w=W)
        self.tt(den, t1, t2, ALU.add, r=R, w=W)
        self.op("dve", lambda e: e.reciprocal(out=den, in_=den), r=R, w=W)
        self.ts(lm1, lr, -1.0, ALU.add, r=R, w=W)
        self.tt(t1, lm1, ar, ALU.mult, r=R, w=W)
        self.tt(t2, li_, ai, ALU.mult, r=R, w=W)
        self.tt(t1, t1, t2, ALU.add, r=R, w=W)
        self.tt(fr, t1, den, ALU.mult, r=R, w=W)
        self.tt(t1, li_, ar, ALU.mult, r=R, w=W)
        self.tt(t2, lm1, ai, ALU.mult, r=R, w=W)
        self.tt(t1, t1, t2, ALU.subtract, r=R, w=W)
        self.tt(fi, t1, den, ALU.mult, r=R, w=W)
        RP, WP = [K, "P"], ["P"]
        self.cp(self.Pr[:, 0, :], lr, r=RP, w=WP)
        self.cp(self.Pi[:, 0, :], li_, r=RP, w=WP)
        for d in range(9):
            pr, pi = self.Pr[:, d, :], self.Pi[:, d, :]
            self.tt(t1, pr, pr, ALU.mult, r=RP, w=[K])
            self.tt(t2, pi, pi, ALU.mult, r=RP, w=[K])
            self.tt(self.Pr[:, d + 1, :], t1, t2, ALU.subtract, r=RP, w=WP)
            self.stt(self.Pi[:, d + 1, :], pr, 2.0, pi, ALU.mult, ALU.mult, r=RP, w=WP)
        self.ts(self.Pn[:, :, :], self.Pi[:, :, :], -1.0, ALU.mult, r=RP, w=WP)

    def ssm_ctables(self, l):
        K = "spt"
        CK = [("ys", 12), ("ys", 13), ("ys", 14), ("ys", 15)]
        for gp in range(16):
            cr_t, ci_t, u1, u2 = [self.ctmp[:, i, :] for i in range(4)]
            self.dma("sp", cr_t, self.ssm_cT_d[l, 0, gp], "c_cr", w=["ctmp0"])
            self.dma("sp", ci_t, self.ssm_cT_d[l, 1, gp], "c_ci", w=["ctmp1"])
            frg, fig = self.spt[:, 11, gp:gp + 1], self.spt[:, 12, gp:gp + 1]
            self.ts(u1, ci_t, fig, ALU.mult, r=["ctmp1", K], w=["ctmp2"])
            self.stt(self.Cre[:, gp, :], cr_t, frg, u1, ALU.mult, ALU.subtract, r=["ctmp0", "ctmp2", K], w=CK[0:2])
            self.ts(u1, ci_t, frg, ALU.mult, r=["ctmp1", K] + CK[0:2], w=["ctmp2"])
            self.stt(u2, cr_t, fig, u1, ALU.mult, ALU.add, r=["ctmp0", "ctmp2", K], w=["ctmp3"])
            self.ts(self.Cim[:, gp, :], u2, -1.0, ALU.mult, r=["ctmp3"], w=CK[2:4])

    def rstd_from_bank(self, bank, dst, scale):
        self.act(dst, self.ps[:, bank, :], AF.Sqrt, r=[("ps", bank)], w=["rs"], bias=EPS, scale=scale)
        self.op("dve", lambda e: e.reciprocal(out=dst, in_=dst), r=["rs"], w=["rs"])

    def prenorm(self, l, gkind, halves):
        def f():
            for hf in halves:
                sbk = 4 + hf
                for c in range(16):
                    sq = self.sq[:, c % 2, :]
                    self.act(sq, self.xT[:, c, hs(hf)], AF.Square, r=[("x", c, hf)], w=[("sq", c % 2)])
                    self.mm(self.ps[:, sbk, :], self.ones[:, :], sq, start=(c == 0), stop=(c == 15),
                            r=["ones", ("sq", c % 2)], w=[("ps", sbk)])
                rs = self.rs[:, hf, :]
                self.rstd_from_bank(sbk, rs, 1.0 / D)
                for c in range(16):
                    self.stt(self.hT[:, c, hs(hf)], self.xT[:, c, hs(hf)], self.gv[:, l, gkind, c:c + 1], rs,
                             ALU.mult, ALU.mult, r=[("x", c, hf), "gv", "rs"], w=[("h", c, hf)])
        self.step(f)

    def inproj_pairs(self, wv, slot, jj, hf):
        return [(wv[:, kc, jj * 128:(jj + 1) * 128], self.hT[:, kc, hs(hf)], [("h", kc, hf)]) for kc in range(16)]

    def phase1_ssm(self, l, tile):
        SCR = self.SCR
        RC = self.RC
        uT = SCR[:, 0:4096].rearrange("p (c t) -> p c t", c=4)
        zT = RC[:, 8192:12288].rearrange("p (c t) -> p c t", c=4)
        stf = [SCR[:, 4096 + 4096 * i:8192 + 4096 * i].bitcast(F32).rearrange("p (c t) -> p c t", c=2) for i in range(2)]
        stb = [SCR[:, 12288 + 2048 * i:14336 + 2048 * i].rearrange("p (c t) -> p c t", c=2) for i in range(2)]
        CK = [("ys", 12), ("ys", 13), ("ys", 14), ("ys", 15)]
        self.step(lambda: self.ssm_ctables(l))
        for t in range(2):
            def f(slot, wv, t=t):
                for jj in range(2):
                    cb = 2 * t + jj
                    for hf in (0, 1):
                        bk = self.nb()
                        self.gemm(bk, self.inproj_pairs(wv, slot, jj, hf), self.wkeys(slot))
                        self.act(uT[:, cb, hs(hf)], self.ps[:, bk, :], AF.Copy, r=[("ps", bk)], w=[("scr", "uT", cb)])
            self.wstep(self.w_in[l, :, 256 * t:256 * t + 256], 16, 256, f)

        def g():
            for pr_ in range(8):
                gps = (2 * pr_, 2 * pr_ + 1)
                cb = pr_ // 2
                chains = []
                for bi, gp in enumerate(gps):
                    sk = ("stf", bi)
                    self.dma("pool", self.Bt[:, bi, 0, :], self.ssm_bT_d[l, 0, gp], ("bt", bi, 0), w=[("bt", bi, 0)])
                    self.dma("pool", self.Bt[:, bi, 1, :], self.ssm_bT_d[l, 1, gp], ("bt", bi, 1), w=[("bt", bi, 1)])
                    for ri in range(2):
                        for hf in (0, 1):
                            bk = 4 + ri * 2 + hf
                            self.mm(self.ps[:, bk, :], self.Bt[:, bi, ri, :], uT[:, cb, hs(hf)], True, True,
                                    r=[("bt", bi, ri), ("scr", "uT", cb)], w=[("ps", bk)])
                            self.act(stf[bi][:, ri, hs(hf)], self.ps[:, bk, :], AF.Copy, r=[("ps", bk)], w=[sk])
                    sr, si = stf[bi][:, 0, :], stf[bi][:, 1, :]
                    ops_ = []

                    def cmadd(a_r, a_i, b_r, b_i, d, extra=(), gp=gp, sk=sk, ops_=ops_, a_ri=None, b_ri=None):
                        rk = [sk, "P"] + list(extra)
                        p_r, p_i, p_n = self.Pr[:, d, gp:gp + 1], self.Pi[:, d, gp:gp + 1], self.Pn[:, d, gp:gp + 1]
                        if False and a_ri is not None:
                            ops_.append((a_ri, b_ri, p_r, rk, sk))
                        else:
                            ops_.append((a_r, b_r, p_r, rk, sk))
                            ops_.append((a_i, b_i, p_r, rk, sk))
                        ops_.append((a_r, b_i, p_n, rk, sk))
                        ops_.append((a_i, b_r, p_i, rk, sk))

                    if tile > 0:
                        cmadd(sr[:, 0:1], si[:, 0:1], self.carry[:, gp, 0:1], self.carry[:, gp, 1:2], 0, extra=["carry"])
                    for d in range(10):
                        s_ = 1 << d
                        A = slice(2 * s_ - 1, TT, 2 * s_)
                        Bv = slice(s_ - 1, TT, 2 * s_)
                        cmadd(sr[:, A], si[:, A], sr[:, Bv], si[:, Bv], d, a_ri=stf[bi][:, :, A], b_ri=stf[bi][:, :, Bv])
                    for d in range(8, -1, -1):
                        s_ = 1 << d
                        Bv = slice(2 * s_ - 1, TT - s_, 2 * s_)
                        A = slice(3 * s_ - 1, TT, 2 * s_)
                        cmadd(sr[:, A], si[:, A], sr[:, Bv], si[:, Bv], d, a_ri=stf[bi][:, :, A], b_ri=stf[bi][:, :, Bv])
                    chains.append(ops_)
                for k in range(max(len(c) for c in chains)):
                    for c in chains:
                        if k < len(c):
                            a, b, p, rk, sk = c[k]
                            self.stt(a, b, p, a, ALU.mult, ALU.add, r=rk, w=[sk])
                for bi, gp in enumerate(gps):
                    sk = ("stf", bi)
                    bk_ = ("stb", bi)
                    sr, si = stf[bi][:, 0, :], stf[bi][:, 1, :]
                    self.cp(self.carry[:, gp, 0:1], sr[:, TT - 1:TT], r=[sk], w=["carry"])
                    self.cp(self.carry[:, gp, 1:2], si[:, TT - 1:TT], r=[sk], w=["carry"])
                    self.act(stb[bi][:, 0, :], sr, AF.Copy, r=[sk], w=[bk_])
                    self.act(stb[bi][:, 1, :], si, AF.Copy, r=[sk], w=[bk_])
                    for hf in (0, 1):
                        yb = hf
                        self.mm(self.ps[:, yb, :], self.Cre[:, gp, :], stb[bi][:, 0, hs(hf)], start=(gp % 4 == 0), stop=False,
                                r=CK + [bk_], w=[("ps", yb)])
                        self.mm(self.ps[:, yb, :], self.Cim[:, gp, :], stb[bi][:, 1, hs(hf)], start=False, stop=(gp % 4 == 3),
                                r=CK + [bk_], w=[("ps", yb)])
                if pr_ % 2 == 1:
                    for hf in (0, 1):
                        tf = self.tmpf[:, hf, :]
                        self.stt(tf, uT[:, cb, hs(hf)], self.pvec[:, l, 0, cb:cb + 1], self.ps[:, hf, :], ALU.mult, ALU.add,
                                 r=[("scr", "uT", cb), "pvec", ("ps", hf)], w=[("tmpf", hf)])
                        self.act(zT[:, cb, hs(hf)], tf, AF.Gelu, r=[("tmpf", hf)], w=[("ys", 8 + cb)])
            self.bank_rr = 2
        self.step(g)

        def glu(slot, wv):
            for j in range(4):
                for hf in (0, 1):
                    b1, b2 = self.nb(), self.nb()
                    zk = lambda kc: [("ys", 8 + kc)]
                    self.gemm(b1, [(wv[:, kc, j * 128:(j + 1) * 128], zT[:, kc, hs(hf)], zk(kc)) for kc in range(4)], self.wkeys(slot))
                    self.gemm(b2, [(wv[:, kc, 512 + j * 128:512 + (j + 1) * 128], zT[:, kc, hs(hf)], zk(kc)) for kc in range(4)], self.wkeys(slot))
                    tf = self.tmpf[:, hf, :]
                    self.act(tf, self.ps[:, b2, :], AF.Sigmoid, r=[("ps", b2)], w=[("tmpf", hf)])
                    self.tt(self.ysT[:, j, hs(hf)], self.ps[:, b1, :], tf, ALU.mult, r=[("ps", b1), ("tmpf", hf)], w=[("ys", j)])
        self.wstep(self.w_glu[l], 4, 1024, glu)

    def phase1_pool(self, l, tile):
        SCR = self.SCR
        PW = 15 + TT
        Ub = SCR[:, 0:2 * PW].bitcast(F32)
        Ab = SCR[:, 2 * PW + 2:4 * PW + 2].bitcast(F32)
        Bb = SCR[:, 4 * PW + 4:6 * PW + 4].bitcast(F32)
        pooled = SCR[:, 6 * PW + 6:6 * PW + 6 + TT]
        for t in range(2):
            def f(slot, wv, t=t):
                for jj in range(2):
                    gi = 2 * t + jj
                    w = 2 << gi
                    for hf in (0, 1):
                        bk = self.nb()
                        self.gemm(bk, self.inproj_pairs(wv, slot, jj, hf), self.wkeys(slot))
                        self.act(Ub[:, 15 + hf * H:15 + (hf + 1) * H], self.ps[:, bk, :], AF.Copy, r=[("ps", bk)], w=["pU"])
                    if tile == 0:
                        self.op("dve", lambda e: e.memset(Ub[:, 0:15], 0.0), w=["pU"])
                    else:
                        self.cp(Ub[:, 0:15], self.poolc[:, gi, :], r=["poolc"], w=["pU"])
                    src, srck = Ub, "pU"
                    for k in range(gi + 1):
                        dst, dk = (Ab, "pA") if k % 2 == 0 else (Bb, "pB")
                        lo = (2 << k) - 1
                        sh = 1 << k
                        self.tt(dst[:, lo:PW], src[:, lo:PW], src[:, lo - sh:PW - sh], ALU.add, r=[srck], w=[dk])
                        src, srck = dst, dk
                    self.stt(pooled[:, :], src[:, 15:PW], 1.0 / w, Ub[:, 15:PW], ALU.mult, ALU.subtract, r=[srck, "pU"], w=["pooled"])
                    if tile == 0:
                        tf = self.tmpf[:, 0, 0:15]
                        self.tt(tf, src[:, 15:30], self.invc[:, gi, :], ALU.mult, r=[srck, "invc"], w=[("tmpf", 0)])
                        self.tt(pooled[:, 0:15], tf, Ub[:, 15:30], ALU.subtract, r=[("tmpf", 0), "pU"], w=["pooled"])
                    self.cp(self.poolc[:, gi, :], Ub[:, PW - 15:PW], r=["pU"], w=["poolc"])
                    for hf in (0, 1):
                        bk = self.nb()
                        self.mm(self.ps[:, bk, :], self.poolw[:, gi, :], pooled[:, hs(hf)], True, True,
                                r=["poolw", "pooled"], w=[("ps", bk)])
                        self.act(self.ysT[:, 4 + gi, hs(hf)], self.ps[:, bk, :], AF.Identity, r=[("ps", bk), "pvec"],
                                 w=[("ys", 4 + gi)], scale=self.pvec[:, l, 1, gi:gi + 1])
            self.wstep(self.w_in[l, :, OFF_POOL + 256 * t:OFF_POOL + 256 * t + 256], 16, 256, f)

    def phase1_conv(self, l, tile):
        SCR = self.SCR
        VW = 30 + TT
        vp = [SCR[:, 0:VW], SCR[:, 1056:1056 + VW]]
        vq = SCR[:, 2112:2112 + VW]
        diag = SCR[:, 3200:3200 + 31 * 128].rearrange("p (k n) -> p k n", k=31)
        acc = SCR[:, 8192:16384].bitcast(F32).rearrange("p (c t) -> p c t", c=4)
        for j in range(4):
            def fc(slot, wv, j=j):
                vb = vp[j % 2]
                vk = ("vp", j % 2)
                wk_ = self.wkeys(slot)
                for hf in (0, 1):
                    b1, b2 = self.nb(), self.nb()
                    self.gemm(b1, [(wv[:, kc, :], self.hT[:, kc, hs(hf)], [("h", kc, hf)]) for kc in range(16)], wk_)
                    self.gemm(b2, [(wv[:, 16 + kc, :], self.hT[:, kc, hs(hf)], [("h", kc, hf)]) for kc in range(16)], wk_)
                    tf = self.tmpf[:, hf, :]
                    self.act(tf, self.ps[:, b2, :], AF.Sigmoid, r=[("ps", b2)], w=[("tmpf", hf)])
                    self.tt(vb[:, 30 + hf * H:30 + (hf + 1) * H], self.ps[:, b1, :], tf, ALU.mult,
                            r=[("ps", b1), ("tmpf", hf)], w=[vk])
                if tile == 0:
                    self.op("dve", lambda e, vb=vb: e.memset(vb[:, 0:30], 0.0), w=[vk])
                else:
                    self.cp(vb[:, 0:30], self.convc[:, j, :], r=["convc"], w=[vk])
                self.cp(self.convc[:, j, :], vb[:, VW - 30:VW], r=[vk], w=["convc"])
                for k in range(31):
                    self.ts(diag[:, k, :], self.identb[:, :], self.convw[:, l, j, k:k + 1], ALU.mult, r=["identb", "convw"], w=["diag"])
                self.cp(vq[:, 0:VW - 1], vb[:, 1:VW], r=[vk], w=["vq"])
                for hf in (0, 1):
                    bk = self.nb()
                    for k in range(31):
                        rhs = vb[:, hf * H + k:hf * H + k + H] if k % 2 == 0 else vq[:, hf * H + k - 1:hf * H + k - 1 + H]
                        self.mm(self.ps[:, bk, :], diag[:, k, :], rhs, start=(k == 0), stop=(k == 30),
                                r=["diag", vk, "vq"], w=[("ps", bk)])
                    self.act(acc[:, j, hs(hf)], self.ps[:, bk, :], AF.Identity, r=[("ps", bk), "pvec"], w=[("acc", j)],
                             bias=self.pvec[:, l, 2, j:j + 1], scale=1.0)
            c0 = OFF_CONV + 128 * j
            self.wstep([(self.w_in[l, :, c0:c0 + 128], 16), (self.w_in[l, :, c0 + 512:c0 + 640], 16)], 32, 128, fc)

        def ln():
            for hf in (0, 1):
                b1, b2 = 4, 5
                for j in range(4):
                    q1 = self.sq[:, 0, :]
                    q2 = self.sq[:, 1, :]
                    self.act(q1, acc[:, j, hs(hf)], AF.Copy, r=[("acc", j)], w=[("sq", 0)])
                    self.act(q2, acc[:, j, hs(hf)], AF.Square, r=[("acc", j)], w=[("sq", 1)])
                    self.mm(self.ps[:, b1, :], self.ones[:, :], q1, start=(j == 0), stop=(j == 3), r=["ones", ("sq", 0)], w=[("ps", b1)])
                    self.mm(self.ps[:, b2, :], self.ones[:, :], q2, start=(j == 0), stop=(j == 3), r=["ones", ("sq", 1)], w=[("ps", b2)])
                mean = self.rs[:, 0, :]
                rstd = self.rs[:, 1, :]
                tf = self.tmpf[:, 0, :]
                self.act(mean, self.ps[:, b1, :], AF.Copy, r=[("ps", b1)], w=["rs"], scale=1.0 / 512.0)
                self.tt(tf, mean, mean, ALU.mult, r=["rs"], w=[("tmpf", 0)])
                self.stt(tf, self.ps[:, b2, :], 1.0 / 512.0, tf, ALU.mult, ALU.subtract, r=[("ps", b2), ("tmpf", 0)], w=[("tmpf", 0)])
                self.act(rstd, tf, AF.Sqrt, r=[("tmpf", 0)], w=["rs"], bias=EPS, scale=1.0)
                self.op("dve", lambda e, rstd=rstd: e.reciprocal(out=rstd, in_=rstd), r=["rs"], w=["rs"])
                for j in range(4):
                    t2 = self.tmpf[:, 1, :]
                    self.tt(t2, acc[:, j, hs(hf)], mean, ALU.subtract, r=[("acc", j), "rs"], w=[("tmpf", 1)])
                    self.tt(t2, t2, rstd, ALU.mult, r=[("tmpf", 1), "rs"], w=[("tmpf", 1)])
                    self.act(self.ysT[:, 8 + j, hs(hf)], t2, AF.Silu, r=[("tmpf", 1), "pvec"], w=[("ys", 8 + j)],
                             scale=self.pvec[:, l, 3, j:j + 1], bias=self.pvec[:, l, 4, j:j + 1])
        self.step(ln)

    def phase1_gmlp(self, l, tile):
        SCR = self.SCR
        ug = SCR[:, 0:4096].rearrange("p (c t) -> p c t", c=4)
        vg = SCR[:, 4096:12288].bitcast(F32).rearrange("p (c t) -> p c t", c=8)
        vn = SCR[:, 12288:16384].rearrange("p (c t) -> p c t", c=8)
        st = self.st4
        for t in range(2):
            def fu(slot, wv, t=t):
                for jj in range(2):
                    j = 2 * t + jj
                    for hf in (0, 1):
                        bk = self.nb()
                        self.gemm(bk, self.inproj_pairs(wv, slot, jj, hf), self.wkeys(slot))
                        self.act(ug[:, j, hs(hf)], self.ps[:, bk, :], AF.Gelu, r=[("ps", bk)], w=[("scr", "ug", j)])
            self.wstep(self.w_in[l, :, OFF_GMLP + 256 * t:OFF_GMLP + 256 * t + 256], 16, 256, fu)
        for ch in range(2):
            def fv(slot, wv, ch=ch):
                for tt_ in range(8):
                    bk = self.nb()
                    pairs = [(self.hT[:, kc, tt_ * 128:(tt_ + 1) * 128], wv[:, kc, :], [("h", kc, tt_ // 4)]) for kc in range(16)]
                    self.gemm(bk, pairs, self.wkeys(slot), n=256)
                    self.act(vg[:, tt_, ch * 256:(ch + 1) * 256], self.ps[:, bk, 0:256], AF.Gelu, r=[("ps", bk)],
                             w=[("scr", "vg", tt_)], accum_out=st[:, tt_, ch:ch + 1])
            self.wstep(self.w_in[l, :, OFF_GMLP + 512 + 256 * ch:OFF_GMLP + 512 + 256 * ch + 256], 16, 256, fv)

        def g():
            for tt_ in range(8):
                R = [("scr", "vg", tt_), "st4"]
                W = ["st4"]
                s1, s2, mean, var, rstd, nb_ = [st[:, tt_, i:i + 1] for i in range(2, 8)]
                tf = self.tmpf[:, tt_ % 2, :]
                tk = ("tmpf", tt_ % 2)
                self.act(tf, vg[:, tt_, :], AF.Square, r=R, w=[tk, "st4"], accum_out=s2)
                self.tt(s1, st[:, tt_, 0:1], st[:, tt_, 1:2], ALU.add, r=R, w=W)
                self.ts(mean, s1, 1.0 / 512.0, ALU.mult, r=R, w=W)
                self.tt(var, mean, mean, ALU.mult, r=R, w=W)
                self.stt(var, s2, 1.0 / 512.0, var, ALU.mult, ALU.subtract, r=R, w=W)
                self.act(rstd, var, AF.Sqrt, r=R, w=W, bias=EPS, scale=1.0)
                self.op("dve", lambda e, rstd=rstd: e.reciprocal(out=rstd, in_=rstd), r=R, w=W)
                self.stt(nb_, mean, -1.0, rstd, ALU.mult, ALU.mult, r=R, w=W)
                self.act(tf, vg[:, tt_, :], AF.Identity, r=R + [tk], w=[tk], scale=rstd, bias=nb_)
                self.tt(tf, tf, self.gln[:, 0, :], ALU.mult, r=[tk, "gln"], w=[tk])
                self.tt(vn[:, tt_, :], tf, self.gln[:, 1, :], ALU.add, r=[tk, "gln"], w=[("scr", "vn", tt_)])
            for hd in range(4):
                for hf in (0, 1):
                    bk = self.nb()
                    for q in range(4):
                        tt_ = hf * 4 + q
                        self.mm(self.ps[:, bk, q * 128:(q + 1) * 128], vn[:, tt_, hd * 128:(hd + 1) * 128], self.wsT[:, hd, :],
                                True, True, r=[("scr", "vn", tt_), "wsT"], w=[("ps", bk)])
                    tf = self.tmpf[:, hf, :]
                    tk = ("tmpf", hf)
                    self.tt(tf.rearrange("p (q t) -> p q t", q=4), self.ps[:, bk, :].rearrange("p (q t) -> p q t", q=4),
                            self.gln[:, 2, hd * 128:(hd + 1) * 128].unsqueeze(1).to_broadcast([128, 4, 128]), ALU.add,
                            r=[("ps", bk), "gln"], w=[tk])
                    self.tt(self.ysT[:, 12 + hd, hs(hf)], tf, ug[:, hd, hs(hf)], ALU.mult, r=[tk, ("scr", "ug", hd)], w=[("ys", 12 + hd)])
        self.step(g)

    def phase2_merge(self, l):
        for j in range(16):
            for k in range(4):
                def fg(slot, wv, j=j, k=k):
                    wk_ = self.wkeys(slot)
                    for hf in (0, 1):
                        bg, by = self.nb(), self.nb()
                        ab = 4 + hf
                        self.gemm(bg, [(wv[:, kc, :], self.hT[:, kc, hs(hf)], [("h", kc, hf)]) for kc in range(16)], wk_)
                        self.gemm(by, [(wv[:, 16 + kc, :], self.ysT[:, 4 * k + kc, hs(hf)], [("ys", 4 * k + kc)]) for kc in range(4)], wk_)
                        sg = self.rs[:, hf, :]
                        self.act(sg, self.ps[:, bg, :], AF.Sigmoid, r=[("ps", bg)], w=[("rsg", hf)])
                        if k == 0:
                            self.tt(self.ps[:, ab, :], self.ps[:, by, :], sg, ALU.mult, r=[("ps", by), ("rsg", hf)], w=[("ps", ab)])
                        else:
                            self.tt(sg, self.ps[:, by, :], sg, ALU.mult, r=[("ps", by), ("rsg", hf)], w=[("rsg", hf)])
                            if k < 3:
                                self.tt(self.ps[:, ab, :], self.ps[:, ab, :], sg, ALU.add, r=[("rsg", hf), ("ps", ab)], w=[("ps", ab)])
                            else:
                                self.tt(self.mgT[:, j, hs(hf)], self.ps[:, ab, :], sg, ALU.add, r=[("rsg", hf), ("ps", ab)], w=[("mg", j)])
                c0 = OFF_GATE + k * D + 128 * j
                self.wstep([(self.w_in[l, :, c0:c0 + 128], 16), (self.w_branch[l, 512 * k:512 * k + 512, 128 * j:128 * j + 128], 4)],
                           20, 128, fg)

    def norm_resid(self, l, gkind, hf, src_chunks, src_keys):
        rs = self.rs[:, hf, :]
        self.rstd_from_bank(6, rs, 1.0 / D)
        for c in range(16):
            tf = self.tmpf[:, c % 2, :]
            tk = ("tmpf", c % 2)
            self.stt(tf, src_chunks[c], self.gv[:, l, gkind, c:c + 1], rs, ALU.mult, ALU.mult, r=[src_keys[c], "gv", "rs"], w=[tk])
            self.tt(self.xT[:, c, hs(hf)], self.xT[:, c, hs(hf)], tf, ALU.add, r=[tk, ("x", c, hf)], w=[("x", c, hf)])

    def phase3_wo(self, l, hf):
        for j in range(16):
            def f(slot, wv, j=j):
                bk = self.nb()
                self.gemm(bk, [(wv[:, kc, :], self.mgT[:, kc, hs(hf)], [("mg", kc)]) for kc in range(16)], self.wkeys(slot))
                self.act(self.mixF[:, j, :], self.ps[:, bk, :], AF.Copy, r=[("ps", bk)], w=[("ys", j)])
                sq = self.sq[:, j % 2, :]
                self.act(sq, self.ps[:, bk, :], AF.Square, r=[("ps", bk)], w=[("sq", j % 2)])
                self.mm(self.ps[:, 6, :], self.ones[:, :], sq, start=(j == 0), stop=(j == 15), r=["ones", ("sq", j % 2)], w=[("ps", 6)])
            self.wstep(self.w_o[l, :, 128 * j:128 * j + 128], 16, 128, f)
        self.step(lambda: self.norm_resid(l, 1, hf, [self.mixF[:, c, :] for c in range(16)], [("ys", c) for c in range(16)]))

    def phase4_ffn(self, l, hf):
        self.prenorm(l, 2, (hf,))
        oh = 1 - hf
        fS = [self.hT[:, c, hs(oh)] for c in range(16)]
        for j in range(64):
            def f1(slot, wv, j=j):
                bk = self.nb()
                self.gemm(bk, [(wv[:, kc, :], self.hT[:, kc, hs(hf)], [("h", kc, hf)]) for kc in range(16)], self.wkeys(slot))
                tf = self.tmpf[:, j % 2, :]
                tk = ("tmpf", j % 2)
                self.act(tf, self.ps[:, bk, :], AF.Relu, r=[("ps", bk)], w=[tk])
                key = ("ys", j // 2) if j < 32 else ("mg", (j - 32) // 2)
                self.tt(self.f1T[:, j, :], tf, tf, ALU.mult, r=[tk], w=[key])
            self.wstep(self.w_ff1[l, :, 128 * j:128 * j + 128], 16, 128, f1)
        for j in range(16):
            for q in range(4):
                def f2(slot, wv, j=j, q=q):
                    bk = j % 2
                    pairs = []
                    for kc in range(16):
                        fc = q * 16 + kc
                        key = ("ys", fc // 2) if fc < 32 else ("mg", (fc - 32) // 2)
                        pairs.append((wv[:, kc, :], self.f1T[:, fc, :], [key]))
                    self.gemm(bk, pairs, self.wkeys(slot), first=(q == 0), last=(q == 3))
                    if q == 3:
                        self.act(fS[j], self.ps[:, bk, :], AF.Copy, r=[("ps", bk)], w=[("h", j, oh)])
                        sq = self.sq[:, j % 2, :]
                        self.act(sq, self.ps[:, bk, :], AF.Square, r=[("ps", bk)], w=[("sq", j % 2)])
                        self.mm(self.ps[:, 6, :], self.ones[:, :], sq, start=(j == 0), stop=(j == 15),
                                r=["ones", ("sq", j % 2)], w=[("ps", 6)])
                    self.bank_rr = 2
                self.wstep(self.w_ff2[l, 2048 * q:2048 * q + 2048, 128 * j:128 * j + 128], 16, 128, f2)
        self.step(lambda: self.norm_resid(l, 3, hf, fS, [("h", c, oh) for c in range(16)]))


_NC_CACHE = {}


def _host_layout(inp, wt_tensors):
    f = lambda a: np.ascontiguousarray(np.asarray(a, dtype=np.float32))
    x = f(inp["x"])
    per_core = []
    for b in range(NB):
        xb = x[b].reshape(NT, TT, 16, 128).transpose(0, 3, 2, 1)
        per_core.append(f(xb))
    sh = {}
    sh["pool_w"] = f(inp["pool_w"]).reshape(L, 512, 128)
    raw = {"w_in": inp["w_in"], "w_glu": inp["ssm_w_glu"], "w_branch": np.asarray(inp["w_branch"]).reshape(L, 2048, D),
           "w_o": inp["w_o"], "w_ff1": inp["w_ff1"], "w_ff2": inp["w_ff2"]}
    for name, _ap, used, recs in wt_tensors:
        buf = np.zeros((128, WCAP), np.float32)
        for (parts, ncols), off, n in recs:
            o = off
            for (wname, l_, r0, nrows, c0, nc_), kk in parts:
                assert nc_ == ncols and nrows == kk * 128
                a = np.asarray(raw[wname][l_][r0:r0 + nrows, c0:c0 + ncols], dtype=np.float32)
                buf[:, o:o + kk * ncols] = a.reshape(kk, 128, ncols).transpose(1, 0, 2).reshape(128, kk * ncols)
                o += kk * ncols
        sh[name] = buf
    gv = np.stack([f(inp[k]) for k in ("g_pre_mix", "g_post_mix", "g_pre_mlp", "g_post_mlp")], axis=1)
    sh["gv"] = f(gv.reshape(L, 4, 16, 128).transpose(3, 0, 1, 2))
    a_re, a_im, ldt = f(inp["ssm_a_re"]), f(inp["ssm_a_im"]), f(inp["ssm_log_dt"])
    ldt_b = np.broadcast_to(ldt[:, :, None], a_re.shape)
    a3 = np.stack([a_re, a_im, ldt_b], axis=1)
    a3 = a3.reshape(L, 3, 16, 2, 64).transpose(3, 4, 0, 1, 2)
    sh["ssm_a"] = f(a3.reshape(128, L, 3, 16))
    bT = np.zeros((L, 2, 16, 128, 128), np.float32)
    cT = np.zeros((L, 2, 16, 128, 128), np.float32)
    br, bi = f(inp["ssm_b_re"]), f(inp["ssm_b_im"])
    cr, ci = f(inp["ssm_c_re"]), f(inp["ssm_c_im"])
    for gp in range(16):
        for g2 in range(2):
            g = 2 * gp + g2
            g8 = g % 8
            for ri, (bb, cc) in enumerate(((br, cr), (bi, ci))):
                bT[:, ri, gp, g8 * 16:(g8 + 1) * 16, g2 * 64:(g2 + 1) * 64] = bb[:, g].transpose(0, 2, 1)
                cT[:, ri, gp, g2 * 64:(g2 + 1) * 64, g8 * 16:(g8 + 1) * 16] = cc[:, g].transpose(0, 2, 1)
    sh["ssm_bT"] = bT
    sh["ssm_cT"] = cT
    pv = np.stack([f(inp[k]) for k in ("ssm_d", "pool_scale", "conv_b", "conv_ln_g", "conv_ln_b")], axis=1)
    sh["pvec"] = f(pv.reshape(L, 5, 4, 128).transpose(3, 0, 1, 2))
    cw = f(inp["conv_w"])
    sh["convw"] = f(cw.reshape(L, 31, 4, 128).transpose(3, 0, 2, 1))
    sh["gln"] = f(np.stack([f(inp["gmlp_ln_g"]), f(inp["gmlp_ln_b"]), f(inp["gmlp_bs"]).reshape(L, 512)], axis=1))
    ws = f(inp["gmlp_ws"])
    sh["wsT"] = f(ws.transpose(0, 3, 1, 2))
    s_idx = np.arange(128)[:, None]
    t_idx = np.arange(128)[None, :]
    sh["mask"] = (t_idx >= s_idx).astype(np.float32)
    inv = np.zeros((128, 4, 15), np.float32)
    for gi in range(4):
        w = 2 << gi
        inv[:, gi, :] = 1.0 / np.minimum(np.arange(15) + 1, w)
    sh["invcnt"] = inv
    sh["ident"] = np.eye(128, dtype=np.float32)
    return per_core, sh


def kernel(**inputs):
    key = "full"
    if key not in _NC_CACHE:
        b = Builder(layers=list(range(L)))
        _NC_CACHE[key] = (b.build(), b.wt_tensors)
    nc, wt_tensors = _NC_CACHE[key]
    per_core, sh = _host_layout(inputs, wt_tensors)
    real = {0: 0, 1: 1, 4: 2, 5: 3}
    zsh = {}
    for k, v in sh.items():
        zsh[k] = np.zeros_like(v) if k.startswith("wt") else v
    zx = np.zeros_like(per_core[0])
    in_maps = []
    for c in range(8):
        if c in real:
            m = dict(sh)
            m["xin"] = per_core[real[c]]
        else:
            m = dict(zsh)
            m["xin"] = zx
        in_maps.append(m)
    res = run_bass_kernel_spmd(nc, in_maps, core_ids=list(range(8)))
    out = np.empty((NB, SEQ, D), np.float32)
    for c, b in real.items():
        y = np.asarray(res.results[c]["yout"])
        out[b] = y.transpose(0, 3, 2, 1).reshape(SEQ, D)
    return out
```

```python
import contextlib
import math
import numpy as np
import concourse.bass as bass
import concourse.mybir as mybir
from concourse.bass_utils import run_bass_kernel_spmd

F32 = mybir.dt.float32
BF16 = mybir.dt.bfloat16
AF = mybir.ActivationFunctionType
ALU = mybir.AluOpType

D = 2048
SEQ = 2048
NB = 4
L = 2
TT = 1024
NT = SEQ // TT
H = 512
IN_COLS = 11264
OFF_POOL, OFF_CONV, OFF_GMLP, OFF_GATE = 512, 1024, 2048, 3072
DFF = 8192
EPS = 1e-6
NUNIT = 4
UNIT = 2048
WSLOT = 2 * UNIT
WCAP = 294912


class Sched:
    def __init__(self, samesync=True):
        self.ops = []
        self.last_w = {}
        self.readers = {}
        self.last_dma = {}
        self.samesync = samesync

    def add(self, eng, fn, reads=(), writes=(), dma_key=None, deps=()):
        idx = len(self.ops)
        d = set(deps)
        for k in reads:
            w = self.last_w.get(k)
            if w is not None:
                d.add(w)
        for k in writes:
            w = self.last_w.get(k)
            if w is not None:
                d.add(w)
            for r in self.readers.get(k, {}).values():
                d.add(r)
        if dma_key is not None:
            p = self.last_dma.get(dma_key)
            if p is not None:
                d.add(p)
            self.last_dma[dma_key] = idx
        d.discard(idx)
        stream = ("dma", idx) if dma_key is not None else eng
        for k in reads:
            self.readers.setdefault(k, {})[stream] = idx
        for k in writes:
            self.last_w[k] = idx
            self.readers[k] = {}
        self.ops.append(dict(eng=eng, fn=fn, deps=sorted(d), dma_key=dma_key, observed=False, waits=[]))
        return idx

    def finalize(self):
        ops = self.ops
        known = {e: {} for e in ("pe", "act", "dve", "pool", "sp")}
        snaps = [None] * len(ops)
        for i, op in enumerate(ops):
            e = op["eng"]
            kn = known[e]
            best = {}
            for j in op["deps"]:
                pj = ops[j]
                if pj["dma_key"] is not None:
                    key = ("dma", pj["dma_key"])
                else:
                    key = pj["eng"]
                    if key == e and op["dma_key"] is None and (not self.samesync or e == "pe"):
                        continue
                if kn.get(key, -1) >= j:
                    continue
                if best.get(key, -1) < j:
                    best[key] = j
            for key, j in sorted(best.items(), key=lambda kv: kv[1]):
                if kn.get(key, -1) >= j:
                    continue
                ops[j]["observed"] = True
                op["waits"].append(j)
                kn[key] = j
                for k2, v2 in snaps[j].items():
                    if kn.get(k2, -1) < v2:
                        kn[k2] = v2
            snaps[i] = dict(kn)
        cnt = {}
        for op in ops:
            if op["dma_key"] is not None:
                key = ("dma", op["dma_key"])
                cnt[key] = cnt.get(key, 0) + 16
                op["token"] = (key, cnt[key])
            elif op["observed"]:
                key = op["eng"]
                cnt[key] = cnt.get(key, 0) + 1
                op["token"] = (key, cnt[key])
            else:
                op["token"] = None
        self.sem_keys = sorted(cnt.keys(), key=str)
        return self

    def stats(self):
        from collections import Counter
        c = Counter(op["eng"] for op in self.ops)
        return dict(ops=dict(c), waits=sum(len(op["waits"]) for op in self.ops),
                    observed=sum(1 for op in self.ops if op["observed"]), sems=len(self.sem_keys))

    def run(self, engine_obj, ename, sems):
        ops = self.ops
        for op in ops:
            if op["eng"] != ename:
                continue
            for j in op["waits"]:
                key, val = ops[j]["token"]
                engine_obj.wait_ge(sems[key], val)
            ins = op["fn"](engine_obj)
            if op["token"] is not None and ins is not None:
                key, val = op["token"]
                ins.then_inc(sems[key], 16 if isinstance(key, tuple) else 1)


def hs(hf):
    return slice(hf * H, (hf + 1) * H)


class WSpec:
    def __init__(self, name, K, N):
        self.name, self.K, self.N = name, K, N

    def __getitem__(self, idx):
        if not isinstance(idx, tuple):
            idx = (idx, slice(None), slice(None))
        l, rs, cs = idx
        r0, r1, _ = rs.indices(self.K)
        c0, c1, _ = cs.indices(self.N)
        return (self.name, l, r0, r1 - r0, c0, c1 - c0)


class Builder:
    def __init__(self, layers, debug=None, ntiles=NT, phases="spcgm34"):
        self.ntiles = ntiles
        self.phases = phases
        self.layers = layers
        self.debug = debug
        self.S = Sched()
        self.steps = []
        self.bank_rr = 0

    def op(self, eng, fn, r=(), w=()):
        return self.S.add(eng, fn, reads=r, writes=w)

    def dma(self, eng, out, in_, key, r=(), w=()):
        return self.S.add(eng, lambda e: e.dma_start(out=out, in_=in_), reads=r, writes=w, dma_key=key)

    def mm(self, out, lhsT, rhs, start, stop, r, w):
        return self.op("pe", lambda e: e.matmul(out, lhsT, rhs, start=start, stop=stop), r=r, w=w)

    def act(self, out, in_, func, r, w, bias=None, scale=None, accum_out=None):
        kw = {}
        if bias is not None:
            kw["bias"] = bias
        if scale is not None:
            kw["scale"] = scale
        if accum_out is not None:
            kw["accum_out"] = accum_out
        return self.op("act", lambda e: e.activation(out=out, in_=in_, func=func, **kw), r=r, w=w)

    def tt(self, out, in0, in1, op_, r, w, eng="dve"):
        return self.op(eng, lambda e: e.tensor_tensor(out=out, in0=in0, in1=in1, op=op_), r=r, w=w)

    def ts(self, out, in0, s1, op0, r, w, s2=None, op1=None, eng="dve"):
        if op1 is None:
            return self.op(eng, lambda e: e.tensor_scalar(out=out, in0=in0, scalar1=s1, scalar2=None, op0=op0), r=r, w=w)
        return self.op(eng, lambda e: e.tensor_scalar(out=out, in0=in0, scalar1=s1, scalar2=s2, op0=op0, op1=op1), r=r, w=w)

    def stt(self, out, in0, scalar, in1, op0, op1, r, w):
        return self.op("dve", lambda e: e.scalar_tensor_tensor(out=out, in0=in0, scalar=scalar, in1=in1, op0=op0, op1=op1), r=r, w=w)

    def cp(self, out, in_, r, w, eng="dve"):
        return self.op(eng, lambda e: e.tensor_copy(out=out, in_=in_), r=r, w=w)

    def nb(self):
        b = self.bank_rr
        self.bank_rr = (self.bank_rr + 1) % 4
        return b

    def wstep(self, src, kc, ncols, fn):
        parts = tuple(src) if isinstance(src, list) else ((src, kc),)
        n = sum(k for _, k in parts) * ncols
        assert n <= WSLOT
        key = (parts, ncols)
        if key not in self.cat:
            if not self.wt_tensors or self.wt_tensors[-1][2] + n > WCAP:
                name = "wt%d" % len(self.wt_tensors)
                self.wt_tensors.append([name, self._dr(name, [128, WCAP]), 0, []])
            tns = self.wt_tensors[-1]
            self.cat[key] = (len(self.wt_tensors) - 1, tns[2], n)
            tns[3].append((key, tns[2], n))
            tns[2] += n
        self.steps.append((self.cat[key], parts, ncols, fn))

    def step(self, fn):
        self.steps.append((None, None, 0, fn))

    def wkeys(self, slot):
        u0, cnt = slot
        return [("w", u0 + i) for i in range(cnt)]

    def run_steps(self):
        steps = self.steps
        tiles = [i for i, s in enumerate(steps) if s[0] is not None]
        owner = [-1] * NUNIT
        loaded = {}
        st = dict(next=0, ptr=0)

        def try_issue(cur):
            while st["next"] < len(tiles):
                i = tiles[st["next"]]
                (ti, off, n), parts, ncols, _ = steps[i]
                cnt = 1 if n <= UNIT else 2
                p = st["ptr"]
                if cnt == 2 and p % 2 == 1:
                    p = (p + 1) % NUNIT
                if p + cnt > NUNIT:
                    p = 0
                if any(owner[p + k] >= cur for k in range(cnt)):
                    return
                kct = sum(kk for _, kk in parts)
                dst = self.wring[:, p * UNIT:p * UNIT + n]
                self.dma("pool", dst, self.wt_tensors[ti][1][:, off:off + n], key=("w", p), w=[("w", p + k) for k in range(cnt)])
                for k in range(cnt):
                    owner[p + k] = i
                loaded[i] = ((p, cnt), dst.rearrange("p (c n) -> p c n", c=kct))
                st["ptr"] = (p + cnt) % NUNIT
                st["next"] += 1

        for i, s in enumerate(steps):
            try_issue(i)
            if s[0] is not None:
                assert i in loaded
                owner_save = None
                s[3](*loaded[i])
            else:
                s[3]()
        self.steps = []

    def gemm(self, bank, pairs, rkeys, first=True, last=True, n=H):
        np_ = len(pairs)
        for i, (lt, rh, rk) in enumerate(pairs):
            self.mm(self.ps[:, bank, 0:n], lt, rh, start=(first and i == 0), stop=(last and i == np_ - 1),
                    r=list(rkeys) + list(rk), w=[("ps", bank)])

    def build(self):
        nc = bass.Bass("TRN2", target_bir_lowering=False)
        self.nc = nc
        es = contextlib.ExitStack()
        self.es = es
        dr = lambda name, shape, kind="ExternalInput": nc.dram_tensor(name, list(shape), F32, kind=kind).ap()
        self.xin = dr("xin", [NT, 128, 16, TT])
        self.yout = dr("yout", [NT, 128, 16, TT], "ExternalOutput")
        self.xs = self.yout
        self.w_in = WSpec("w_in", D, IN_COLS)
        self.w_glu = WSpec("w_glu", 512, 1024)
        self.pool_w = dr("pool_w", [L, 512, 128])
        self.w_branch = WSpec("w_branch", 2048, D)
        self.w_o = WSpec("w_o", D, D)
        self.w_ff1 = WSpec("w_ff1", D, DFF)
        self.w_ff2 = WSpec("w_ff2", DFF, D)
        self.cat = {}
        self.wt_tensors = []
        self._dr = dr
        self.gv_d = dr("gv", [128, L, 4, 16])
        self.ssm_a_d = dr("ssm_a", [128, L, 3, 16])
        self.ssm_bT_d = dr("ssm_bT", [L, 2, 16, 128, 128])
        self.ssm_cT_d = dr("ssm_cT", [L, 2, 16, 128, 128])
        self.pvec_d = dr("pvec", [128, L, 5, 4])
        self.convw_d = dr("convw", [128, L, 4, 31])
        self.gln_d = dr("gln", [L, 3, 512])
        self.wsT_d = dr("wsT", [L, 128, 4, 128])
        self.mask_d = dr("mask", [128, 128])
        self.invcnt_d = dr("invcnt", [128, 4, 15])
        self.ident_d = dr("ident", [128, 128])
        if self.debug:
            self.dbg = dr("dbg", [128, 16, TT], "ExternalOutput")

        sb = lambda name, shape, dt=F32: es.enter_context(nc.sbuf_tensor(name, list(shape), dt))
        with es:
            self.xT = sb("xT", [128, 16, TT])
            self.hT = sb("hT", [128, 16, TT], BF16)
            self.RC = sb("RC", [128, 32768], BF16)
            self.wring = sb("wring", [128, NUNIT * UNIT], BF16)
            self.ps = es.enter_context(nc.psum_tensor("ps", [128, 8, H], F32))
            RC = self.RC
            self.ysT = RC[:, 0:16384].rearrange("p (c t) -> p c t", c=16)
            self.mgT = RC[:, 16384:32768].rearrange("p (c t) -> p c t", c=16)
            self.mixF = RC[:, 0:16384].bitcast(F32).rearrange("p (c t) -> p c t", c=16)
            self.f1T = RC[:, :].rearrange("p (c t) -> p c t", c=64)
            SCR = RC[:, 16384:32768]
            self.SCR = SCR
            self.ones = sb("ones", [128, 128], BF16)
            self.gv = sb("gvs", [128, L, 4, 16])
            self.pvec = sb("pvecs", [128, L, 5, 4])
            self.convw = sb("convws", [128, L, 4, 31])
            self.ssma = sb("ssmas", [128, 3, 16])
            self.spt = sb("spt", [128, 16, 16])
            self.Pr = sb("Pr", [128, 10, 16])
            self.Pi = sb("Pi", [128, 10, 16])
            self.Pn = sb("Pn", [128, 10, 16])
            self.Cre = RC[:, 12288:14336].rearrange("p (c t) -> p c t", c=16)
            self.Cim = RC[:, 14336:16384].rearrange("p (c t) -> p c t", c=16)
            self.ctmp = sb("ctmp", [128, 4, 128])
            self.Bt = sb("Bt", [128, 2, 2, 128], BF16)
            self.carry = sb("carry", [128, 16, 2])
            self.poolc = sb("poolc", [128, 4, 15])
            self.convc = sb("convc", [128, 4, 30], BF16)
            self.identb = sb("identb", [128, 128], BF16)
            self.gln = sb("glns", [128, 3, 512])
            self.wsT = sb("wsTs", [128, 4, 128], BF16)
            self.maskt = sb("maskt", [128, 128])
            self.invc = sb("invc", [128, 4, 15])
            self.poolw = sb("poolw", [128, 4, 128], BF16)
            self.rs = sb("rs", [128, 2, H])
            self.tmpf = sb("tmpf", [128, 2, H])
            self.sq = sb("sq", [128, 2, H], BF16)
            self.st4 = sb("st4", [128, 8, 8])

            self.emit_all()

            self.S.finalize()
            print("sched:", self.S.stats(), "sbuf_remaining", nc.sbuf_bytes_remaining, flush=True)
            sems = {k: es.enter_context(nc.semaphore("sem%d" % i)) for i, k in enumerate(self.S.sem_keys)}
            block = es.enter_context(nc.Block())
            S = self.S

            @block.tensor
            def _(e):
                S.run(e, "pe", sems)

            @block.scalar
            def _(e):
                S.run(e, "act", sems)

            @block.vector
            def _(e):
                S.run(e, "dve", sems)

            @block.gpsimd
            def _(e):
                S.run(e, "pool", sems)

            @block.sync
            def _(e):
                S.run(e, "sp", sems)
        return nc

    def emit_all(self):
        self.op("dve", lambda e: e.memset(self.ones[:, :], 1.0), w=["ones"])
        self.dma("sp", self.gv[:], self.gv_d, "c_gv", w=["gv"])
        self.dma("sp", self.pvec[:], self.pvec_d, "c_pvec", w=["pvec"])
        self.dma("sp", self.convw[:], self.convw_d, "c_convw", w=["convw"])
        self.dma("sp", self.maskt[:], self.mask_d, "c_mask", w=["mask"])
        self.dma("sp", self.invc[:], self.invcnt_d, "c_invc", w=["invc"])
        self.dma("pool", self.identb[:], self.ident_d, "c_ident", w=["identb"])
        fin = []
        nl = len(self.layers)
        passes = [(li, l, tile) for li, l in enumerate(self.layers) for tile in range(self.ntiles)]

        def xload(pi, hf):
            li, l, tile = passes[pi]
            src = self.xin if li == 0 else self.xs
            for q in range(2):
                self.dma("sp", self.xT[:, 8 * q:8 * q + 8, hs(hf)], src[tile, :, 8 * q:8 * q + 8, hs(hf)], "xload%d_%d" % (hf, q),
                         r=[("xs", tile, hf)] if li > 0 else [], w=[("x", c, hf) for c in range(8 * q, 8 * q + 8)])

        def xstore(pi, hf):
            li, l, tile = passes[pi]
            dst = self.yout if li == nl - 1 else self.xs
            for q in range(2):
                d_ = self.dma("sp", dst[tile, :, 8 * q:8 * q + 8, hs(hf)], self.xT[:, 8 * q:8 * q + 8, hs(hf)], "xstore%d_%d" % (hf, q),
                              r=[("x", c, hf) for c in range(8 * q, 8 * q + 8)], w=[("xs", tile, hf)])
                if li == nl - 1:
                    fin.append(d_)

        def after_half(pi, hf):
            xstore(pi, hf)
            if pi + 1 < len(passes):
                xload(pi + 1, hf)

        xload(0, 0)
        xload(0, 1)
        for pi, (li, l, tile) in enumerate(passes):
            if tile == 0:
                self.layer_setup(l)
            self.prenorm(l, 0, (0, 1))
            P = self.phases
            if "s" in P:
                self.phase1_ssm(l, tile)
            if "p" in P:
                self.phase1_pool(l, tile)
            if "c" in P:
                self.phase1_conv(l, tile)
            if "g" in P:
                self.phase1_gmlp(l, tile)
            if "m" in P:
                self.phase2_merge(l)
            if "3" in P:
                for hf in (0, 1):
                    self.phase3_wo(l, hf)
            for hf in (0, 1):
                if "4" in P:
                    self.phase4_ffn(l, hf)
                self.step(lambda pi=pi, hf=hf: after_half(pi, hf))
            self.run_steps()
        self.S.add("sp", lambda e: None, deps=fin)

    def dump_debug(self):
        pass

    def layer_setup(self, l):
        t = lambda i: self.spt[:, i, :]
        K = "spt"
        self.dma("sp", self.ssma[:], self.ssm_a_d[:, l, :, :], "c_ssma", w=["ssma"])
        self.dma("sp", self.gln[:, 0, :], self.gln_d[l, 0:1, :].partition_broadcast(128), "c_gln0", w=["gln"])
        self.dma("sp", self.gln[:, 1, :], self.gln_d[l, 1:2, :].partition_broadcast(128), "c_gln1", w=["gln"])
        self.dma("sp", self.gln[:, 2, :], self.gln_d[l, 2:3, :].partition_broadcast(128), "c_gln2", w=["gln"])
        CT = ["ctmp0", "ctmp1", "ctmp2", "ctmp3"]
        self.dma("sp", self.ctmp[:], self.wsT_d[l], "c_wsf", w=CT)
        self.dma("pool", self.poolw[:], self.pool_w[l].rearrange("(g p) d -> p g d", p=128), "c_poolw", w=["poolw"])
        for hd in range(4):
            self.tt(self.wsT[:, hd, :], self.ctmp[:, hd, :], self.maskt[:, :], ALU.mult, r=CT + ["mask"], w=["wsT"])
        ar, ai, ldt = self.ssma[:, 0, :], self.ssma[:, 1, :], self.ssma[:, 2, :]
        dt, mag, ang, c, s, t1, t2, lr, li_, den, lm1, fr, fi, t3 = [t(i) for i in range(14)]
        R, W = ["ssma", K], [K]
        self.act(dt, ldt, AF.Exp, r=R, w=W)
        self.tt(t1, ar, dt, ALU.mult, r=R, w=W)
        self.act(mag, t1, AF.Exp, r=R, w=W)
        self.tt(ang, ai, dt, ALU.mult, r=R, w=W)
        self.act(s, ang, AF.Sin, r=R, w=W, scale=1.0 / 32.0)
        self.ts(t2, ang, 1.0 / 32.0, ALU.mult, r=R, w=W, s2=math.pi / 2.0, op1=ALU.add)
        self.act(c, t2, AF.Sin, r=R, w=W)
        for _ in range(5):
            self.tt(t1, c, c, ALU.mult, r=R, w=W)
            self.tt(t2, s, s, ALU.mult, r=R, w=W)
            self.stt(s, c, 2.0, s, ALU.mult, ALU.mult, r=R, w=W)
            self.tt(c, t1, t2, ALU.subtract, r=R, w=W)
        self.tt(lr, mag, c, ALU.mult, r=R, w=W)
        self.tt(li_, mag, s, ALU.mult, r=R, w=W)
        self.tt(t1, ar, ar, ALU.mult, r=R, w=W)
        self.tt(t2, ai, ai, ALU.mult, r=R, w=W)
        self.tt(den, t1, t2, ALU.add, r=R, w=W)
        self.op("dve", lambda e: e.reciprocal(out=den, in_=den), r=R, w=W)
        self.ts(lm1, lr, -1.0, ALU.add, r=R, w=W)
        self.tt(t1, lm1, ar, ALU.mult, r=R, w=W)
        self.tt(t2, li_, ai, ALU.mult, r=R, w=W)
        self.tt(t1, t1, t2, ALU.add, r=R, w=W)
        self.tt(fr, t1, den, ALU.mult, r=R, w=W)
        self.tt(t1, li_, ar, ALU.mult, r=R, w=W)
        self.tt(t2, lm1, ai, ALU.mult, r=R, w=W)
        self.tt(t1, t1, t2, ALU.subtract, r=R, w=W)
        self.tt(fi, t1, den, ALU.mult, r=R, w=W)
        RP, WP = [K, "P"], ["P"]
        self.cp(self.Pr[:, 0, :], lr, r=RP, w=WP)
        self.cp(self.Pi[:, 0, :], li_, r=RP, w=WP)
        for d in range(9):
            pr, pi = self.Pr[:, d, :], self.Pi[:, d, :]
            self.tt(t1, pr, pr, ALU.mult, r=RP, w=[K])
            self.tt(t2, pi, pi, ALU.mult, r=RP, w=[K])
            self.tt(self.Pr[:, d + 1, :], t1, t2, ALU.subtract, r=RP, w=WP)
            self.stt(self.Pi[:, d + 1, :], pr, 2.0, pi, ALU.mult, ALU.mult, r=RP, w=WP)
        self.ts(self.Pn[:, :, :], self.Pi[:, :, :], -1.0, ALU.mult, r=RP, w=WP)

    def ssm_ctables(self, l):
        K = "spt"
        CK = [("ys", 12), ("ys", 13), ("ys", 14), ("ys", 15)]
        for gp in range(16):
            cr_t, ci_t, u1, u2 = [self.ctmp[:, i, :] for i in range(4)]
            self.dma("sp", cr_t, self.ssm_cT_d[l, 0, gp], "c_cr", w=["ctmp0"])
            self.dma("sp", ci_t, self.ssm_cT_d[l, 1, gp], "c_ci", w=["ctmp1"])
            frg, fig = self.spt[:, 11, gp:gp + 1], self.spt[:, 12, gp:gp + 1]
            self.ts(u1, ci_t, fig, ALU.mult, r=["ctmp1", K], w=["ctmp2"])
            self.stt(self.Cre[:, gp, :], cr_t, frg, u1, ALU.mult, ALU.subtract, r=["ctmp0", "ctmp2", K], w=CK[0:2])
            self.ts(u1, ci_t, frg, ALU.mult, r=["ctmp1", K] + CK[0:2], w=["ctmp2"])
            self.stt(u2, cr_t, fig, u1, ALU.mult, ALU.add, r=["ctmp0", "ctmp2", K], w=["ctmp3"])
            self.ts(self.Cim[:, gp, :], u2, -1.0, ALU.mult, r=["ctmp3"], w=CK[2:4])

    def rstd_from_bank(self, bank, dst, scale):
        self.act(dst, self.ps[:, bank, :], AF.Sqrt, r=[("ps", bank)], w=["rs"], bias=EPS, scale=scale)
        self.op("dve", lambda e: e.reciprocal(out=dst, in_=dst), r=["rs"], w=["rs"])

    def prenorm(self, l, gkind, halves):
        def f():
            for hf in halves:
                sbk = 4 + hf
                for c in range(16):
                    sq = self.sq[:, c % 2, :]
                    self.act(sq, self.xT[:, c, hs(hf)], AF.Square, r=[("x", c, hf)], w=[("sq", c % 2)])
                    self.mm(self.ps[:, sbk, :], self.ones[:, :], sq, start=(c == 0), stop=(c == 15),
                            r=["ones", ("sq", c % 2)], w=[("ps", sbk)])
                rs = self.rs[:, hf, :]
                self.rstd_from_bank(sbk, rs, 1.0 / D)
                for c in range(16):
                    self.stt(self.hT[:, c, hs(hf)], self.xT[:, c, hs(hf)], self.gv[:, l, gkind, c:c + 1], rs,
                             ALU.mult, ALU.mult, r=[("x", c, hf), "gv", "rs"], w=[("h", c, hf)])
        self.step(f)

    def inproj_pairs(self, wv, slot, jj, hf):
        return [(wv[:, kc, jj * 128:(jj + 1) * 128], self.hT[:, kc, hs(hf)], [("h", kc, hf)]) for kc in range(16)]

    def phase1_ssm(self, l, tile):
        SCR = self.SCR
        RC = self.RC
        uT = SCR[:, 0:4096].rearrange("p (c t) -> p c t", c=4)
        zT = RC[:, 8192:12288].rearrange("p (c t) -> p c t", c=4)
        stf = [SCR[:, 4096 + 4096 * i:8192 + 4096 * i].bitcast(F32).rearrange("p (c t) -> p c t", c=2) for i in range(2)]
        stb = [SCR[:, 12288 + 2048 * i:14336 + 2048 * i].rearrange("p (c t) -> p c t", c=2) for i in range(2)]
        CK = [("ys", 12), ("ys", 13), ("ys", 14), ("ys", 15)]
        self.step(lambda: self.ssm_ctables(l))
        for t in range(2):
            def f(slot, wv, t=t):
                for jj in range(2):
                    cb = 2 * t + jj
                    for hf in (0, 1):
                        bk = self.nb()
                        self.gemm(bk, self.inproj_pairs(wv, slot, jj, hf), self.wkeys(slot))
                        self.act(uT[:, cb, hs(hf)], self.ps[:, bk, :], AF.Copy, r=[("ps", bk)], w=[("scr", "uT", cb)])
            self.wstep(self.w_in[l, :, 256 * t:256 * t + 256], 16, 256, f)

        def g():
            for pr_ in range(8):
                gps = (2 * pr_, 2 * pr_ + 1)
                cb = pr_ // 2
                chains = []
                for bi, gp in enumerate(gps):
                    sk = ("stf", bi)
                    self.dma("pool", self.Bt[:, bi, 0, :], self.ssm_bT_d[l, 0, gp], ("bt", bi, 0), w=[("bt", bi, 0)])
                    self.dma("pool", self.Bt[:, bi, 1, :], self.ssm_bT_d[l, 1, gp], ("bt", bi, 1), w=[("bt", bi, 1)])
                    for ri in range(2):
                        for hf in (0, 1):
                            bk = 4 + ri * 2 + hf
                            self.mm(self.ps[:, bk, :], self.Bt[:, bi, ri, :], uT[:, cb, hs(hf)], True, True,
                                    r=[("bt", bi, ri), ("scr", "uT", cb)], w=[("ps", bk)])
                            self.act(stf[bi][:, ri, hs(hf)], self.ps[:, bk, :], AF.Copy, r=[("ps", bk)], w=[sk])
                    sr, si = stf[bi][:, 0, :], stf[bi][:, 1, :]
                    ops_ = []

                    def cmadd(a_r, a_i, b_r, b_i, d, extra=(), gp=gp, sk=sk, ops_=ops_, a_ri=None, b_ri=None):
                        rk = [sk, "P"] + list(extra)
                        p_r, p_i, p_n = self.Pr[:, d, gp:gp + 1], self.Pi[:, d, gp:gp + 1], self.Pn[:, d, gp:gp + 1]
                        if a_ri is not None:
                            ops_.append((a_ri, b_ri, p_r, rk, sk))
                        else:
                            ops_.append((a_r, b_r, p_r, rk, sk))
                            ops_.append((a_i, b_i, p_r, rk, sk))
                        ops_.append((a_r, b_i, p_n, rk, sk))
                        ops_.append((a_i, b_r, p_i, rk, sk))

                    if tile > 0:
                        cmadd(sr[:, 0:1], si[:, 0:1], self.carry[:, gp, 0:1], self.carry[:, gp, 1:2], 0, extra=["carry"])
                    for d in range(10):
                        s_ = 1 << d
                        A = slice(2 * s_ - 1, TT, 2 * s_)
                        Bv = slice(s_ - 1, TT, 2 * s_)
                        cmadd(sr[:, A], si[:, A], sr[:, Bv], si[:, Bv], d, a_ri=stf[bi][:, :, A], b_ri=stf[bi][:, :, Bv])
                    for d in range(8, -1, -1):
                        s_ = 1 << d
                        Bv = slice(2 * s_ - 1, TT - s_, 2 * s_)
                        A = slice(3 * s_ - 1, TT, 2 * s_)
                        cmadd(sr[:, A], si[:, A], sr[:, Bv], si[:, Bv], d, a_ri=stf[bi][:, :, A], b_ri=stf[bi][:, :, Bv])
                    chains.append(ops_)
                for k in range(max(len(c) for c in chains)):
                    for c in chains:
                        if k < len(c):
                            a, b, p, rk, sk = c[k]
                            self.stt(a, b, p, a, ALU.mult, ALU.add, r=rk, w=[sk])
                for bi, gp in enumerate(gps):
                    sk = ("stf", bi)
                    bk_ = ("stb", bi)
                    sr, si = stf[bi][:, 0, :], stf[bi][:, 1, :]
                    self.cp(self.carry[:, gp, 0:1], sr[:, TT - 1:TT], r=[sk], w=["carry"])
                    self.cp(self.carry[:, gp, 1:2], si[:, TT - 1:TT], r=[sk], w=["carry"])
                    self.act(stb[bi][:, 0, :], sr, AF.Copy, r=[sk], w=[bk_])
                    self.act(stb[bi][:, 1, :], si, AF.Copy, r=[sk], w=[bk_])
                    for hf in (0, 1):
                        yb = hf
                        self.mm(self.ps[:, yb, :], self.Cre[:, gp, :], stb[bi][:, 0, hs(hf)], start=(gp % 4 == 0), stop=False,
                                r=CK + [bk_], w=[("ps", yb)])
                        self.mm(self.ps[:, yb, :], self.Cim[:, gp, :], stb[bi][:, 1, hs(hf)], start=False, stop=(gp % 4 == 3),
                                r=CK + [bk_], w=[("ps", yb)])
                if pr_ % 2 == 1:
                    for hf in (0, 1):
                        tf = self.tmpf[:, hf, :]
                        self.stt(tf, uT[:, cb, hs(hf)], self.pvec[:, l, 0, cb:cb + 1], self.ps[:, hf, :], ALU.mult, ALU.add,
                                 r=[("scr", "uT", cb), "pvec", ("ps", hf)], w=[("tmpf", hf)])
                        self.act(zT[:, cb, hs(hf)], tf, AF.Gelu, r=[("tmpf", hf)], w=[("ys", 8 + cb)])
            self.bank_rr = 2
        self.step(g)

        def glu(slot, wv):
            for j in range(4):
                for hf in (0, 1):
                    b1, b2 = self.nb(), self.nb()
                    zk = lambda kc: [("ys", 8 + kc)]
                    self.gemm(b1, [(wv[:, kc, j * 128:(j + 1) * 128], zT[:, kc, hs(hf)], zk(kc)) for kc in range(4)], self.wkeys(slot))
                    self.gemm(b2, [(wv[:, kc, 512 + j * 128:512 + (j + 1) * 128], zT[:, kc, hs(hf)], zk(kc)) for kc in range(4)], self.wkeys(slot))
                    tf = self.tmpf[:, hf, :]
                    self.act(tf, self.ps[:, b2, :], AF.Sigmoid, r=[("ps", b2)], w=[("tmpf", hf)])
                    self.tt(self.ysT[:, j, hs(hf)], self.ps[:, b1, :], tf, ALU.mult, r=[("ps", b1), ("tmpf", hf)], w=[("ys", j)])
        self.wstep(self.w_glu[l], 4, 1024, glu)

    def phase1_pool(self, l, tile):
        SCR = self.SCR
        PW = 15 + TT
        Ub = SCR[:, 0:2 * PW].bitcast(F32)
        Ab = SCR[:, 2 * PW + 2:4 * PW + 2].bitcast(F32)
        Bb = SCR[:, 4 * PW + 4:6 * PW + 4].bitcast(F32)
        pooled = SCR[:, 6 * PW + 6:6 * PW + 6 + TT]
        for t in range(2):
            def f(slot, wv, t=t):
                for jj in range(2):
                    gi = 2 * t + jj
                    w = 2 << gi
                    for hf in (0, 1):
                        bk = self.nb()
                        self.gemm(bk, self.inproj_pairs(wv, slot, jj, hf), self.wkeys(slot))
                        self.act(Ub[:, 15 + hf * H:15 + (hf + 1) * H], self.ps[:, bk, :], AF.Copy, r=[("ps", bk)], w=["pU"])
                    if tile == 0:
                        self.op("dve", lambda e: e.memset(Ub[:, 0:15], 0.0), w=["pU"])
                    else:
                        self.cp(Ub[:, 0:15], self.poolc[:, gi, :], r=["poolc"], w=["pU"])
                    src, srck = Ub, "pU"
                    for k in range(gi + 1):
                        dst, dk = (Ab, "pA") if k % 2 == 0 else (Bb, "pB")
                        lo = (2 << k) - 1
                        sh = 1 << k
                        self.tt(dst[:, lo:PW], src[:, lo:PW], src[:, lo - sh:PW - sh], ALU.add, r=[srck], w=[dk])
                        src, srck = dst, dk
                    self.stt(pooled[:, :], src[:, 15:PW], 1.0 / w, Ub[:, 15:PW], ALU.mult, ALU.subtract, r=[srck, "pU"], w=["pooled"])
                    if tile == 0:
                        tf = self.tmpf[:, 0, 0:15]
                        self.tt(tf, src[:, 15:30], self.invc[:, gi, :], ALU.mult, r=[srck, "invc"], w=[("tmpf", 0)])
                        self.tt(pooled[:, 0:15], tf, Ub[:, 15:30], ALU.subtract, r=[("tmpf", 0), "pU"], w=["pooled"])
                    self.cp(self.poolc[:, gi, :], Ub[:, PW - 15:PW], r=["pU"], w=["poolc"])
                    for hf in (0, 1):
                        bk = self.nb()
                        self.mm(self.ps[:, bk, :], self.poolw[:, gi, :], pooled[:, hs(hf)], True, True,
                                r=["poolw", "pooled"], w=[("ps", bk)])
                        self.act(self.ysT[:, 4 + gi, hs(hf)], self.ps[:, bk, :], AF.Identity, r=[("ps", bk), "pvec"],
                                 w=[("ys", 4 + gi)], scale=self.pvec[:, l, 1, gi:gi + 1])
            self.wstep(self.w_in[l, :, OFF_POOL + 256 * t:OFF_POOL + 256 * t + 256], 16, 256, f)

    def phase1_conv(self, l, tile):
        SCR = self.SCR
        VW = 30 + TT
        vp = [SCR[:, 0:VW], SCR[:, 1056:1056 + VW]]
        vq = SCR[:, 2112:2112 + VW]
        diag = SCR[:, 3200:3200 + 31 * 128].rearrange("p (k n) -> p k n", k=31)
        acc = SCR[:, 8192:16384].bitcast(F32).rearrange("p (c t) -> p c t", c=4)
        for j in range(4):
            def fc(slot, wv, j=j):
                vb = vp[j % 2]
                vk = ("vp", j % 2)
                wk_ = self.wkeys(slot)
                for hf in (0, 1):
                    b1, b2 = self.nb(), self.nb()
                    self.gemm(b1, [(wv[:, kc, :], self.hT[:, kc, hs(hf)], [("h", kc, hf)]) for kc in range(16)], wk_)
                    self.gemm(b2, [(wv[:, 16 + kc, :], self.hT[:, kc, hs(hf)], [("h", kc, hf)]) for kc in range(16)], wk_)
                    tf = self.tmpf[:, hf, :]
                    self.act(tf, self.ps[:, b2, :], AF.Sigmoid, r=[("ps", b2)], w=[("tmpf", hf)])
                    self.tt(vb[:, 30 + hf * H:30 + (hf + 1) * H], self.ps[:, b1, :], tf, ALU.mult,
                            r=[("ps", b1), ("tmpf", hf)], w=[vk])
                if tile == 0:
                    self.op("dve", lambda e, vb=vb: e.memset(vb[:, 0:30], 0.0), w=[vk])
                else:
                    self.cp(vb[:, 0:30], self.convc[:, j, :], r=["convc"], w=[vk])
                self.cp(self.convc[:, j, :], vb[:, VW - 30:VW], r=[vk], w=["convc"])
                for k in range(31):
                    self.ts(diag[:, k, :], self.identb[:, :], self.convw[:, l, j, k:k + 1], ALU.mult, r=["identb", "convw"], w=["diag"])
                self.cp(vq[:, 0:VW - 1], vb[:, 1:VW], r=[vk], w=["vq"])
                for hf in (0, 1):
                    bk = self.nb()
                    for k in range(31):
                        rhs = vb[:, hf * H + k:hf * H + k + H] if k % 2 == 0 else vq[:, hf * H + k - 1:hf * H + k - 1 + H]
                        self.mm(self.ps[:, bk, :], diag[:, k, :], rhs, start=(k == 0), stop=(k == 30),
                                r=["diag", vk, "vq"], w=[("ps", bk)])
                    self.act(acc[:, j, hs(hf)], self.ps[:, bk, :], AF.Identity, r=[("ps", bk), "pvec"], w=[("acc", j)],
                             bias=self.pvec[:, l, 2, j:j + 1], scale=1.0)
            c0 = OFF_CONV + 128 * j
            self.wstep([(self.w_in[l, :, c0:c0 + 128], 16), (self.w_in[l, :, c0 + 512:c0 + 640], 16)], 32, 128, fc)

        def ln():
            for hf in (0, 1):
                b1, b2 = 4, 5
                for j in range(4):
                    q1 = self.sq[:, 0, :]
                    q2 = self.sq[:, 1, :]
                    self.act(q1, acc[:, j, hs(hf)], AF.Copy, r=[("acc", j)], w=[("sq", 0)])
                    self.act(q2, acc[:, j, hs(hf)], AF.Square, r=[("acc", j)], w=[("sq", 1)])
                    self.mm(self.ps[:, b1, :], self.ones[:, :], q1, start=(j == 0), stop=(j == 3), r=["ones", ("sq", 0)], w=[("ps", b1)])
                    self.mm(self.ps[:, b2, :], self.ones[:, :], q2, start=(j == 0), stop=(j == 3), r=["ones", ("sq", 1)], w=[("ps", b2)])
                mean = self.rs[:, 0, :]
                rstd = self.rs[:, 1, :]
                tf = self.tmpf[:, 0, :]
                self.act(mean, self.ps[:, b1, :], AF.Copy, r=[("ps", b1)], w=["rs"], scale=1.0 / 512.0)
                self.tt(tf, mean, mean, ALU.mult, r=["rs"], w=[("tmpf", 0)])
                self.stt(tf, self.ps[:, b2, :], 1.0 / 512.0, tf, ALU.mult, ALU.subtract, r=[("ps", b2), ("tmpf", 0)], w=[("tmpf", 0)])
                self.act(rstd, tf, AF.Sqrt, r=[("tmpf", 0)], w=["rs"], bias=EPS, scale=1.0)
                self.op("dve", lambda e, rstd=rstd: e.reciprocal(out=rstd, in_=rstd), r=["rs"], w=["rs"])
                for j in range(4):
                    t2 = self.tmpf[:, 1, :]
                    self.tt(t2, acc[:, j, hs(hf)], mean, ALU.subtract, r=[("acc", j), "rs"], w=[("tmpf", 1)])
                    self.tt(t2, t2, rstd, ALU.mult, r=[("tmpf", 1), "rs"], w=[("tmpf", 1)])
                    self.act(self.ysT[:, 8 + j, hs(hf)], t2, AF.Silu, r=[("tmpf", 1), "pvec"], w=[("ys", 8 + j)],
                             scale=self.pvec[:, l, 3, j:j + 1], bias=self.pvec[:, l, 4, j:j + 1])
        self.step(ln)

    def phase1_gmlp(self, l, tile):
        SCR = self.SCR
        ug = SCR[:, 0:4096].rearrange("p (c t) -> p c t", c=4)
        vg = SCR[:, 4096:12288].bitcast(F32).rearrange("p (c t) -> p c t", c=8)
        vn = SCR[:, 12288:16384].rearrange("p (c t) -> p c t", c=8)
        st = self.st4
        for t in range(2):
            def fu(slot, wv, t=t):
                for jj in range(2):
                    j = 2 * t + jj
                    for hf in (0, 1):
                        bk = self.nb()
                        self.gemm(bk, self.inproj_pairs(wv, slot, jj, hf), self.wkeys(slot))
                        self.act(ug[:, j, hs(hf)], self.ps[:, bk, :], AF.Gelu, r=[("ps", bk)], w=[("scr", "ug", j)])
            self.wstep(self.w_in[l, :, OFF_GMLP + 256 * t:OFF_GMLP + 256 * t + 256], 16, 256, fu)
        for ch in range(2):
            def fv(slot, wv, ch=ch):
                for tt_ in range(8):
                    bk = self.nb()
                    pairs = [(self.hT[:, kc, tt_ * 128:(tt_ + 1) * 128], wv[:, kc, :], [("h", kc, tt_ // 4)]) for kc in range(16)]
                    self.gemm(bk, pairs, self.wkeys(slot), n=256)
                    self.act(vg[:, tt_, ch * 256:(ch + 1) * 256], self.ps[:, bk, 0:256], AF.Gelu, r=[("ps", bk)],
                             w=[("scr", "vg", tt_)], accum_out=st[:, tt_, ch:ch + 1])
            self.wstep(self.w_in[l, :, OFF_GMLP + 512 + 256 * ch:OFF_GMLP + 512 + 256 * ch + 256], 16, 256, fv)

        def g():
            for tt_ in range(8):
                R = [("scr", "vg", tt_), "st4"]
                W = ["st4"]
                s1, s2, mean, var, rstd, nb_ = [st[:, tt_, i:i + 1] for i in range(2, 8)]
                tf = self.tmpf[:, tt_ % 2, :]
                tk = ("tmpf", tt_ % 2)
                self.act(tf, vg[:, tt_, :], AF.Square, r=R, w=[tk, "st4"], accum_out=s2)
                self.tt(s1, st[:, tt_, 0:1], st[:, tt_, 1:2], ALU.add, r=R, w=W)
                self.ts(mean, s1, 1.0 / 512.0, ALU.mult, r=R, w=W)
                self.tt(var, mean, mean, ALU.mult, r=R, w=W)
                self.stt(var, s2, 1.0 / 512.0, var, ALU.mult, ALU.subtract, r=R, w=W)
                self.act(rstd, var, AF.Sqrt, r=R, w=W, bias=EPS, scale=1.0)
                self.op("dve", lambda e, rstd=rstd: e.reciprocal(out=rstd, in_=rstd), r=R, w=W)
                self.stt(nb_, mean, -1.0, rstd, ALU.mult, ALU.mult, r=R, w=W)
                self.act(tf, vg[:, tt_, :], AF.Identity, r=R + [tk], w=[tk], scale=rstd, bias=nb_)
                self.tt(tf, tf, self.gln[:, 0, :], ALU.mult, r=[tk, "gln"], w=[tk])
                self.tt(vn[:, tt_, :], tf, self.gln[:, 1, :], ALU.add, r=[tk, "gln"], w=[("scr", "vn", tt_)])
            for hd in range(4):
                for hf in (0, 1):
                    bk = self.nb()
                    for q in range(4):
                        tt_ = hf * 4 + q
                        self.mm(self.ps[:, bk, q * 128:(q + 1) * 128], vn[:, tt_, hd * 128:(hd + 1) * 128], self.wsT[:, hd, :],
                                True, True, r=[("scr", "vn", tt_), "wsT"], w=[("ps", bk)])
                    tf = self.tmpf[:, hf, :]
                    tk = ("tmpf", hf)
                    self.tt(tf.rearrange("p (q t) -> p q t", q=4), self.ps[:, bk, :].rearrange("p (q t) -> p q t", q=4),
                            self.gln[:, 2, hd * 128:(hd + 1) * 128].unsqueeze(1).to_broadcast([128, 4, 128]), ALU.add,
                            r=[("ps", bk), "gln"], w=[tk])
                    self.tt(self.ysT[:, 12 + hd, hs(hf)], tf, ug[:, hd, hs(hf)], ALU.mult, r=[tk, ("scr", "ug", hd)], w=[("ys", 12 + hd)])
        self.step(g)

    def phase2_merge(self, l):
        for j in range(16):
            for k in range(4):
                def fg(slot, wv, j=j, k=k):
                    wk_ = self.wkeys(slot)
                    for hf in (0, 1):
                        bg, by = self.nb(), self.nb()
                        ab = 4 + hf
                        self.gemm(bg, [(wv[:, kc, :], self.hT[:, kc, hs(hf)], [("h", kc, hf)]) for kc in range(16)], wk_)
                        self.gemm(by, [(wv[:, 16 + kc, :], self.ysT[:, 4 * k + kc, hs(hf)], [("ys", 4 * k + kc)]) for kc in range(4)], wk_)
                        sg = self.rs[:, hf, :]
                        self.act(sg, self.ps[:, bg, :], AF.Sigmoid, r=[("ps", bg)], w=[("rsg", hf)])
                        if k == 0:
                            self.tt(self.ps[:, ab, :], self.ps[:, by, :], sg, ALU.mult, r=[("ps", by), ("rsg", hf)], w=[("ps", ab)])
                        else:
                            self.tt(sg, self.ps[:, by, :], sg, ALU.mult, r=[("ps", by), ("rsg", hf)], w=[("rsg", hf)])
                            if k < 3:
                                self.tt(self.ps[:, ab, :], self.ps[:, ab, :], sg, ALU.add, r=[("rsg", hf), ("ps", ab)], w=[("ps", ab)])
                            else:
                                self.tt(self.mgT[:, j, hs(hf)], self.ps[:, ab, :], sg, ALU.add, r=[("rsg", hf), ("ps", ab)], w=[("mg", j)])
                c0 = OFF_GATE + k * D + 128 * j
                self.wstep([(self.w_in[l, :, c0:c0 + 128], 16), (self.w_branch[l, 512 * k:512 * k + 512, 128 * j:128 * j + 128], 4)],
                           20, 128, fg)

    def norm_resid(self, l, gkind, hf, src_chunks, src_keys):
        rs = self.rs[:, hf, :]
        self.rstd_from_bank(6, rs, 1.0 / D)
        for c in range(16):
            tf = self.tmpf[:, c % 2, :]
            tk = ("tmpf", c % 2)
            self.stt(tf, src_chunks[c], self.gv[:, l, gkind, c:c + 1], rs, ALU.mult, ALU.mult, r=[src_keys[c], "gv", "rs"], w=[tk])
            self.tt(self.xT[:, c, hs(hf)], self.xT[:, c, hs(hf)], tf, ALU.add, r=[tk, ("x", c, hf)], w=[("x", c, hf)])

    def phase3_wo(self, l, hf):
        for j in range(16):
            def f(slot, wv, j=j):
                bk = self.nb()
                self.gemm(bk, [(wv[:, kc, :], self.mgT[:, kc, hs(hf)], [("mg", kc)]) for kc in range(16)], self.wkeys(slot))
                self.act(self.mixF[:, j, :], self.ps[:, bk, :], AF.Copy, r=[("ps", bk)], w=[("ys", j)])
                sq = self.sq[:, j % 2, :]
                self.act(sq, self.ps[:, bk, :], AF.Square, r=[("ps", bk)], w=[("sq", j % 2)])
                self.mm(self.ps[:, 6, :], self.ones[:, :], sq, start=(j == 0), stop=(j == 15), r=["ones", ("sq", j % 2)], w=[("ps", 6)])
            self.wstep(self.w_o[l, :, 128 * j:128 * j + 128], 16, 128, f)
        self.step(lambda: self.norm_resid(l, 1, hf, [self.mixF[:, c, :] for c in range(16)], [("ys", c) for c in range(16)]))

    def phase4_ffn(self, l, hf):
        self.prenorm(l, 2, (hf,))
        oh = 1 - hf
        fS = [self.hT[:, c, hs(oh)] for c in range(16)]
        for j in range(64):
            def f1(slot, wv, j=j):
                bk = self.nb()
                self.gemm(bk, [(wv[:, kc, :], self.hT[:, kc, hs(hf)], [("h", kc, hf)]) for kc in range(16)], self.wkeys(slot))
                tf = self.tmpf[:, j % 2, :]
                tk = ("tmpf", j % 2)
                self.act(tf, self.ps[:, bk, :], AF.Relu, r=[("ps", bk)], w=[tk])
                key = ("ys", j // 2) if j < 32 else ("mg", (j - 32) // 2)
                self.tt(self.f1T[:, j, :], tf, tf, ALU.mult, r=[tk], w=[key])
            self.wstep(self.w_ff1[l, :, 128 * j:128 * j + 128], 16, 128, f1)
        for j in range(16):
            for q in range(4):
                def f2(slot, wv, j=j, q=q):
                    bk = j % 2
                    pairs = []
                    for kc in range(16):
                        fc = q * 16 + kc
                        key = ("ys", fc // 2) if fc < 32 else ("mg", (fc - 32) // 2)
                        pairs.append((wv[:, kc, :], self.f1T[:, fc, :], [key]))
                    self.gemm(bk, pairs, self.wkeys(slot), first=(q == 0), last=(q == 3))
                    if q == 3:
                        self.act(fS[j], self.ps[:, bk, :], AF.Copy, r=[("ps", bk)], w=[("h", j, oh)])
                        sq = self.sq[:, j % 2, :]
                        self.act(sq, self.ps[:, bk, :], AF.Square, r=[("ps", bk)], w=[("sq", j % 2)])
                        self.mm(self.ps[:, 6, :], self.ones[:, :], sq, start=(j == 0), stop=(j == 15),
                                r=["ones", ("sq", j % 2)], w=[("ps", 6)])
                    self.bank_rr = 2
                self.wstep(self.w_ff2[l, 2048 * q:2048 * q + 2048, 128 * j:128 * j + 128], 16, 128, f2)
        self.step(lambda: self.norm_resid(l, 3, hf, fS, [("h", c, oh) for c in range(16)]))


_NC_CACHE = {}


def _host_layout(inp, wt_tensors):
    f = lambda a: np.ascontiguousarray(np.asarray(a, dtype=np.float32))
    x = f(inp["x"])
    per_core = []
    for b in range(NB):
        xb = x[b].reshape(NT, TT, 16, 128).transpose(0, 3, 2, 1)
        per_core.append(f(xb))
    sh = {}
    sh["pool_w"] = f(inp["pool_w"]).reshape(L, 512, 128)
    raw = {"w_in": inp["w_in"], "w_glu": inp["ssm_w_glu"], "w_branch": np.asarray(inp["w_branch"]).reshape(L, 2048, D),
           "w_o": inp["w_o"], "w_ff1": inp["w_ff1"], "w_ff2": inp["w_ff2"]}
    for name, _ap, used, recs in wt_tensors:
        buf = np.zeros((128, WCAP), np.float32)
        for (parts, ncols), off, n in recs:
            o = off
            for (wname, l_, r0, nrows, c0, nc_), kk in parts:
                assert nc_ == ncols and nrows == kk * 128
                a = np.asarray(raw[wname][l_][r0:r0 + nrows, c0:c0 + ncols], dtype=np.float32)
                buf[:, o:o + kk * ncols] = a.reshape(kk, 128, ncols).transpose(1, 0, 2).reshape(128, kk * ncols)
                o += kk * ncols
        sh[name] = buf
    gv = np.stack([f(inp[k]) for k in ("g_pre_mix", "g_post_mix", "g_pre_mlp", "g_post_mlp")], axis=1)
    sh["gv"] = f(gv.reshape(L, 4, 16, 128).transpose(3, 0, 1, 2))
    a_re, a_im, ldt = f(inp["ssm_a_re"]), f(inp["ssm_a_im"]), f(inp["ssm_log_dt"])
    ldt_b = np.broadcast_to(ldt[:, :, None], a_re.shape)
    a3 = np.stack([a_re, a_im, ldt_b], axis=1)
    a3 = a3.reshape(L, 3, 16, 2, 64).transpose(3, 4, 0, 1, 2)
    sh["ssm_a"] = f(a3.reshape(128, L, 3, 16))
    bT = np.zeros((L, 2, 16, 128, 128), np.float32)
    cT = np.zeros((L, 2, 16, 128, 128), np.float32)
    br, bi = f(inp["ssm_b_re"]), f(inp["ssm_b_im"])
    cr, ci = f(inp["ssm_c_re"]), f(inp["ssm_c_im"])
    for gp in range(16):
        for g2 in range(2):
            g = 2 * gp + g2
            g8 = g % 8
            for ri, (bb, cc) in enumerate(((br, cr), (bi, ci))):
                bT[:, ri, gp, g8 * 16:(g8 + 1) * 16, g2 * 64:(g2 + 1) * 64] = bb[:, g].transpose(0, 2, 1)
                cT[:, ri, gp, g2 * 64:(g2 + 1) * 64, g8 * 16:(g8 + 1) * 16] = cc[:, g].transpose(0, 2, 1)
    sh["ssm_bT"] = bT
    sh["ssm_cT"] = cT
    pv = np.stack([f(inp[k]) for k in ("ssm_d", "pool_scale", "conv_b", "conv_ln_g", "conv_ln_b")], axis=1)
    sh["pvec"] = f(pv.reshape(L, 5, 4, 128).transpose(3, 0, 1, 2))
    cw = f(inp["conv_w"])
    sh["convw"] = f(cw.reshape(L, 31, 4, 128).transpose(3, 0, 2, 1))
    sh["gln"] = f(np.stack([f(inp["gmlp_ln_g"]), f(inp["gmlp_ln_b"]), f(inp["gmlp_bs"]).reshape(L, 512)], axis=1))
    ws = f(inp["gmlp_ws"])
    sh["wsT"] = f(ws.transpose(0, 3, 1, 2))
    s_idx = np.arange(128)[:, None]
    t_idx = np.arange(128)[None, :]
    sh["mask"] = (t_idx >= s_idx).astype(np.float32)
    inv = np.zeros((128, 4, 15), np.float32)
    for gi in range(4):
        w = 2 << gi
        inv[:, gi, :] = 1.0 / np.minimum(np.arange(15) + 1, w)
    sh["invcnt"] = inv
    sh["ident"] = np.eye(128, dtype=np.float32)
    return per_core, sh


def kernel(**inputs):
    key = "full"
    if key not in _NC_CACHE:
        b = Builder(layers=list(range(L)))
        _NC_CACHE[key] = (b.build(), b.wt_tensors)
    nc, wt_tensors = _NC_CACHE[key]
    per_core, sh = _host_layout(inputs, wt_tensors)
    real = {0: 0, 1: 1, 4: 2, 5: 3}
    zsh = {}
    for k, v in sh.items():
        zsh[k] = np.zeros_like(v) if k.startswith("wt") else v
    zx = np.zeros_like(per_core[0])
    in_maps = []
    for c in range(8):
        if c in real:
            m = dict(sh)
            m["xin"] = per_core[real[c]]
        else:
            m = dict(zsh)
            m["xin"] = zx
        in_maps.append(m)
    res = run_bass_kernel_spmd(nc, in_maps, core_ids=list(range(8)))
    out = np.empty((NB, SEQ, D), np.float32)
    for c, b in real.items():
        y = np.asarray(res.results[c]["yout"])
        out[b] = y.transpose(0, 3, 2, 1).reshape(SEQ, D)
    return out
```
